# Optimizing a Trainium2 kernel written in Bass

```python
import jax, jax.numpy as jnp
from jax import lax
import numpy as np

D_MODEL = 1024
BATCH = 8
SEQ = 2048
DEPTH = 2

N_META = 16
BLOCK = 128
META_PAD = BLOCK - N_META
HEAD_DIM = 64
ROPE_THETA = 10000.0
NORM_EPS = 1e-6
NEG_INF = -1e30
SWA_HEADS = D_MODEL // (2 * HEAD_DIM)
SWA_KV_HEADS = SWA_HEADS // 4
SWA_GROUP = SWA_HEADS // SWA_KV_HEADS
SWA_WINDOW = 128
SWA_WIDTH = SWA_HEADS * HEAD_DIM
SWA_KV_WIDTH = SWA_KV_HEADS * HEAD_DIM
CONV_CHANNELS = D_MODEL // 2
CONV_WIDTH = 31
CONV_LN_EPS = 1e-5
SB_HEADS = D_MODEL // HEAD_DIM
SB_WIDTH = SB_HEADS * HEAD_DIM
AB_SPLITS = (SWA_WIDTH, SWA_KV_WIDTH, SWA_KV_WIDTH, SWA_WIDTH, 2 * CONV_CHANNELS, CONV_CHANNELS)
AB_IN = sum(AB_SPLITS)
AB_MIX = SWA_WIDTH + CONV_CHANNELS
SB_SPLITS = (SB_WIDTH, SB_WIDTH, SB_WIDTH, SB_WIDTH)
SB_IN = sum(SB_SPLITS)
N_EVEN = (DEPTH + 1) // 2
N_ODD = DEPTH // 2

kernel_name = "hybrid_swa_conformer_stickbreaking_trunk"


def _split(x, sizes):
    idx = [int(i) for i in np.cumsum(sizes)[:-1]]
    return jnp.split(x, idx, axis=-1)


def rms_norm(x, g):
    xf = x.astype(jnp.float32)
    y = xf * lax.rsqrt(jnp.mean(xf * xf, axis=-1, keepdims=True) + NORM_EPS)
    return (y * g.astype(jnp.float32)).astype(x.dtype)


def layer_norm(x, g, b):
    xf = x.astype(jnp.float32)
    mu = jnp.mean(xf, axis=-1, keepdims=True)
    xc = xf - mu
    y = xc * lax.rsqrt(jnp.mean(xc * xc, axis=-1, keepdims=True) + CONV_LN_EPS)
    return (y * g.astype(jnp.float32) + b.astype(jnp.float32)).astype(x.dtype)


def apply_rope(x, pos):
    half = x.shape[-1] // 2
    inv = ROPE_THETA ** (-jnp.arange(half, dtype=jnp.float32) / half)
    ang = pos.astype(jnp.float32)[:, None] * inv[None, :]
    cos = jnp.cos(ang)[None, :, None, :]
    sin = jnp.sin(ang)[None, :, None, :]
    xf = x.astype(jnp.float32)
    x1, x2 = xf[..., :half], xf[..., half:]
    return jnp.concatenate([x1 * cos - x2 * sin, x2 * cos + x1 * sin], axis=-1).astype(x.dtype)


def sliding_window_sink_attention(q, k, v, sinks):
    b, l = q.shape[0], q.shape[1]
    lp = l + META_PAD
    nb = lp // BLOCK
    padw = ((0, 0), (META_PAD, 0), (0, 0), (0, 0))
    qb = jnp.pad(q, padw).reshape(b, nb, BLOCK, SWA_KV_HEADS, SWA_GROUP, HEAD_DIM)
    kb = jnp.pad(k, padw).reshape(b, nb, BLOCK, SWA_KV_HEADS, HEAD_DIM)
    vb = jnp.pad(v, padw).reshape(b, nb, BLOCK, SWA_KV_HEADS, HEAD_DIM)

    def band(t):
        prev = jnp.concatenate([jnp.zeros_like(t[:, :1]), t[:, :-1]], axis=1)
        meta = jnp.broadcast_to(t[:, :1], t.shape)
        return jnp.concatenate([meta, prev, t], axis=2)

    kk, vv = band(kb), band(vb)
    scale = HEAD_DIM ** -0.5
    s = jnp.einsum('bnqgrd,bnkgd->bngrqk', qb, kk).astype(jnp.float32) * scale
    blk = jnp.arange(nb)[:, None, None]
    r = jnp.arange(BLOCK)
    qpos = blk * BLOCK + r[None, :, None]
    mpos = r[None, None, :]
    bpos = (blk - 1) * BLOCK + jnp.arange(2 * BLOCK)[None, None, :]
    meta_ok = (mpos >= META_PAD) & (qpos - mpos >= SWA_WINDOW)
    band_ok = (bpos >= META_PAD) & (qpos >= bpos) & (qpos - bpos < SWA_WINDOW)
    mask = jnp.concatenate([jnp.broadcast_to(meta_ok, (nb, BLOCK, BLOCK)), band_ok], axis=-1)
    s = jnp.where(mask[None, :, None, None], s, NEG_INF)
    sink = jnp.broadcast_to(sinks.astype(jnp.float32).reshape(1, 1, SWA_KV_HEADS, SWA_GROUP, 1, 1),
                            s.shape[:-1] + (1,))
    p = jax.nn.softmax(jnp.concatenate([s, sink], axis=-1), axis=-1)[..., :-1]
    o = jnp.einsum('bngrqk,bnkgd->bnqgrd', p.astype(v.dtype), vv)
    return o.reshape(b, lp, SWA_HEADS, HEAD_DIM)[:, META_PAD:]


def causal_depthwise_conv(u, w, bias):
    y = lax.conv_general_dilated(u, w[:, None, :].astype(u.dtype), window_strides=(1,),
                                 padding=((CONV_WIDTH - 1, 0),),
                                 dimension_numbers=('NWC', 'WIO', 'NWC'),
                                 feature_group_count=u.shape[-1])
    return y + bias


def stick_breaking_attention(q, k, v):
    b, l, h, d = q.shape
    lp = l + META_PAD
    nb = lp // BLOCK
    padw = ((0, 0), (META_PAD, 0), (0, 0), (0, 0))
    qp, kp, vp = jnp.pad(q, padw), jnp.pad(k, padw), jnp.pad(v, padw)
    scale = d ** -0.5
    outs = []
    for i in range(nb):
        kend = (i + 1) * BLOCK
        z = jnp.einsum('bqhd,bkhd->bhqk', qp[:, i * BLOCK:kend], kp[:, :kend]).astype(jnp.float32) * scale
        qpos = i * BLOCK + jnp.arange(BLOCK)[:, None]
        kpos = jnp.arange(kend)[None, :]
        valid = (kpos >= META_PAD) & (kpos < qpos)
        log_beta = jax.nn.log_sigmoid(z)
        log_1m = jnp.where(valid, jax.nn.log_sigmoid(-z), 0.0)
        after = lax.cumsum(log_1m, axis=3, reverse=True) - log_1m
        a = jnp.where(valid, jnp.exp(log_beta + after), 0.0)
        outs.append(jnp.einsum('bhqk,bkhd->bqhd', a.astype(v.dtype), vp[:, :kend]))
    return jnp.concatenate(outs, axis=1)[:, META_PAD:]


def swa_conv_mixer(h, pos, w_in, sinks, conv_w, conv_b, ln_g, ln_b, w_pw2, w_out):
    b, l, _ = h.shape
    q, k, v, g_a, glu_in, g_b = _split(h @ w_in, AB_SPLITS)
    q = apply_rope(q.reshape(b, l, SWA_HEADS, HEAD_DIM), pos)
    k = apply_rope(k.reshape(b, l, SWA_KV_HEADS, HEAD_DIM), pos)
    v = v.reshape(b, l, SWA_KV_HEADS, HEAD_DIM)
    a = sliding_window_sink_attention(q, k, v, sinks).reshape(b, l, SWA_WIDTH) * jax.nn.silu(g_a)
    u = glu_in[..., :CONV_CHANNELS] * jax.nn.sigmoid(glu_in[..., CONV_CHANNELS:])
    c = jax.nn.silu(layer_norm(causal_depthwise_conv(u, conv_w, conv_b), ln_g, ln_b))
    c = (c @ w_pw2) * jax.nn.silu(g_b)
    return jnp.concatenate([a, c], axis=-1) @ w_out


def stick_breaking_mixer(h, w_in, w_out):
    b, l, _ = h.shape
    q, k, v, g = _split(h @ w_in, SB_SPLITS)
    shp = (b, l, SB_HEADS, HEAD_DIM)
    o = stick_breaking_attention(q.reshape(shp), k.reshape(shp), v.reshape(shp))
    return (o.reshape(b, l, SB_WIDTH) * jax.nn.silu(g)) @ w_out


def setup_inputs(seed: int = 0) -> dict:
    key = jax.random.key(seed)
    ks = jax.random.split(key, 16)
    f32 = jnp.float32
    nrm = lambda k, s: jax.random.normal(k, s, dtype=f32)
    return {
        "x": nrm(ks[0], (BATCH, SEQ, D_MODEL)),
        "meta_tokens": nrm(ks[1], (N_META, D_MODEL)),
        "ab_pre_norm": 1.0 + 0.05 * nrm(ks[2], (N_EVEN, D_MODEL)),
        "ab_w_in": nrm(ks[3], (N_EVEN, D_MODEL, AB_IN)) * D_MODEL ** -0.5,
        "ab_sinks": nrm(ks[4], (N_EVEN, SWA_HEADS)),
        "ab_conv_w": nrm(ks[5], (N_EVEN, CONV_WIDTH, CONV_CHANNELS)) * CONV_WIDTH ** -0.5,
        "ab_conv_b": 0.02 * nrm(ks[6], (N_EVEN, CONV_CHANNELS)),
        "ab_conv_ln_g": 1.0 + 0.05 * nrm(ks[7], (N_EVEN, CONV_CHANNELS)),
        "ab_conv_ln_b": 0.02 * nrm(ks[8], (N_EVEN, CONV_CHANNELS)),
        "ab_w_pw2": nrm(ks[9], (N_EVEN, CONV_CHANNELS, CONV_CHANNELS)) * CONV_CHANNELS ** -0.5,
        "ab_w_out": nrm(ks[10], (N_EVEN, AB_MIX, D_MODEL)) * AB_MIX ** -0.5,
        "ab_post_norm": 1.0 + 0.05 * nrm(ks[11], (N_EVEN, D_MODEL)),
        "sb_pre_norm": 1.0 + 0.05 * nrm(ks[12], (N_ODD, D_MODEL)),
        "sb_w_in": nrm(ks[13], (N_ODD, D_MODEL, SB_IN)) * D_MODEL ** -0.5,
        "sb_w_out": nrm(ks[14], (N_ODD, SB_WIDTH, D_MODEL)) * SB_WIDTH ** -0.5,
        "sb_post_norm": 1.0 + 0.05 * nrm(ks[15], (N_ODD, D_MODEL)),
    }


def reference(x, meta_tokens, ab_pre_norm, ab_w_in, ab_sinks, ab_conv_w, ab_conv_b, ab_conv_ln_g,
              ab_conv_ln_b, ab_w_pw2, ab_w_out, ab_post_norm, sb_pre_norm, sb_w_in, sb_w_out,
              sb_post_norm):
    b = x.shape[0]
    meta = jnp.broadcast_to(meta_tokens[None].astype(x.dtype), (b, N_META, D_MODEL))
    h = jnp.concatenate([meta, x], axis=1)
    pos = jnp.arange(h.shape[1])
    for layer in range(DEPTH):
        i = layer // 2
        if layer % 2 == 0:
            y = swa_conv_mixer(rms_norm(h, ab_pre_norm[i]), pos, ab_w_in[i], ab_sinks[i],
                               ab_conv_w[i], ab_conv_b[i], ab_conv_ln_g[i], ab_conv_ln_b[i],
                               ab_w_pw2[i], ab_w_out[i])
            h = h + rms_norm(y, ab_post_norm[i])
        else:
            y = stick_breaking_mixer(rms_norm(h, sb_pre_norm[i]), sb_w_in[i], sb_w_out[i])
            h = h + rms_norm(y, sb_post_norm[i])
    return h[:, N_META:]
```

```python
import os
import numpy as np
from contextlib import ExitStack
import concourse.bass as bass
import concourse.mybir as mybir
from concourse.bass_utils import run_bass_kernel_spmd

F32 = mybir.dt.float32
BF16 = mybir.dt.bfloat16
AF = mybir.ActivationFunctionType
ALU = mybir.AluOpType

D = 1024
T = 2176
NB = 17
PAD = 112
SEQ = 2048
NEG = -240000.0
NEG_SB = -30000.0
W0C = 27 * 128

ENGS = ['pe', 'act', 'dve', 'pool', 'sp']


class Sched:
    def __init__(self, nc, stack):
        self.nc = nc
        self.stack = stack
        self.q = {e: [] for e in ENGS}
        self.cnt = {e: 0 for e in ENGS}
        self.waited = {e: {} for e in ENGS}
        self.lastw = {}
        self.readers = {}
        self.semh = {}
        self.dval = {}
        for e in ENGS[:4]:
            self.semh['E_' + e] = stack.enter_context(nc.semaphore('E_' + e))

    def _deps(self, reads, writes, eng=None):
        deps = []
        for k in reads:
            t = self.lastw.get(k)
            if t is not None:
                deps.append(t)
            if isinstance(k, tuple) and k[0] == 'ps':
                deps.extend(t2 for t2 in self.readers.get(k, {}).values() if t2[2] != eng)
        for k in writes:
            t = self.lastw.get(k)
            if t is not None:
                deps.append(t)
            deps.extend(self.readers.get(k, {}).values())
        return deps

    def _wait(self, eng, deps):
        w = self.waited[eng]
        for (sk, v, src) in deps:
            if src == 'pe' and eng == 'pe':
                continue
            if w.get(sk, 0) >= v:
                continue
            w[sk] = v
            h = self.semh[sk]
            self.q[eng].append(lambda e, h=h, v=v: e.wait_ge(h, v))

    def _record(self, tok, reads, writes):
        for k in reads:
            self.readers.setdefault(k, {})[tok[0]] = tok
        for k in writes:
            self.lastw[k] = tok
            self.readers[k] = {}

    def op(self, eng, name, kw, reads=(), writes=(), extra=()):
        self._wait(eng, self._deps(reads, writes, eng) + list(extra))
        self.cnt[eng] += 1
        sk = 'E_' + eng
        tok = (sk, self.cnt[eng], eng)
        h = self.semh[sk]
        self.q[eng].append(lambda e, name=name, kw=kw, h=h: getattr(e, name)(**kw).then_inc(h, 1))
        self._record(tok, reads, writes)
        return tok

    def dma(self, eng, name, kw, reads=(), writes=(), extra=()):
        sk = 'D_' + name
        self._wait(eng, self._deps(reads, writes) + list(extra))
        if sk not in self.semh:
            self.semh[sk] = self.stack.enter_context(self.nc.semaphore(sk))
            self.dval[sk] = 0
        if self.dval[sk] > 0:
            self._wait(eng, [(sk, self.dval[sk], 'dma')])
        self.dval[sk] += 16
        tok = (sk, self.dval[sk], 'dma')
        h = self.semh[sk]
        self.q[eng].append(lambda e, kw=kw, h=h: e.dma_start(**kw).then_inc(h, 16))
        self._record(tok, reads, writes)
        return tok

    def barrier(self):
        toks = []
        for e in ENGS[:4]:
            if self.cnt[e] > 0:
                toks.append(('E_' + e, self.cnt[e], e + '_b'))
        for sk, v in self.dval.items():
            toks.append((sk, v, 'dma'))
        for e in ENGS:
            self._wait(e, toks)

    def wait_all(self, eng, toks):
        self._wait(eng, list(toks))

    def run(self):
        nc = self.nc
        with nc.Block() as block:
            @block.tensor
            def _(e):
                for f in self.q['pe']:
                    f(e)

            @block.scalar
            def _(e):
                for f in self.q['act']:
                    f(e)

            @block.vector
            def _(e):
                for f in self.q['dve']:
                    f(e)

            @block.gpsimd
            def _(e):
                for f in self.q['pool']:
                    f(e)

            @block.sync
            def _(e):
                for f in self.q['sp']:
                    f(e)


class Arena:
    def __init__(self, ap, nwords):
        self.ap = ap
        self.n = nwords
        self.off = 0

    def f32(self, n):
        assert self.off + n <= self.n, ('SBUF arena overflow', self.off, n, self.n)
        a = self.ap[:, self.off:self.off + n]
        self.off += n
        return a

    def bf16(self, n):
        w = (n + 1) // 2
        return self.f32(w).bitcast(BF16)

    def mark(self):
        return self.off

    def reset(self, m):
        self.off = m


C_Q, C_QS, C_K, C_KS, C_GA, C_GLA, C_GLB, C_GB, C_V = 0, 4, 8, 9, 10, 14, 18, 22, 26
K_ID, K_TRI, K_COMP, K_ONES = 0, 128, 256, 384
K_MASK = 512
K_MCUR = K_MASK
K_MPREV = K_MCUR + 512
K_MMETA = K_MPREV + 512
K_MCUR0 = K_MMETA + 512
K_SBM = K_MASK
NCB = K_MASK + 5 * 512
D_L0M = 512
D_L1M = 512 + 2048
NCD = 512 + 2048 + 2560
V_GPRE0, V_GPRE1, V_CW, V_CB, V_LNG, V_LNB = 0, 8, 16, 16 + 124, 16 + 128, 16 + 132
NVEC = 16 + 136
R_GPOST0, R_GPOST1, R_SINK = 0, 1024, 2048
NROW = 2048 + 8
RS_GPOST, RS_SINK = 0, 1024
NROWS = 1024 + 8


def build(do_l0=True, do_l1=True):
    nc = bass.Bass('TRN2', target_bir_lowering=False)
    dt = lambda name, shape, kind='ExternalInput': nc.dram_tensor(name, shape, F32, kind=kind).ap()
    x_d = dt('x', [SEQ, D])
    meta_d = dt('meta', [16, D])
    w0_d = dt('w0', [D, W0C])
    wpw2_d = dt('wpw2', [512, 512])
    wout0_d = dt('wout0', [D, D])
    w1_d = dt('w1', [8, D, 512])
    wout1_d = dt('wout1', [D, D])
    vecs_d = dt('vecs', [128, NVEC])
    rows_d = dt('rows', [128, NROW])
    cst_d = dt('cst', [128, NCD])
    rope_d = dt('rope', [128, 2 * T])
    if not do_l0:
        hin_d = dt('hin', [T, D])
    if do_l1:
        out_d = dt('out', [SEQ, D], kind='ExternalOutput')
    else:
        hout_d = dt('hout', [T, D], kind='ExternalOutput')

    with ExitStack() as st:
        NW = 53200
        arena_t = st.enter_context(nc.sbuf_tensor('arena', [128, NW], F32))
        ps = st.enter_context(nc.psum_tensor('ps', [128, 4096], F32))
        S = Sched(nc, st)
        A = Arena(arena_t, NW)

        def bank(i):
            return ps[:, i * 512:(i + 1) * 512]

        def bankbf(i):
            return ps[:, i * 512:(i + 1) * 512].bitcast(BF16)

        def ACT(out, in_, func, reads, writes, **kw):
            return S.op('act', 'activation', dict(out=out, in_=in_, func=func, **kw), reads, writes)

        def TT(eng, out, in0, in1, op, reads, writes):
            return S.op(eng, 'tensor_tensor', dict(out=out, in0=in0, in1=in1, op=op), reads, writes)

        def TS(eng, out, in0, s1, s2, op0, op1, reads, writes):
            kw = dict(out=out, in0=in0, scalar1=s1, scalar2=s2, op0=op0)
            if op1 is not None:
                kw['op1'] = op1
            return S.op(eng, 'tensor_scalar', kw, reads, writes)

        def STT(out, in0, scalar, in1, op0, op1, reads, writes):
            return S.op('dve', 'scalar_tensor_tensor', dict(out=out, in0=in0, scalar=scalar, in1=in1, op0=op0, op1=op1),
                        reads, writes)

        def MM(out, lhsT, rhs, start, stop, reads, writes, **kw):
            return S.op('pe', 'matmul', dict(out=out, lhsT=lhsT, rhs=rhs, start=start, stop=stop, **kw), reads, writes)

        def TR(out, in_, reads, writes):
            return S.op('pe', 'transpose', dict(out=out, in_=in_, identity=ident), reads, writes)

        def CP(eng, out, in_, reads, writes):
            return S.op(eng, 'tensor_copy', dict(out=out, in_=in_), reads, writes)

        def DMA(eng, name, out, in_, reads=(), writes=()):
            return S.dma(eng, name, dict(out=out, in_=in_), reads, writes)

        h = A.f32(NB * D).rearrange('p (b d) -> p b d', b=NB)
        cst = A.bf16(NCB)
        vecs = A.f32(NVEC)
        rows = A.f32(NROWS)
        stat = A.f32(64)
        esink = A.f32(8)
        ident = cst[:, K_ID:K_ID + 128]
        tri = cst[:, K_TRI:K_TRI + 128]
        comp = cst[:, K_COMP:K_COMP + 128]
        ones512 = cst[:, K_ONES:K_ONES + 128]
        base_mark = A.mark()

        DMA('pool', 'cst', cst[:, 0:512], cst_d[:, 0:512], writes=['cst'])
        DMA('sp', 'vecs', vecs, vecs_d, writes=['vecs'])
        DMA('sp', 'sinks', rows[:, RS_SINK:RS_SINK + 8], rows_d[:, R_SINK:R_SINK + 8], writes=['sinks'])

        def load_layer_consts(layer):
            if layer == 0:
                DMA('pool', 'cstm', cst[:, K_MASK:K_MASK + 2048], cst_d[:, D_L0M:D_L0M + 2048], writes=['cstm'])
                DMA('sp', 'rows', rows[:, 0:1024], rows_d[:, R_GPOST0:R_GPOST0 + 1024], writes=['rows'])
            else:
                DMA('pool', 'cstm', cst[:, K_MASK:K_MASK + 640], cst_d[:, D_L1M:D_L1M + 640], writes=['cstm'])
                DMA('sp', 'rows', rows[:, 0:1024], rows_d[:, R_GPOST1:R_GPOST1 + 1024], writes=['rows'])
        if do_l0:
            S.op('dve', 'memset', dict(ap=h[:, 0, :], constant=0.0), [], [('h', 0)])
            DMA('sp', 'h0', h[PAD:128, 0, :], meta_d, writes=[('h', 0)])
            for b in range(1, NB):
                DMA('sp', 'h%d' % b, h[:, b, :], x_d[(b - 1) * 128:b * 128, :], writes=[('h', b)])
        else:
            for b in range(NB):
                DMA('sp', 'h%d' % b, h[:, b, :], hin_d[b * 128:(b + 1) * 128, :], writes=[('h', b)])

        def norm_stats(b, xs, junk, eps=1e-6, par=0):
            hb = h[:, b, :]
            sc = 40 + 3 * par
            k0, k1, k2, kx = ('nst', par, 0), ('nst', par, 1), ('nst', par, 2), ('xs', par)
            ACT(junk, hb, AF.Square, [('h', b)], ['junk', k0], accum_out=stat[:, sc:sc + 1])
            ACT(stat[:, sc + 1:sc + 2], stat[:, sc:sc + 1], AF.Ln, [k0], [k1], scale=1.0 / D, bias=eps)
            ACT(stat[:, sc + 2:sc + 3], stat[:, sc + 1:sc + 2], AF.Exp, [k1], [k2], scale=-0.5)
            TS('dve', xs, hb, stat[:, sc + 2:sc + 3], None, ALU.mult, None, [('h', b), k2], [kx])

        def norm_tr(gcol, xs, dst_all, key_fn, pbank, par=0):
            kx = ('xs', par)
            pb = bankbf(pbank)
            for kc in range(8):
                TR(pb[:, kc * 128:(kc + 1) * 128], xs[:, kc * 128:(kc + 1) * 128], [kx, 'cst'], [('ps', pbank)])
            TT('dve', dst_all, pb.rearrange('p (k n) -> p k n', k=8), vecs[:, gcol:gcol + 8].unsqueeze(2).to_broadcast([128, 8, 128]),
               ALU.mult, [('ps', pbank), 'vecs'], [key_fn(kc) for kc in range(8)])

        def norm_transpose(b, gcol, xs, junk, dst_all, key_fn, pbank, eps=1e-6, par=0):
            hb = h[:, b, :]
            sc = 40 + 3 * par
            k0, k1, k2, kx = ('nst', par, 0), ('nst', par, 1), ('nst', par, 2), ('xs', par)
            ACT(junk, hb, AF.Square, [('h', b)], ['junk', k0], accum_out=stat[:, sc:sc + 1])
            ACT(stat[:, sc + 1:sc + 2], stat[:, sc:sc + 1], AF.Ln, [k0], [k1], scale=1.0 / D, bias=eps)
            ACT(stat[:, sc + 2:sc + 3], stat[:, sc + 1:sc + 2], AF.Exp, [k1], [k2], scale=-0.5)
            TS('dve', xs, hb, stat[:, sc + 2:sc + 3], None, ALU.mult, None, [('h', b), k2], [kx])
            pb = bankbf(pbank)
            for kc in range(8):
                TR(pb[:, kc * 128:(kc + 1) * 128], xs[:, kc * 128:(kc + 1) * 128], [kx, 'cst'], [('ps', pbank)])
            TT('dve', dst_all, pb.rearrange('p (k n) -> p k n', k=8), vecs[:, gcol:gcol + 8].unsqueeze(2).to_broadcast([128, 8, 128]),
               ALU.mult, [('ps', pbank), 'vecs'], [key_fn(kc) for kc in range(8)])

        def outproj_mm(lhs_fn, mix_keys, wout, wkey, banks):
            for half in range(2):
                bk = banks[half]
                for c in range(8):
                    MM(bank(bk), lhs_fn(c), wout[:, c, half * 512:(half + 1) * 512], c == 0, c == 7,
                       list(mix_keys) + [(wkey, c)], [('ps', bk)])

        def outproj_epi(b, junk, ptmp, banks, sc=8, eps=1e-6):
            for half in range(2):
                bk = banks[half]
                ACT(junk[:, 0:512], bank(bk), AF.Square, [('ps', bk)], ['junk', ('st', sc + half)],
                    accum_out=stat[:, sc + half:sc + half + 1])
            TT('dve', stat[:, sc + 2:sc + 3], stat[:, sc:sc + 1], stat[:, sc + 1:sc + 2], ALU.add, [('st', sc), ('st', sc + 1)],
               [('st', sc + 2)])
            ACT(stat[:, sc + 3:sc + 4], stat[:, sc + 2:sc + 3], AF.Ln, [('st', sc + 2)], [('st', sc + 3)], scale=1.0 / D, bias=eps)
            ACT(stat[:, sc + 4:sc + 5], stat[:, sc + 3:sc + 4], AF.Exp, [('st', sc + 3)], [('st', sc + 4)], scale=-0.5)
            for half in range(2):
                bk = banks[half]
                STT(ptmp[:, 0:512], bank(bk), stat[:, sc + 4:sc + 5], rows[:, half * 512:(half + 1) * 512], ALU.mult, ALU.mult,
                    [('ps', bk), ('st', sc + 4), 'rows'], ['ptmp'])
                TT('pool', h[:, b, half * 512:(half + 1) * 512], h[:, b, half * 512:(half + 1) * 512], ptmp[:, 0:512], ALU.add,
                   [('h', b), 'ptmp'], [('h', b)])

        def outproj_block(b, lhs_fn, mix_keys, wout, wkey, gpost_off, junk, ptmp, banks, eps=1e-6):
            outproj_mm(lhs_fn, mix_keys, wout, wkey, banks)
            outproj_epi(b, junk, ptmp, banks)

        if do_l0:
            w0 = A.bf16(8 * W0C).rearrange('p (k n) -> p k n', k=8)
            wpw2 = A.bf16(4 * 512).rearrange('p (k n) -> p k n', k=4)
            wout0 = A.bf16(8 * D).rearrange('p (k n) -> p k n', k=8)
            NT = 256
            xnT = A.bf16(8 * NT).rearrange('p (k n) -> p k n', k=8)
            qT = A.bf16(4 * NT).rearrange('p (k n) -> p k n', k=4)
            kT = A.bf16(T)
            vext = A.bf16(NB * 2 * 66).rearrange('p (b g d) -> p b g d', b=NB, g=2)
            gaT = A.bf16(4 * NT).rearrange('p (k n) -> p k n', k=4)
            gbT = A.bf16(4 * NT).rearrange('p (k n) -> p k n', k=4)
            ubuf = A.bf16(4 * (30 + NT)).rearrange('p (k n) -> p k n', k=4)
            dg_all = A.bf16(8 * 128)
            dgi = [0]
            acc = A.f32(4 * NT).rearrange('p (k n) -> p k n', k=4)
            ybf = A.bf16(4 * NT).rearrange('p (k n) -> p k n', k=4)
            ysq = A.bf16(4 * NT).rearrange('p (k n) -> p k n', k=4)
            mean_sb = A.f32(NT)
            var_sb = A.f32(NT)
            rstdc = var_sb
            zt = acc
            cact = A.bf16(4 * NT).rearrange('p (k n) -> p k n', k=4)
            mixT = A.bf16(8 * NT).rearrange('p (k n) -> p k n', k=8)
            pT = [A.bf16(512) for _ in range(3)]
            a_tok = A.bf16(512)
            ropeA = A.f32(NT)
            ropeB = A.f32(NT)
            ptmp = A.f32(512)
            sig = ropeA
            cs = A.f32(2 * NT).rearrange('p (k n) -> p k n', k=2)
            xs = A.bf16(D)
            xs_b = A.bf16(D)
            junk = A.bf16(D)
            den = A.f32(8)
            load_layer_consts(0)
            if os.environ.get('ARENA_DBG'):
                print('L0 arena used', A.mark(), 'of', NW)

            W0G = [(0, 10 * 128), (10 * 128, 14 * 128), (26 * 128, 27 * 128), (14 * 128, 26 * 128)]

            def w0grp(oc):
                c = oc * 128
                for gi, (lo, hi) in enumerate(W0G):
                    if lo <= c < hi:
                        return gi
            for gi, (lo, hi) in enumerate(W0G):
                for kc in range(8):
                    DMA('pool', 'w0_%d_%d' % (gi, kc), w0[:, kc, lo:hi], w0_d[kc * 128:(kc + 1) * 128, lo:hi],
                        writes=[('w0', gi, kc)])

            for kc in range(4):
                DMA('pool', 'wpw2_%d' % kc, wpw2[:, kc, :], wpw2_d[kc * 128:(kc + 1) * 128, :], writes=[('wpw2', kc)])
            for kc in range(8):
                DMA('pool', 'wout0_%d' % kc, wout0[:, kc, :], wout0_d[kc * 128:(kc + 1) * 128, :], writes=[('wout0', kc)])

            def w0keys_of(oc):
                return [('w0', w0grp(oc), kc) for kc in range(8)]
            S.op('pool', 'memset', dict(ap=ubuf[:, :, 0:30], constant=0.0), [], ['ubuf'])
            S.op('pool', 'memset', dict(ap=vext[:, :, :, 64:66], constant=1.0), [], ['vones'])
            ACT(esink, rows[:, RS_SINK:RS_SINK + 8], AF.Exp, ['sinks'], ['esink'])
            rope3 = rope_d.rearrange('p (k n) -> p k n', k=2)

            pbi = [0]
            NDUM0 = int(os.environ.get('NDUM0', '1'))

            def next_bank():
                b_ = [1, 2, 3][pbi[0] % 3]
                pbi[0] += 1
                return b_

            STG = int(os.environ.get('L0_STAGE', '99'))

            def l0_norm(b0, nb):
                for bi in range(nb):
                    norm_transpose(b0 + bi, V_GPRE0, xs, junk, xnT[:, :, bi * 128:(bi + 1) * 128],
                                   (lambda kc, bi=bi: ('xnT', bi, kc)), 0)

            def l0_chunk(b0, nb, nxt=None):
                t0 = b0 * 128
                nt = nb * 128
                xkeys = [('xnT', bi, kc) for bi in range(nb) for kc in range(8)]

                def proj(bk, coff, oc):
                    for kc in range(8):
                        MM(bank(bk)[:, coff:coff + nt], w0[:, kc, oc * 128:(oc + 1) * 128], xnT[:, kc, 0:nt], kc == 0, kc == 7,
                           xkeys + w0keys_of(oc), [('ps', bk)])

                if STG <= 0:
                    return
                if os.environ.get('NOCS') is None:
                    DMA('sp', 'cs', cs[:, :, 0:nt], rope3[:, :, t0:t0 + nt], writes=['cs'])
                if STG <= 1:
                    return
                for i in range(5):
                    bk = next_bank()
                    oc_a, oc_b = (C_Q + i, C_QS + i) if i < 4 else (C_K, C_KS)
                    proj(bk, 0, oc_a)
                    proj(bk, 256, oc_b)
                    TT('dve', ropeA[:, 0:nt], bank(bk)[:, 0:nt], cs[:, 0, 0:nt], ALU.mult, [('ps', bk), 'cs'], ['ropeA'])
                    TT('dve', ropeB[:, 0:nt], bank(bk)[:, 256:256 + nt], cs[:, 1, 0:nt], ALU.mult, [('ps', bk), 'cs'], ['ropeB'])
                    if i < 4:
                        TT('pool', qT[:, i, 0:nt], ropeA[:, 0:nt], ropeB[:, 0:nt], ALU.add, ['ropeA', 'ropeB'], [('qT', i)])
                    else:
                        TT('pool', kT[:, t0:t0 + nt], ropeA[:, 0:nt], ropeB[:, 0:nt], ALU.add, ['ropeA', 'ropeB'],
                           [('kT', b0 + bi) for bi in range(nb)])
                if STG <= 2:
                    return
                for i in range(0, 4, 2):
                    bk = next_bank()
                    proj(bk, 0, C_GA + i)
                    proj(bk, 256, C_GA + i + 1)
                    for u in range(2):
                        ACT(gaT[:, i + u, 0:nt], bank(bk)[:, u * 256:u * 256 + nt], AF.Silu, [('ps', bk)], [('gaT', i + u)])
                for bi in range(nb):
                    bk = next_bank()
                    for kc in range(8):
                        MM(bank(bk)[:, 0:128], xnT[:, kc, bi * 128:(bi + 1) * 128], w0[:, kc, C_V * 128:(C_V + 1) * 128],
                           kc == 0, kc == 7, xkeys + w0keys_of(C_V), [('ps', bk)])
                    ACT(vext[:, b0 + bi, :, 0:64], bank(bk)[:, 0:128].rearrange('p (g d) -> p g d', g=2), AF.Copy,
                        [('ps', bk)], [('v', b0 + bi)])
                if STG <= 3:
                    return
                side = []

                def job_glu(i):
                    def run():
                        bk = next_bank()
                        proj(bk, 0, C_GLA + i)
                        proj(bk, 256, C_GLB + i)
                        ACT(sig[:, 0:nt], bank(bk)[:, 256:256 + nt], AF.Sigmoid, [('ps', bk)], ['ropeA'])
                        TT('dve', ubuf[:, i, 30:30 + nt], bank(bk)[:, 0:nt], sig[:, 0:nt], ALU.mult, [('ps', bk), 'ropeA'],
                           [('ubuf', i)])
                    return run

                def job_gb(i):
                    def run():
                        bk = next_bank()
                        proj(bk, 0, C_GB + i)
                        proj(bk, 256, C_GB + i + 1)
                        for u in range(2):
                            ACT(gbT[:, i + u, 0:nt], bank(bk)[:, u * 256:u * 256 + nt], AF.Silu, [('ps', bk)], [('gbT', i + u)])
                    return run
                for i in range(4):
                    side.append(job_glu(i))
                for i in range(0, 4, 2):
                    side.append(job_gb(i))
                n_iter = [0]
                for bi in range(nb):
                    n = b0 + bi
                    for g in range(2):
                        rhs_q = qT[g * 64:(g + 1) * 64, :, bi * 128:(bi + 1) * 128]
                        tiles = [(n, K_MCUR0 if n == 0 else K_MCUR)]
                        if n >= 2:
                            tiles.append((n - 1, K_MPREV))
                        if n >= 1:
                            tiles.append((0, K_MMETA))
                        for ti, (kb, mcol) in enumerate(tiles):
                            bk = 4 + ti
                            MM(bank(bk), kT[g * 64:(g + 1) * 64, kb * 128:(kb + 1) * 128], rhs_q, True, False,
                               [('qT', i) for i in range(4)] + [('kT', kb)], [('ps', bk)])
                            MM(bank(bk), ident, cst[:, mcol:mcol + 512], False, True, ['cst', 'cstm'], [('ps', bk)])
                            ACT(pT[ti], bank(bk), AF.Exp, [('ps', bk)], [('pT', ti)], scale=0.125)
                            for _ in range(NDUM0):
                                MM(bank(0), ident, cst[:, K_MCUR:K_MCUR + 512], True, True, ['cst', 'cstm'], [('ps', 0)])
                        for _ in range(2 if n_iter[0] < 2 else 1):
                            if side:
                                side.pop(0)()
                        n_iter[0] += 1
                        ob = bank(7)[:, 0:260].rearrange('p (i d) -> p i d', i=4)
                        for i in range(4):
                            for ti, (kb, mcol) in enumerate(tiles):
                                MM(ob[:, i, :], pT[ti][:, i * 128:(i + 1) * 128], vext[:, kb, g, 0:65], ti == 0,
                                   ti == len(tiles) - 1, [('pT', ti), ('v', kb), 'vones'], [('ps', 7)])
                        TT('dve', den[:, 0:4], ob[:, :, 64], esink[:, 4 * g:4 * g + 4], ALU.add, [('ps', 7), 'esink'], ['den'])
                        S.op('dve', 'reciprocal', dict(out=den[:, 4:8], in_=den[:, 0:4]), ['den'], ['rden'])
                        TT('dve', a_tok[:, g * 256:(g + 1) * 256].rearrange('p (i d) -> p i d', i=4), ob[:, :, 0:64],
                           den[:, 4:8].unsqueeze(2).to_broadcast([128, 4, 64]), ALU.mult, [('ps', 7), 'rden'], [('a_tok', g)])
                    pb = bankbf(0)
                    for c in range(4):
                        TR(pb[:, c * 128:(c + 1) * 128], a_tok[:, c * 128:(c + 1) * 128], [('a_tok', 0), ('a_tok', 1), 'cst'],
                           [('ps', 0)])
                    TT('dve', mixT[:, 0:4, bi * 128:(bi + 1) * 128], pb[:, 0:512].rearrange('p (c n) -> p c n', c=4),
                       gaT[:, :, bi * 128:(bi + 1) * 128], ALU.mult, [('ps', 0)] + [('gaT', i) for i in range(4)],
                       [('mixT', bi)])
                if STG <= 4:
                    return
                while side:
                    side.pop(0)()
                if STG <= 5:
                    return
                def stat_mm(i):
                    MM(bank(4)[:, 0:nt], ones512, ybf[:, i, 0:nt], i == 0, i == 3, [('ybf', i), 'cst'], [('ps', 4)])
                    MM(bank(5)[:, 0:nt], ones512, ysq[:, i, 0:nt], i == 0, i == 3, [('ysq', i), 'cst'], [('ps', 5)])
                for i in range(4):
                    bk = next_bank()
                    for j0 in range(0, 31, 4):
                        kk = min(4, 31 - j0)
                        hb = dgi[0] % 2
                        dgi[0] += 1
                        dgv = dg_all[:, hb * 512:(hb + 1) * 512].rearrange('p (k m) -> p k m', k=4)
                        wb = vecs[:, V_CW + i * 31 + j0:V_CW + i * 31 + j0 + kk]
                        TT('dve', dgv[:, 0:kk, :], ident.unsqueeze(1).to_broadcast([128, kk, 128]),
                           wb.unsqueeze(2).to_broadcast([128, kk, 128]), ALU.mult, ['cst', 'vecs'], [('dg', hb)])
                        for jj in range(kk):
                            j = j0 + jj
                            MM(bank(bk)[:, 0:nt], dgv[:, jj, :], ubuf[:, i, j:j + nt], j == 0, j == 30,
                               [('dg', hb), ('ubuf', i), ('uhalo', i)], [('ps', bk)])
                    if nxt is not None and i in (1, 2) and i - 1 < nxt[1]:
                        norm_tr(V_GPRE0, xs if i == 1 else xs_b, xnT[:, :, (i - 1) * 128:i * 128],
                                (lambda kc, bi=i - 1: ('xnT', bi, kc)), 0, par=i - 1)
                    if nxt is not None and i in (0, 1) and i < nxt[1]:
                        norm_stats(nxt[0] + i, xs if i == 0 else xs_b, junk, par=i)
                    ACT(acc[:, i, 0:nt], bank(bk)[:, 0:nt], AF.Identity, [('ps', bk), 'vecs'], [('acc', i)],
                        bias=vecs[:, V_CB + i:V_CB + i + 1])
                    ACT(ybf[:, i, 0:nt], acc[:, i, 0:nt], AF.Copy, [('acc', i)], [('ybf', i)])
                    ACT(ysq[:, i, 0:nt], acc[:, i, 0:nt], AF.Square, [('acc', i)], [('ysq', i)])
                    if i >= 1:
                        stat_mm(i - 1)
                stat_mm(3)
                for i in range(4):
                    CP('pool', ubuf[:, i, 0:30], ubuf[:, i, nt:nt + 30], [('ubuf', i)], [('uhalo', i)])
                ACT(mean_sb[:, 0:nt], bank(4)[:, 0:nt], AF.Copy, [('ps', 4)], ['mean'])
                TT('dve', var_sb[:, 0:nt], mean_sb[:, 0:nt], mean_sb[:, 0:nt], ALU.mult, ['mean'], ['var'])
                TT('dve', var_sb[:, 0:nt], bank(5)[:, 0:nt], var_sb[:, 0:nt], ALU.subtract, [('ps', 5), 'var'], ['var'])
                ACT(var_sb[:, 0:nt], var_sb[:, 0:nt], AF.Ln, ['var'], ['var'], bias=1e-5)
                ACT(rstdc[:, 0:nt], var_sb[:, 0:nt], AF.Exp, ['var'], ['var'], scale=-0.5)
                for i in range(4):
                    TT('dve', zt[:, i, 0:nt], acc[:, i, 0:nt], mean_sb[:, 0:nt], ALU.subtract, [('acc', i), 'mean'], [('acc', i)])
                    TT('dve', zt[:, i, 0:nt], zt[:, i, 0:nt], rstdc[:, 0:nt], ALU.mult, [('acc', i), 'var'], [('acc', i)])
                    ACT(cact[:, i, 0:nt], zt[:, i, 0:nt], AF.Silu, [('acc', i), 'vecs'], [('cact', i)],
                        scale=vecs[:, V_LNG + i:V_LNG + i + 1], bias=vecs[:, V_LNB + i:V_LNB + i + 1])
                for oc in range(4):
                    bk = next_bank()
                    for kc in range(4):
                        MM(bank(bk)[:, 0:nt], wpw2[:, kc, oc * 128:(oc + 1) * 128], cact[:, kc, 0:nt], kc == 0, kc == 3,
                           [('cact', kc), ('wpw2', kc)], [('ps', bk)])
                    TT('dve', mixT[:, 4 + oc, 0:nt], bank(bk)[:, 0:nt], gbT[:, oc, 0:nt], ALU.mult, [('ps', bk), ('gbT', oc)],
                       [('mixTc', oc)])
                if STG <= 7:
                    return
                obanks = [(5, 6), (4, 7)]
                for bi in range(nb):
                    outproj_mm((lambda c, bi=bi: mixT[:, c, bi * 128:(bi + 1) * 128]),
                               [('mixT', bi)] + [('mixTc', oc) for oc in range(4)], wout0, 'wout0', obanks[bi])
                for bi in range(nb):
                    outproj_epi(b0 + bi, junk, ptmp, obanks[bi], sc=8 + 8 * bi)

            l0_norm(0, 2)
            for b0 in range(0, NB, 2):
                nb0 = b0 + 2
                l0_chunk(b0, min(2, NB - b0), (nb0, min(2, NB - nb0)) if nb0 < NB else None)
            S.barrier()
            A.reset(base_mark)

        if do_l1:
            xn_flat = A.bf16(8 * T)
            xnT1 = xn_flat.rearrange('p (k n) -> p k n', k=8)
            wout1 = xn_flat[:, 0:8 * D].rearrange('p (k n) -> p k n', k=8)
            mixT1 = A.bf16(8 * SEQ).rearrange('p (k n) -> p k n', k=8)
            qT1s = [A.bf16(T) for _ in range(2)]
            kT1s = [A.bf16(T) for _ in range(2)]
            gT1s = [A.bf16(SEQ) for _ in range(2)]
            v1s = [A.bf16(NB * 128).rearrange('p (b d) -> p b d', b=NB) for _ in range(2)]
            w1b = A.bf16(8 * 512).rearrange('p (k n) -> p k n', k=8)
            tmpg = A.f32(256)
            tm = A.mark()
            e_sb = [[A.f32(512) for _ in range(2)] for _ in range(2)]
            eP = [A.f32(512) for _ in range(2)]
            sp = [[A.bf16(512) for _ in range(2)] for _ in range(2)]
            a_sb = [[A.bf16(512) for _ in range(2)] for _ in range(2)]
            tm_end = A.mark()
            A.reset(tm)
            xs = A.bf16(D)
            xs_b = A.bf16(D)
            junk = A.bf16(D)
            ptmp = A.f32(512)
            A.reset(max(tm_end, A.mark()))
            load_layer_consts(1)
            if os.environ.get('ARENA_DBG'):
                print('L1 arena used', A.mark(), 'of', NW)

            def load_w1(c):
                for kc in range(8):
                    DMA('pool', 'w0_0_%d' % kc, w1b[:, kc, :], w1_d[c, kc * 128:(kc + 1) * 128, :], writes=[('w1', kc)])
            ZB = [[0, 1], [2, 3]]
            PB = [4, 5]
            OBK = 6
            JUNK = 7
            NDUM = int(os.environ.get('NDUM', '1'))
            ntiles_tok = [(tt * 512, min(512, T - tt * 512)) for tt in range(5)]
            xkeys_all = [('xnT1', b, kc) for b in range(NB) for kc in range(8)]

            tiles = []
            for j in range(4):
                kmax = 4 * j + 4
                for kb in range(kmax, -1, -1):
                    tiles.append(dict(j=j, kb=kb, first=(kb == kmax), last=(kb == 0)))
            ntl = len(tiles)

            def proj_groups(c):
                st_ = c % 2
                qd, kd, gd, vd = qT1s[st_], kT1s[st_], gT1s[st_], v1s[st_]
                groups = []

                def g_qkg(which, tt0, ntk):
                    def emit(bk):
                        col = [0, 128, 384][which]
                        blks = range(tt0 // 128, (tt0 + ntk) // 128)
                        for kc in range(8):
                            MM(bank(bk)[:, 0:ntk], w1b[:, kc, col:col + 128], xnT1[:, kc, tt0:tt0 + ntk], kc == 0, kc == 7,
                               [('xnT1', b, kc) for b in blks] + [('w1', kc)], [('ps', bk)])

                        def evac():
                            if which == 0:
                                TS('dve', qd[:, tt0:tt0 + ntk], bank(bk)[:, 0:ntk], 0.125, None, ALU.mult, None, [('ps', bk)],
                                   [('qT1', st_)])
                            elif which == 1:
                                CP('dve', kd[:, tt0:tt0 + ntk], bank(bk)[:, 0:ntk], [('ps', bk)], [('kT1', st_)])
                            else:
                                gsl = gd[:, tt0 - 128:tt0 - 128 + ntk]
                                CP('dve', gsl, bank(bk)[:, 0:ntk], [('ps', bk)], [('gT1', st_)])
                                CP('dve', tmpg[:, 0:ntk], bank(bk)[:, 0:ntk], [('ps', bk)], ['tmpg'])
                                ACT(tmpg[:, 0:ntk], tmpg[:, 0:ntk], AF.Exp, ['tmpg'], ['tmpg'], scale=-1.0)
                                TS('dve', tmpg[:, 0:ntk], tmpg[:, 0:ntk], 1.0, 1e30, ALU.add, ALU.min, ['tmpg'], ['tmpg'])
                                S.op('dve', 'reciprocal', dict(out=tmpg[:, 0:ntk], in_=tmpg[:, 0:ntk]), ['tmpg'], ['tmpg'])
                                TT('dve', gsl, gsl, tmpg[:, 0:ntk], ALU.mult, [('gT1', st_), 'tmpg'], [('gT1', st_)])
                        return evac
                    return emit

                def g_v(b):
                    def emit(bk):
                        for kc in range(8):
                            MM(bank(bk)[:, 0:128], xnT1[:, kc, b * 128:(b + 1) * 128], w1b[:, kc, 256:384], kc == 0, kc == 7,
                               [('xnT1', b, kc), ('w1', kc)], [('ps', bk)])
                        CP('dve', vd[:, b, :], bank(bk)[:, 0:128], [('ps', bk)], [('v1', st_)])
                    return emit
                for which in (0, 1):
                    for tt0 in range(0, T, 256):
                        ntk = min(256, T - tt0)
                        groups.append(((tt0 + ntk) // 128 - 1, g_qkg(which, tt0, ntk)))
                def g_v2(b):
                    nbk = min(2, NB - b)

                    def emit(bk):
                        for u in range(nbk):
                            for kc in range(8):
                                MM(bank(bk)[:, u * 128:(u + 1) * 128], xnT1[:, kc, (b + u) * 128:(b + u + 1) * 128],
                                   w1b[:, kc, 256:384], kc == 0, kc == 7, [('xnT1', b + u, kc), ('w1', kc)], [('ps', bk)])

                        def evac():
                            CP('dve', vd[:, b:b + nbk, :], bank(bk)[:, 0:nbk * 128].rearrange('p (u d) -> p u d', u=nbk),
                               [('ps', bk)], [('v1', st_)])
                        return evac
                    return emit
                for b in range(0, NB, 2):
                    groups.append((min(b + 1, NB - 1), g_v2(b)))
                for tt0 in range(128, T, 256):
                    groups.append(((tt0 + 256) // 128 - 1, g_qkg(2, tt0, 256)))
                return groups

            def l1_chunk(c):
                st_ = c % 2
                qT1, kT1, gT1, v1 = qT1s[st_], kT1s[st_], gT1s[st_], v1s[st_]
                kq, kk, kg, kv = ('qT1', st_), ('kT1', st_), ('gT1', st_), ('v1', st_)
                if c < 7:
                    load_w1(c + 1)
                    nxt = proj_groups(c + 1)
                else:
                    for kc in range(8):
                        DMA('pool', 'w0_1_%d' % kc, wout1[:, kc, :], wout1_d[kc * 128:(kc + 1) * 128, :], reads=[],
                            writes=xkeys_all + [('wout1', kc)])
                    nxt = []
                nxt_i = [0]
                pend_ev = []

                def fill(i):
                    if i >= 3 and nxt_i[0] < len(nxt) and c0_of(i) <= 128:
                        pend_ev.append(nxt[nxt_i[0]][1](JUNK))
                        nxt_i[0] += 1
                        return True
                    dummies(NDUM)
                    return False

                def c0_of(i):
                    t = tiles[i]
                    return max(0, t['kb'] - (4 * t['j'] + 1)) * 128

                def s1(s, i):
                    t = tiles[i]
                    j, kb = t['j'], t['kb']
                    q0 = (4 * j + 1) * 128
                    zb = ZB[s][i % 2]
                    d = kb - (4 * j + 1)
                    c0 = c0_of(i)
                    masked = (d >= 0) or (kb == 0)
                    MM(bank(zb)[:, c0:512], kT1[s * 64:(s + 1) * 64, kb * 128:(kb + 1) * 128],
                       qT1[s * 64:(s + 1) * 64, q0 + c0:q0 + 512], True, not masked, [kq, kk], [('ps', zb)])
                    if d >= 0:
                        MM(bank(zb)[:, c0:c0 + 128], ident, cst[:, K_SBM:K_SBM + 128], False, True, ['cst', 'cstm'], [('ps', zb)])
                    elif kb == 0:
                        MM(bank(zb), ident, cst[:, K_SBM + 128:K_SBM + 640], False, True, ['cst', 'cstm'], [('ps', zb)])

                def s2a(s, i):
                    zb = ZB[s][i % 2]
                    c0 = c0_of(i)
                    ACT(e_sb[s][i % 2][:, c0:512], bank(zb)[:, c0:512], AF.Exp, [('ps', zb)], [('e', s, i % 2)])

                def s2b(s, i):
                    c0 = c0_of(i)
                    ACT(sp[s][i % 2][:, c0:512], e_sb[s][i % 2][:, c0:512], AF.Ln, [('e', s, i % 2)], [('sp', s, i % 2)], bias=1.0)

                def s3a(s, i):
                    t = tiles[i]
                    pbk = PB[s]
                    if not t['first']:
                        cp = c0_of(i - 1)
                        MM(bank(pbk)[:, cp:512], comp, sp[s][(i - 1) % 2][:, cp:512], False, False,
                           [('sp', s, (i - 1) % 2), 'cst'], [('ps', pbk)], skip_group_check=True)

                def s3(s, i):
                    t = tiles[i]
                    pbk = PB[s]
                    c0 = c0_of(i)
                    MM(bank(pbk)[:, c0:512], tri, sp[s][i % 2][:, c0:512], t['first'], False, [('sp', s, i % 2), 'cst'],
                       [('ps', pbk)], skip_group_check=True)

                def s4(s, i):
                    pbk = PB[s]
                    c0 = c0_of(i)
                    ACT(eP[s][:, c0:512], bank(pbk)[:, c0:512], AF.Exp, [('ps', pbk)], [('eP', s)], scale=-1.0)

                def s5(s, i):
                    c0 = c0_of(i)
                    TT('dve', a_sb[s][i % 2][:, c0:512], e_sb[s][i % 2][:, c0:512], eP[s][:, c0:512], ALU.mult,
                       [('e', s, i % 2), ('eP', s)], [('a', s, i % 2)])

                def s6(s, i):
                    t = tiles[i]
                    c0 = c0_of(i)
                    MM(bank(OBK)[s * 64:(s + 1) * 64, c0:512], v1[:, t['kb'], s * 64:(s + 1) * 64], a_sb[s][i % 2][:, c0:512],
                       t['first'], t['last'], [('a', s, i % 2), kv], [('ps', OBK)], skip_group_check=True)

                def evac(i):
                    j = tiles[i]['j']
                    q0 = (4 * j + 1) * 128
                    TT('dve', mixT1[:, c, q0 - 128:q0 - 128 + 512], bank(OBK), gT1[:, q0 - 128:q0 - 128 + 512], ALU.mult,
                       [('ps', OBK), kg], [('mixT1', c)])

                def dummies(n):
                    for _ in range(n):
                        MM(bank(JUNK), tri, cst[:, K_SBM + 128:K_SBM + 640], True, True, ['cst', 'cstm'], [('ps', JUNK)])

                s1(0, 0)
                s1(1, 0)
                filled = False
                for i in range(ntl):
                    s3a(0, i)
                    s3a(1, i)
                    if i + 1 < ntl:
                        s1(0, i + 1)
                        s1(1, i + 1)
                    if not filled:
                        dummies(NDUM)
                    s2a(0, i)
                    s2a(1, i)
                    s2b(0, i)
                    s2b(1, i)
                    s3(0, i)
                    s3(1, i)
                    if i >= 1:
                        s6(0, i - 1)
                        s6(1, i - 1)
                        if tiles[i - 1]['last']:
                            evac(i - 1)
                    filled = fill(i)
                    s4(0, i)
                    s4(1, i)
                    s5(0, i)
                    s5(1, i)
                    while pend_ev:
                        pend_ev.pop(0)()
                s6(0, ntl - 1)
                s6(1, ntl - 1)
                evac(ntl - 1)
                while nxt_i[0] < len(nxt):
                    nxt[nxt_i[0]][1](nxt_i[0] % 4)()
                    nxt_i[0] += 1

            load_w1(0)
            pend = proj_groups(0)
            n_g = 8
            late, pend = pend[-n_g:], pend[:-n_g]
            n_emit = [0]
            norm_stats(0, xs, junk, par=0)
            for b in range(NB):
                if b + 1 < NB:
                    norm_stats(b + 1, xs if (b + 1) % 2 == 0 else xs_b, junk, par=(b + 1) % 2)
                norm_tr(V_GPRE1, xs if b % 2 == 0 else xs_b, xnT1[:, :, b * 128:(b + 1) * 128],
                        (lambda kc, b=b: ('xnT1', b, kc)), b % 2, par=b % 2)
                keep = []
                for (need, g) in pend:
                    if need <= b - 1:
                        g([2, 3][n_emit[0] % 2])()
                        n_emit[0] += 1
                    else:
                        keep.append((need, g))
                pend = keep
            for gi, (need, g) in enumerate(pend + late):
                g(gi % 4)()
            pend = []
            assert not pend
            S.barrier()
            for c in range(8):
                l1_chunk(c)
            S.barrier()
            outs = []
            for pi, b in enumerate(range(1, NB, 2)):
                bq = [(0, 1), (2, 3)] if pi % 2 == 0 else [(4, 5), (6, 7)]
                for u in range(2):
                    outproj_mm((lambda c, bb=b + u: mixT1[:, c, (bb - 1) * 128:bb * 128]), [('mixT1', c) for c in range(8)],
                               wout1, 'wout1', bq[u])
                for u in range(2):
                    outproj_epi(b + u, junk, ptmp, bq[u], sc=8 + 8 * ((2 * pi + u) % 4))
                    outs.append(DMA('sp', 'h%d' % (b + u), out_d[(b + u - 1) * 128:(b + u) * 128, :], h[:, b + u, :],
                                    reads=[('h', b + u)]))
            S.wait_all('sp', outs)
        else:
            outs = []
            for b in range(NB):
                outs.append(DMA('sp', 'out%d' % b, hout_d[b * 128:(b + 1) * 128, :], h[:, b, :], reads=[('h', b)]))
            S.wait_all('sp', outs)
        S.run()
    return nc


def _consts():
    c = np.zeros((128, NCD), np.float32)
    K_SBM, K_MCUR, K_MPREV, K_MMETA, K_MCUR0 = D_L1M, D_L0M, D_L0M + 512, D_L0M + 1024, D_L0M + 1536
    r = np.arange(128)
    c[:, K_ID:K_ID + 128] = np.eye(128, dtype=np.float32)
    c[:, K_TRI:K_TRI + 128] = (r[:, None] >= r[None, :]).astype(np.float32)
    c[:, K_COMP:K_COMP + 128] = (r[:, None] < r[None, :]).astype(np.float32)
    c[:, K_ONES:K_ONES + 128] = 1.0 / 512.0
    c[:, K_SBM:K_SBM + 128] = np.where(r[:, None] < r[None, :], 0.0, NEG_SB)
    mp = np.zeros((128, 512), np.float32)
    mp[:PAD, :] = NEG_SB
    c[:, K_SBM + 128:K_SBM + 640] = mp
    cur = np.where(r[:, None] <= r[None, :], 0.0, NEG).astype(np.float32)
    prev = np.where(r[:, None] > r[None, :], 0.0, NEG).astype(np.float32)
    c[:, K_MCUR:K_MCUR + 512] = np.tile(cur, (1, 4))
    c[:, K_MPREV:K_MPREV + 512] = np.tile(prev, (1, 4))
    mm = np.zeros((128, 512), np.float32)
    mm[:PAD, :] = NEG
    c[:, K_MMETA:K_MMETA + 512] = mm
    cur0 = cur.copy()
    cur0[:PAD, :] = NEG
    c[:, K_MCUR0:K_MCUR0 + 512] = np.tile(cur0, (1, 4))
    return c


def _rope_tables():
    p = np.arange(128)
    d = p % 64
    i = d % 32
    inv = (10000.0 ** (-(i.astype(np.float32)) / np.float32(32.0))).astype(np.float32)
    pos = np.maximum(np.arange(T) - PAD, 0).astype(np.float32)
    ang = (pos[None, :] * inv[:, None]).astype(np.float32)
    cos = np.cos(ang).astype(np.float32)
    sin = np.sin(ang).astype(np.float32)
    sgn = np.where(d < 32, -1.0, 1.0).astype(np.float32)[:, None]
    return np.ascontiguousarray(np.concatenate([cos, sin * sgn], axis=1).astype(np.float32))


def _swap_halves(w):
    n = w.shape[1] // 64
    w4 = w.reshape(w.shape[0], n, 2, 32)
    return w4[:, :, ::-1, :].reshape(w.shape[0], n * 64)


def _layout_w0(w_in):
    q, k, v, ga, glu, gb = np.split(w_in, [512, 640, 768, 1280, 2304], axis=1)
    qh = q.reshape(D, 8, 64)
    qperm = np.stack([qh[:, [i, 4 + i], :].reshape(D, 128) for i in range(4)], axis=1).reshape(D, 512)
    qs = _swap_halves(qperm)
    ks = _swap_halves(k)
    return np.ascontiguousarray(np.concatenate([qperm, qs, k, ks, ga, glu, gb, v], axis=1).astype(np.float32))


def _layout_w1(w_in):
    q, k, v, g = np.split(w_in, 4, axis=1)
    out = np.empty((8, D, 512), np.float32)
    for c in range(8):
        sl = slice(c * 128, (c + 1) * 128)
        out[c] = np.concatenate([q[:, sl], k[:, sl], v[:, sl], g[:, sl]], axis=1)
    return out


def _fm(vec, nchunk):
    return np.ascontiguousarray(np.asarray(vec, np.float32).reshape(nchunk, 128).T)


_NC_CACHE = {}


def kernel(x, meta_tokens, ab_pre_norm, ab_w_in, ab_sinks, ab_conv_w, ab_conv_b, ab_conv_ln_g, ab_conv_ln_b, ab_w_pw2,
           ab_w_out, ab_post_norm, sb_pre_norm, sb_w_in, sb_w_out, sb_post_norm):
    f = lambda a: np.ascontiguousarray(np.asarray(a, dtype=np.float32))
    x = f(x)
    vecs = np.zeros((128, NVEC), np.float32)
    vecs[:, V_GPRE0:V_GPRE0 + 8] = _fm(f(ab_pre_norm)[0], 8)
    vecs[:, V_GPRE1:V_GPRE1 + 8] = _fm(f(sb_pre_norm)[0], 8)
    cw = f(ab_conv_w)[0]
    vecs[:, V_CW:V_CW + 124] = cw.T.reshape(4, 128, 31).transpose(1, 0, 2).reshape(128, 124)
    vecs[:, V_CB:V_CB + 4] = _fm(f(ab_conv_b)[0], 4)
    vecs[:, V_LNG:V_LNG + 4] = _fm(f(ab_conv_ln_g)[0], 4)
    vecs[:, V_LNB:V_LNB + 4] = _fm(f(ab_conv_ln_b)[0], 4)
    rows = np.zeros((128, NROW), np.float32)
    rows[:, R_GPOST0:R_GPOST0 + D] = f(ab_post_norm)[0][None, :]
    rows[:, R_GPOST1:R_GPOST1 + D] = f(sb_post_norm)[0][None, :]
    rows[:, R_SINK:R_SINK + 8] = f(ab_sinks)[0][None, :]
    shared = {
        'meta': f(meta_tokens), 'w0': _layout_w0(f(ab_w_in)[0]), 'wpw2': f(ab_w_pw2)[0], 'wout0': f(ab_w_out)[0],
        'w1': _layout_w1(f(sb_w_in)[0]), 'wout1': f(sb_w_out)[0], 'vecs': vecs, 'rows': rows, 'cst': _consts(),
        'rope': _rope_tables(),
    }
    if 'nc' not in _NC_CACHE:
        _NC_CACHE['nc'] = build(True, True)
    nc = _NC_CACHE['nc']
    in_maps = [dict(shared, x=x[b]) for b in range(8)]
    res = run_bass_kernel_spmd(nc, in_maps, core_ids=list(range(8)))
    return np.stack([res.results[b]['out'] for b in range(8)], axis=0).astype(np.float32)
```

```python
import os
import numpy as np
from contextlib import ExitStack
import concourse.bass as bass
import concourse.mybir as mybir
from concourse.bass_utils import run_bass_kernel_spmd

F32 = mybir.dt.float32
BF16 = mybir.dt.bfloat16
AF = mybir.ActivationFunctionType
ALU = mybir.AluOpType

D = 1024
T = 2176
NB = 17
PAD = 112
SEQ = 2048
NEG = -240000.0
NEG_SB = -30000.0
W0C = 27 * 128

ENGS = ['pe', 'act', 'dve', 'pool', 'sp']


class Sched:
    def __init__(self, nc, stack):
        self.nc = nc
        self.stack = stack
        self.q = {e: [] for e in ENGS}
        self.cnt = {e: 0 for e in ENGS}
        self.waited = {e: {} for e in ENGS}
        self.lastw = {}
        self.readers = {}
        self.semh = {}
        self.dval = {}
        for e in ENGS[:4]:
            self.semh['E_' + e] = stack.enter_context(nc.semaphore('E_' + e))

    def _deps(self, reads, writes, eng=None):
        deps = []
        for k in reads:
            t = self.lastw.get(k)
            if t is not None:
                deps.append(t)
            if isinstance(k, tuple) and k[0] == 'ps':
                deps.extend(t2 for t2 in self.readers.get(k, {}).values() if t2[2] != eng)
        for k in writes:
            t = self.lastw.get(k)
            if t is not None:
                deps.append(t)
            deps.extend(self.readers.get(k, {}).values())
        return deps

    def _wait(self, eng, deps):
        w = self.waited[eng]
        for (sk, v, src) in deps:
            if src == 'pe' and eng == 'pe':
                continue
            if w.get(sk, 0) >= v:
                continue
            w[sk] = v
            h = self.semh[sk]
            self.q[eng].append(lambda e, h=h, v=v: e.wait_ge(h, v))

    def _record(self, tok, reads, writes):
        for k in reads:
            self.readers.setdefault(k, {})[tok[0]] = tok
        for k in writes:
            self.lastw[k] = tok
            self.readers[k] = {}

    def op(self, eng, name, kw, reads=(), writes=(), extra=()):
        self._wait(eng, self._deps(reads, writes, eng) + list(extra))
        self.cnt[eng] += 1
        sk = 'E_' + eng
        tok = (sk, self.cnt[eng], eng)
        h = self.semh[sk]
        self.q[eng].append(lambda e, name=name, kw=kw, h=h: getattr(e, name)(**kw).then_inc(h, 1))
        self._record(tok, reads, writes)
        return tok

    def dma(self, eng, name, kw, reads=(), writes=(), extra=()):
        sk = 'D_' + name
        self._wait(eng, self._deps(reads, writes) + list(extra))
        if sk not in self.semh:
            self.semh[sk] = self.stack.enter_context(self.nc.semaphore(sk))
            self.dval[sk] = 0
        if self.dval[sk] > 0:
            self._wait(eng, [(sk, self.dval[sk], 'dma')])
        self.dval[sk] += 16
        tok = (sk, self.dval[sk], 'dma')
        h = self.semh[sk]
        self.q[eng].append(lambda e, kw=kw, h=h: e.dma_start(**kw).then_inc(h, 16))
        self._record(tok, reads, writes)
        return tok

    def barrier(self):
        toks = []
        for e in ENGS[:4]:
            if self.cnt[e] > 0:
                toks.append(('E_' + e, self.cnt[e], e + '_b'))
        for sk, v in self.dval.items():
            toks.append((sk, v, 'dma'))
        for e in ENGS:
            self._wait(e, toks)

    def wait_all(self, eng, toks):
        self._wait(eng, list(toks))

    def run(self):
        nc = self.nc
        with nc.Block() as block:
            @block.tensor
            def _(e):
                for f in self.q['pe']:
                    f(e)

            @block.scalar
            def _(e):
                for f in self.q['act']:
                    f(e)

            @block.vector
            def _(e):
                for f in self.q['dve']:
                    f(e)

            @block.gpsimd
            def _(e):
                for f in self.q['pool']:
                    f(e)

            @block.sync
            def _(e):
                for f in self.q['sp']:
                    f(e)


class Arena:
    def __init__(self, ap, nwords):
        self.ap = ap
        self.n = nwords
        self.off = 0

    def f32(self, n):
        assert self.off + n <= self.n, ('SBUF arena overflow', self.off, n, self.n)
        a = self.ap[:, self.off:self.off + n]
        self.off += n
        return a

    def bf16(self, n):
        w = (n + 1) // 2
        return self.f32(w).bitcast(BF16)

    def mark(self):
        return self.off

    def reset(self, m):
        self.off = m


C_Q, C_QS, C_K, C_KS, C_GA, C_GLA, C_GLB, C_GB, C_V = 0, 4, 8, 9, 10, 14, 18, 22, 26
K_ID, K_TRI, K_COMP, K_ONES = 0, 128, 256, 384
K_MASK = 512
K_MCUR = K_MASK
K_MPREV = K_MCUR + 512
K_MMETA = K_MPREV + 512
K_MCUR0 = K_MMETA + 512
K_SBM = K_MASK
NCB = K_MASK + 5 * 512
D_L0M = 512
D_L1M = 512 + 2048
NCD = 512 + 2048 + 2560
V_GPRE0, V_GPRE1, V_CW, V_CB, V_LNG, V_LNB = 0, 8, 16, 16 + 124, 16 + 128, 16 + 132
NVEC = 16 + 136
R_GPOST0, R_GPOST1, R_SINK = 0, 1024, 2048
NROW = 2048 + 8
RS_GPOST, RS_SINK = 0, 1024
NROWS = 1024 + 8


def build(do_l0=True, do_l1=True):
    nc = bass.Bass('TRN2', target_bir_lowering=False)
    dt = lambda name, shape, kind='ExternalInput': nc.dram_tensor(name, shape, F32, kind=kind).ap()
    x_d = dt('x', [SEQ, D])
    meta_d = dt('meta', [16, D])
    w0_d = dt('w0', [D, W0C])
    wpw2_d = dt('wpw2', [512, 512])
    wout0_d = dt('wout0', [D, D])
    w1_d = dt('w1', [8, D, 512])
    wout1_d = dt('wout1', [D, D])
    vecs_d = dt('vecs', [128, NVEC])
    rows_d = dt('rows', [128, NROW])
    cst_d = dt('cst', [128, NCD])
    rope_d = dt('rope', [128, 2 * T])
    if not do_l0:
        hin_d = dt('hin', [T, D])
    if do_l1:
        out_d = dt('out', [SEQ, D], kind='ExternalOutput')
    else:
        hout_d = dt('hout', [T, D], kind='ExternalOutput')

    with ExitStack() as st:
        NW = 53200
        arena_t = st.enter_context(nc.sbuf_tensor('arena', [128, NW], F32))
        ps = st.enter_context(nc.psum_tensor('ps', [128, 4096], F32))
        S = Sched(nc, st)
        A = Arena(arena_t, NW)

        def bank(i):
            return ps[:, i * 512:(i + 1) * 512]

        def bankbf(i):
            return ps[:, i * 512:(i + 1) * 512].bitcast(BF16)

        def ACT(out, in_, func, reads, writes, **kw):
            return S.op('act', 'activation', dict(out=out, in_=in_, func=func, **kw), reads, writes)

        def TT(eng, out, in0, in1, op, reads, writes):
            return S.op(eng, 'tensor_tensor', dict(out=out, in0=in0, in1=in1, op=op), reads, writes)

        def TS(eng, out, in0, s1, s2, op0, op1, reads, writes):
            kw = dict(out=out, in0=in0, scalar1=s1, scalar2=s2, op0=op0)
            if op1 is not None:
                kw['op1'] = op1
            return S.op(eng, 'tensor_scalar', kw, reads, writes)

        def STT(out, in0, scalar, in1, op0, op1, reads, writes):
            return S.op('dve', 'scalar_tensor_tensor', dict(out=out, in0=in0, scalar=scalar, in1=in1, op0=op0, op1=op1),
                        reads, writes)

        def MM(out, lhsT, rhs, start, stop, reads, writes, **kw):
            return S.op('pe', 'matmul', dict(out=out, lhsT=lhsT, rhs=rhs, start=start, stop=stop, **kw), reads, writes)

        def TR(out, in_, reads, writes):
            return S.op('pe', 'transpose', dict(out=out, in_=in_, identity=ident), reads, writes)

        def CP(eng, out, in_, reads, writes):
            return S.op(eng, 'tensor_copy', dict(out=out, in_=in_), reads, writes)

        def DMA(eng, name, out, in_, reads=(), writes=()):
            return S.dma(eng, name, dict(out=out, in_=in_), reads, writes)

        h = A.f32(NB * D).rearrange('p (b d) -> p b d', b=NB)
        cst = A.bf16(NCB)
        vecs = A.f32(NVEC)
        rows = A.f32(NROWS)
        stat = A.f32(64)
        esink = A.f32(8)
        ident = cst[:, K_ID:K_ID + 128]
        tri = cst[:, K_TRI:K_TRI + 128]
        comp = cst[:, K_COMP:K_COMP + 128]
        ones512 = cst[:, K_ONES:K_ONES + 128]
        base_mark = A.mark()

        DMA('pool', 'cst', cst[:, 0:512], cst_d[:, 0:512], writes=['cst'])
        DMA('sp', 'vecs', vecs, vecs_d, writes=['vecs'])
        DMA('sp', 'sinks', rows[:, RS_SINK:RS_SINK + 8], rows_d[:, R_SINK:R_SINK + 8], writes=['sinks'])

        def load_layer_consts(layer):
            if layer == 0:
                DMA('pool', 'cstm', cst[:, K_MASK:K_MASK + 2048], cst_d[:, D_L0M:D_L0M + 2048], writes=['cstm'])
                DMA('sp', 'rows', rows[:, 0:1024], rows_d[:, R_GPOST0:R_GPOST0 + 1024], writes=['rows'])
            else:
                DMA('pool', 'cstm', cst[:, K_MASK:K_MASK + 640], cst_d[:, D_L1M:D_L1M + 640], writes=['cstm'])
                DMA('sp', 'rows', rows[:, 0:1024], rows_d[:, R_GPOST1:R_GPOST1 + 1024], writes=['rows'])
        if do_l0:
            S.op('dve', 'memset', dict(ap=h[:, 0, :], constant=0.0), [], [('h', 0)])
            DMA('sp', 'h0', h[PAD:128, 0, :], meta_d, writes=[('h', 0)])
            for b in range(1, NB):
                DMA('sp', 'h%d' % b, h[:, b, :], x_d[(b - 1) * 128:b * 128, :], writes=[('h', b)])
        else:
            for b in range(NB):
                DMA('sp', 'h%d' % b, h[:, b, :], hin_d[b * 128:(b + 1) * 128, :], writes=[('h', b)])

        def norm_stats(b, xs, junk, eps=1e-6, par=0):
            hb = h[:, b, :]
            sc = 40 + 3 * par
            k0, k1, k2, kx = ('nst', par, 0), ('nst', par, 1), ('nst', par, 2), ('xs', par)
            ACT(junk, hb, AF.Square, [('h', b)], ['junk', k0], accum_out=stat[:, sc:sc + 1])
            ACT(stat[:, sc + 1:sc + 2], stat[:, sc:sc + 1], AF.Ln, [k0], [k1], scale=1.0 / D, bias=eps)
            ACT(stat[:, sc + 2:sc + 3], stat[:, sc + 1:sc + 2], AF.Exp, [k1], [k2], scale=-0.5)
            TS('dve', xs, hb, stat[:, sc + 2:sc + 3], None, ALU.mult, None, [('h', b), k2], [kx])

        def norm_tr(gcol, xs, dst_all, key_fn, pbank, par=0):
            kx = ('xs', par)
            pb = bankbf(pbank)
            for kc in range(8):
                TR(pb[:, kc * 128:(kc + 1) * 128], xs[:, kc * 128:(kc + 1) * 128], [kx, 'cst'], [('ps', pbank)])
            TT('dve', dst_all, pb.rearrange('p (k n) -> p k n', k=8), vecs[:, gcol:gcol + 8].unsqueeze(2).to_broadcast([128, 8, 128]),
               ALU.mult, [('ps', pbank), 'vecs'], [key_fn(kc) for kc in range(8)])

        def norm_transpose(b, gcol, xs, junk, dst_all, key_fn, pbank, eps=1e-6, par=0):
            hb = h[:, b, :]
            sc = 40 + 3 * par
            k0, k1, k2, kx = ('nst', par, 0), ('nst', par, 1), ('nst', par, 2), ('xs', par)
            ACT(junk, hb, AF.Square, [('h', b)], ['junk', k0], accum_out=stat[:, sc:sc + 1])
            ACT(stat[:, sc + 1:sc + 2], stat[:, sc:sc + 1], AF.Ln, [k0], [k1], scale=1.0 / D, bias=eps)
            ACT(stat[:, sc + 2:sc + 3], stat[:, sc + 1:sc + 2], AF.Exp, [k1], [k2], scale=-0.5)
            TS('dve', xs, hb, stat[:, sc + 2:sc + 3], None, ALU.mult, None, [('h', b), k2], [kx])
            pb = bankbf(pbank)
            for kc in range(8):
                TR(pb[:, kc * 128:(kc + 1) * 128], xs[:, kc * 128:(kc + 1) * 128], [kx, 'cst'], [('ps', pbank)])
            TT('dve', dst_all, pb.rearrange('p (k n) -> p k n', k=8), vecs[:, gcol:gcol + 8].unsqueeze(2).to_broadcast([128, 8, 128]),
               ALU.mult, [('ps', pbank), 'vecs'], [key_fn(kc) for kc in range(8)])

        def outproj_mm(lhs_fn, mix_keys, wout, wkey, banks):
            for half in range(2):
                bk = banks[half]
                for c in range(8):
                    MM(bank(bk), lhs_fn(c), wout[:, c, half * 512:(half + 1) * 512], c == 0, c == 7,
                       list(mix_keys) + [(wkey, c)], [('ps', bk)])

        def outproj_epi(b, junk, ptmp, banks, sc=8, eps=1e-6):
            for half in range(2):
                bk = banks[half]
                ACT(junk[:, 0:512], bank(bk), AF.Square, [('ps', bk)], ['junk', ('st', sc + half)],
                    accum_out=stat[:, sc + half:sc + half + 1])
            TT('dve', stat[:, sc + 2:sc + 3], stat[:, sc:sc + 1], stat[:, sc + 1:sc + 2], ALU.add, [('st', sc), ('st', sc + 1)],
               [('st', sc + 2)])
            ACT(stat[:, sc + 3:sc + 4], stat[:, sc + 2:sc + 3], AF.Ln, [('st', sc + 2)], [('st', sc + 3)], scale=1.0 / D, bias=eps)
            ACT(stat[:, sc + 4:sc + 5], stat[:, sc + 3:sc + 4], AF.Exp, [('st', sc + 3)], [('st', sc + 4)], scale=-0.5)
            for half in range(2):
                bk = banks[half]
                STT(ptmp[:, 0:512], bank(bk), stat[:, sc + 4:sc + 5], rows[:, half * 512:(half + 1) * 512], ALU.mult, ALU.mult,
                    [('ps', bk), ('st', sc + 4), 'rows'], ['ptmp'])
                TT('pool', h[:, b, half * 512:(half + 1) * 512], h[:, b, half * 512:(half + 1) * 512], ptmp[:, 0:512], ALU.add,
                   [('h', b), 'ptmp'], [('h', b)])

        def outproj_block(b, lhs_fn, mix_keys, wout, wkey, gpost_off, junk, ptmp, banks, eps=1e-6):
            outproj_mm(lhs_fn, mix_keys, wout, wkey, banks)
            outproj_epi(b, junk, ptmp, banks)

        if do_l0:
            w0 = A.bf16(8 * W0C).rearrange('p (k n) -> p k n', k=8)
            wpw2 = A.bf16(4 * 512).rearrange('p (k n) -> p k n', k=4)
            wout0 = A.bf16(8 * D).rearrange('p (k n) -> p k n', k=8)
            NT = 256
            xnT = A.bf16(8 * NT).rearrange('p (k n) -> p k n', k=8)
            qT = A.bf16(4 * NT).rearrange('p (k n) -> p k n', k=4)
            kT = A.bf16(T)
            vext = A.bf16(NB * 2 * 66).rearrange('p (b g d) -> p b g d', b=NB, g=2)
            gaT = A.bf16(4 * NT).rearrange('p (k n) -> p k n', k=4)
            gbT = A.bf16(4 * NT).rearrange('p (k n) -> p k n', k=4)
            ubuf = A.bf16(4 * (30 + NT)).rearrange('p (k n) -> p k n', k=4)
            dg_all = A.bf16(8 * 128)
            dgi = [0]
            acc = A.f32(4 * NT).rearrange('p (k n) -> p k n', k=4)
            ybf = A.bf16(4 * NT).rearrange('p (k n) -> p k n', k=4)
            ysq = A.bf16(4 * NT).rearrange('p (k n) -> p k n', k=4)
            mean_sb = A.f32(NT)
            var_sb = A.f32(NT)
            rstdc = var_sb
            zt = acc
            cact = A.bf16(4 * NT).rearrange('p (k n) -> p k n', k=4)
            mixT = A.bf16(8 * NT).rearrange('p (k n) -> p k n', k=8)
            pT = [A.bf16(512) for _ in range(3)]
            a_tok = A.bf16(512)
            ropeA = A.f32(NT)
            ropeB = A.f32(NT)
            ptmp = A.f32(512)
            sig = ropeA
            cs = A.f32(2 * NT).rearrange('p (k n) -> p k n', k=2)
            xs = A.bf16(D)
            xs_b = A.bf16(D)
            junk = A.bf16(D)
            den = A.f32(8)
            load_layer_consts(0)
            if os.environ.get('ARENA_DBG'):
                print('L0 arena used', A.mark(), 'of', NW)

            W0G = [(0, 10 * 128), (10 * 128, 14 * 128), (26 * 128, 27 * 128), (14 * 128, 26 * 128)]

            def w0grp(oc):
                c = oc * 128
                for gi, (lo, hi) in enumerate(W0G):
                    if lo <= c < hi:
                        return gi
            for gi, (lo, hi) in enumerate(W0G):
                for kc in range(8):
                    DMA('pool', 'w0_%d_%d' % (gi, kc), w0[:, kc, lo:hi], w0_d[kc * 128:(kc + 1) * 128, lo:hi],
                        writes=[('w0', gi, kc)])

            for kc in range(4):
                DMA('pool', 'wpw2_%d' % kc, wpw2[:, kc, :], wpw2_d[kc * 128:(kc + 1) * 128, :], writes=[('wpw2', kc)])
            for kc in range(8):
                DMA('pool', 'wout0_%d' % kc, wout0[:, kc, :], wout0_d[kc * 128:(kc + 1) * 128, :], writes=[('wout0', kc)])

            def w0keys_of(oc):
                return [('w0', w0grp(oc), kc) for kc in range(8)]
            S.op('pool', 'memset', dict(ap=ubuf[:, :, 0:30], constant=0.0), [], ['ubuf'])
            S.op('pool', 'memset', dict(ap=vext[:, :, :, 64:66], constant=1.0), [], ['vones'])
            ACT(esink, rows[:, RS_SINK:RS_SINK + 8], AF.Exp, ['sinks'], ['esink'])
            rope3 = rope_d.rearrange('p (k n) -> p k n', k=2)

            pbi = [0]
            NDUM0 = int(os.environ.get('NDUM0', '0'))

            def next_bank():
                b_ = [1, 2, 3][pbi[0] % 3]
                pbi[0] += 1
                return b_

            STG = int(os.environ.get('L0_STAGE', '99'))

            def l0_norm(b0, nb):
                for bi in range(nb):
                    norm_transpose(b0 + bi, V_GPRE0, xs, junk, xnT[:, :, bi * 128:(bi + 1) * 128],
                                   (lambda kc, bi=bi: ('xnT', bi, kc)), 0)

            def l0_chunk(b0, nb, nxt=None):
                t0 = b0 * 128
                nt = nb * 128
                xkeys = [('xnT', bi, kc) for bi in range(nb) for kc in range(8)]

                def proj(bk, coff, oc):
                    for kc in range(8):
                        MM(bank(bk)[:, coff:coff + nt], w0[:, kc, oc * 128:(oc + 1) * 128], xnT[:, kc, 0:nt], kc == 0, kc == 7,
                           xkeys + w0keys_of(oc), [('ps', bk)])

                if STG <= 0:
                    return
                if os.environ.get('NOCS') is None:
                    DMA('sp', 'cs', cs[:, :, 0:nt], rope3[:, :, t0:t0 + nt], writes=['cs'])
                if STG <= 1:
                    return
                for i in range(5):
                    bk = next_bank()
                    oc_a, oc_b = (C_Q + i, C_QS + i) if i < 4 else (C_K, C_KS)
                    proj(bk, 0, oc_a)
                    proj(bk, 256, oc_b)
                    TT('dve', ropeA[:, 0:nt], bank(bk)[:, 0:nt], cs[:, 0, 0:nt], ALU.mult, [('ps', bk), 'cs'], ['ropeA'])
                    TT('dve', ropeB[:, 0:nt], bank(bk)[:, 256:256 + nt], cs[:, 1, 0:nt], ALU.mult, [('ps', bk), 'cs'], ['ropeB'])
                    if i < 4:
                        TT('pool', qT[:, i, 0:nt], ropeA[:, 0:nt], ropeB[:, 0:nt], ALU.add, ['ropeA', 'ropeB'], [('qT', i)])
                    else:
                        TT('pool', kT[:, t0:t0 + nt], ropeA[:, 0:nt], ropeB[:, 0:nt], ALU.add, ['ropeA', 'ropeB'],
                           [('kT', b0 + bi) for bi in range(nb)])
                if STG <= 2:
                    return
                for i in range(0, 4, 2):
                    bk = next_bank()
                    proj(bk, 0, C_GA + i)
                    proj(bk, 256, C_GA + i + 1)
                    for u in range(2):
                        ACT(gaT[:, i + u, 0:nt], bank(bk)[:, u * 256:u * 256 + nt], AF.Silu, [('ps', bk)], [('gaT', i + u)])
                for bi in range(nb):
                    bk = next_bank()
                    for kc in range(8):
                        MM(bank(bk)[:, 0:128], xnT[:, kc, bi * 128:(bi + 1) * 128], w0[:, kc, C_V * 128:(C_V + 1) * 128],
                           kc == 0, kc == 7, xkeys + w0keys_of(C_V), [('ps', bk)])
                    ACT(vext[:, b0 + bi, :, 0:64], bank(bk)[:, 0:128].rearrange('p (g d) -> p g d', g=2), AF.Copy,
                        [('ps', bk)], [('v', b0 + bi)])
                if STG <= 3:
                    return
                side = []

                def job_glu(i):
                    def run():
                        bk = next_bank()
                        proj(bk, 0, C_GLA + i)
                        proj(bk, 256, C_GLB + i)
                        ACT(sig[:, 0:nt], bank(bk)[:, 256:256 + nt], AF.Sigmoid, [('ps', bk)], ['ropeA'])
                        TT('dve', ubuf[:, i, 30:30 + nt], bank(bk)[:, 0:nt], sig[:, 0:nt], ALU.mult, [('ps', bk), 'ropeA'],
                           [('ubuf', i)])
                    return run

                def job_gb(i):
                    def run():
                        bk = next_bank()
                        proj(bk, 0, C_GB + i)
                        proj(bk, 256, C_GB + i + 1)
                        for u in range(2):
                            ACT(gbT[:, i + u, 0:nt], bank(bk)[:, u * 256:u * 256 + nt], AF.Silu, [('ps', bk)], [('gbT', i + u)])
                    return run
                for i in range(4):
                    side.append(job_glu(i))
                for i in range(0, 4, 2):
                    side.append(job_gb(i))
                n_iter = [0]
                for bi in range(nb):
                    n = b0 + bi
                    for g in range(2):
                        rhs_q = qT[g * 64:(g + 1) * 64, :, bi * 128:(bi + 1) * 128]
                        tiles = [(n, K_MCUR0 if n == 0 else K_MCUR)]
                        if n >= 2:
                            tiles.append((n - 1, K_MPREV))
                        if n >= 1:
                            tiles.append((0, K_MMETA))
                        for ti, (kb, mcol) in enumerate(tiles):
                            bk = 4 + ti
                            MM(bank(bk), kT[g * 64:(g + 1) * 64, kb * 128:(kb + 1) * 128], rhs_q, True, False,
                               [('qT', i) for i in range(4)] + [('kT', kb)], [('ps', bk)])
                            MM(bank(bk), ident, cst[:, mcol:mcol + 512], False, True, ['cst', 'cstm'], [('ps', bk)])
                            ACT(pT[ti], bank(bk), AF.Exp, [('ps', bk)], [('pT', ti)], scale=0.125)
                            for _ in range(NDUM0):
                                MM(bank(0), ident, cst[:, K_MCUR:K_MCUR + 512], True, True, ['cst', 'cstm'], [('ps', 0)])
                        for _ in range(int(os.environ.get('SIDEPAT', '1122')[min(n_iter[0], 3)])):
                            if side:
                                side.pop(0)()
                        n_iter[0] += 1
                        ob = bank(7)[:, 0:260].rearrange('p (i d) -> p i d', i=4)
                        for i in range(4):
                            for ti, (kb, mcol) in enumerate(tiles):
                                MM(ob[:, i, :], pT[ti][:, i * 128:(i + 1) * 128], vext[:, kb, g, 0:65], ti == 0,
                                   ti == len(tiles) - 1, [('pT', ti), ('v', kb), 'vones'], [('ps', 7)])
                        TT('dve', den[:, 0:4], ob[:, :, 64], esink[:, 4 * g:4 * g + 4], ALU.add, [('ps', 7), 'esink'], ['den'])
                        S.op('dve', 'reciprocal', dict(out=den[:, 4:8], in_=den[:, 0:4]), ['den'], ['rden'])
                        TT('dve', a_tok[:, g * 256:(g + 1) * 256].rearrange('p (i d) -> p i d', i=4), ob[:, :, 0:64],
                           den[:, 4:8].unsqueeze(2).to_broadcast([128, 4, 64]), ALU.mult, [('ps', 7), 'rden'], [('a_tok', g)])
                    pb = bankbf(0)
                    for c in range(4):
                        TR(pb[:, c * 128:(c + 1) * 128], a_tok[:, c * 128:(c + 1) * 128], [('a_tok', 0), ('a_tok', 1), 'cst'],
                           [('ps', 0)])
                    TT('dve', mixT[:, 0:4, bi * 128:(bi + 1) * 128], pb[:, 0:512].rearrange('p (c n) -> p c n', c=4),
                       gaT[:, :, bi * 128:(bi + 1) * 128], ALU.mult, [('ps', 0)] + [('gaT', i) for i in range(4)],
                       [('mixT', bi)])
                if STG <= 4:
                    return
                while side:
                    side.pop(0)()
                if STG <= 5:
                    return
                def stat_mm(i):
                    MM(bank(4)[:, 0:nt], ones512, ybf[:, i, 0:nt], i == 0, i == 3, [('ybf', i), 'cst'], [('ps', 4)])
                    MM(bank(5)[:, 0:nt], ones512, ysq[:, i, 0:nt], i == 0, i == 3, [('ysq', i), 'cst'], [('ps', 5)])
                for i in range(4):
                    bk = next_bank()
                    for j0 in range(0, 31, 4):
                        kk = min(4, 31 - j0)
                        hb = dgi[0] % 2
                        dgi[0] += 1
                        dgv = dg_all[:, hb * 512:(hb + 1) * 512].rearrange('p (k m) -> p k m', k=4)
                        wb = vecs[:, V_CW + i * 31 + j0:V_CW + i * 31 + j0 + kk]
                        TT('dve', dgv[:, 0:kk, :], ident.unsqueeze(1).to_broadcast([128, kk, 128]),
                           wb.unsqueeze(2).to_broadcast([128, kk, 128]), ALU.mult, ['cst', 'vecs'], [('dg', hb)])
                        for jj in range(kk):
                            j = j0 + jj
                            MM(bank(bk)[:, 0:nt], dgv[:, jj, :], ubuf[:, i, j:j + nt], j == 0, j == 30,
                               [('dg', hb), ('ubuf', i), ('uhalo', i)], [('ps', bk)])
                    if nxt is not None and i in (1, 2) and i - 1 < nxt[1]:
                        norm_tr(V_GPRE0, xs if i == 1 else xs_b, xnT[:, :, (i - 1) * 128:i * 128],
                                (lambda kc, bi=i - 1: ('xnT', bi, kc)), 0, par=i - 1)
                    if nxt is not None and i in (0, 1) and i < nxt[1]:
                        norm_stats(nxt[0] + i, xs if i == 0 else xs_b, junk, par=i)
                    ACT(acc[:, i, 0:nt], bank(bk)[:, 0:nt], AF.Identity, [('ps', bk), 'vecs'], [('acc', i)],
                        bias=vecs[:, V_CB + i:V_CB + i + 1])
                    ACT(ybf[:, i, 0:nt], acc[:, i, 0:nt], AF.Copy, [('acc', i)], [('ybf', i)])
                    ACT(ysq[:, i, 0:nt], acc[:, i, 0:nt], AF.Square, [('acc', i)], [('ysq', i)])
                    if i >= 1:
                        stat_mm(i - 1)
                stat_mm(3)
                for i in range(4):
                    CP('pool', ubuf[:, i, 0:30], ubuf[:, i, nt:nt + 30], [('ubuf', i)], [('uhalo', i)])
                ACT(mean_sb[:, 0:nt], bank(4)[:, 0:nt], AF.Copy, [('ps', 4)], ['mean'])
                TT('dve', var_sb[:, 0:nt], mean_sb[:, 0:nt], mean_sb[:, 0:nt], ALU.mult, ['mean'], ['var'])
                TT('dve', var_sb[:, 0:nt], bank(5)[:, 0:nt], var_sb[:, 0:nt], ALU.subtract, [('ps', 5), 'var'], ['var'])
                ACT(var_sb[:, 0:nt], var_sb[:, 0:nt], AF.Ln, ['var'], ['var'], bias=1e-5)
                ACT(rstdc[:, 0:nt], var_sb[:, 0:nt], AF.Exp, ['var'], ['var'], scale=-0.5)
                for i in range(4):
                    TT('dve', zt[:, i, 0:nt], acc[:, i, 0:nt], mean_sb[:, 0:nt], ALU.subtract, [('acc', i), 'mean'], [('acc', i)])
                    TT('dve', zt[:, i, 0:nt], zt[:, i, 0:nt], rstdc[:, 0:nt], ALU.mult, [('acc', i), 'var'], [('acc', i)])
                    ACT(cact[:, i, 0:nt], zt[:, i, 0:nt], AF.Silu, [('acc', i), 'vecs'], [('cact', i)],
                        scale=vecs[:, V_LNG + i:V_LNG + i + 1], bias=vecs[:, V_LNB + i:V_LNB + i + 1])
                for oc in range(4):
                    bk = next_bank()
                    for kc in range(4):
                        MM(bank(bk)[:, 0:nt], wpw2[:, kc, oc * 128:(oc + 1) * 128], cact[:, kc, 0:nt], kc == 0, kc == 3,
                           [('cact', kc), ('wpw2', kc)], [('ps', bk)])
                    TT('dve', mixT[:, 4 + oc, 0:nt], bank(bk)[:, 0:nt], gbT[:, oc, 0:nt], ALU.mult, [('ps', bk), ('gbT', oc)],
                       [('mixTc', oc)])
                if STG <= 7:
                    return
                obanks = [(5, 6), (4, 7)]
                for bi in range(nb):
                    outproj_mm((lambda c, bi=bi: mixT[:, c, bi * 128:(bi + 1) * 128]),
                               [('mixT', bi)] + [('mixTc', oc) for oc in range(4)], wout0, 'wout0', obanks[bi])
                for bi in range(nb):
                    outproj_epi(b0 + bi, junk, ptmp, obanks[bi], sc=8 + 8 * bi)

            l0_norm(0, 2)
            for b0 in range(0, NB, 2):
                nb0 = b0 + 2
                l0_chunk(b0, min(2, NB - b0), (nb0, min(2, NB - nb0)) if nb0 < NB else None)
            S.barrier()
            A.reset(base_mark)

        if do_l1:
            xn_flat = A.bf16(8 * T)
            xnT1 = xn_flat.rearrange('p (k n) -> p k n', k=8)
            wout1 = xn_flat[:, 0:8 * D].rearrange('p (k n) -> p k n', k=8)
            mixT1 = A.bf16(8 * SEQ).rearrange('p (k n) -> p k n', k=8)
            qT1s = [A.bf16(T) for _ in range(2)]
            kT1s = [A.bf16(T) for _ in range(2)]
            gT1s = [A.bf16(SEQ) for _ in range(2)]
            v1s = [A.bf16(NB * 128).rearrange('p (b d) -> p b d', b=NB) for _ in range(2)]
            w1b = A.bf16(8 * 512).rearrange('p (k n) -> p k n', k=8)
            tmpg = A.f32(256)
            tm = A.mark()
            e_sb = [[A.f32(512) for _ in range(2)] for _ in range(2)]
            eP = [A.f32(512) for _ in range(2)]
            sp = [[A.bf16(512) for _ in range(2)] for _ in range(2)]
            a_sb = [[A.bf16(512) for _ in range(2)] for _ in range(2)]
            tm_end = A.mark()
            A.reset(tm)
            xs = A.bf16(D)
            xs_b = A.bf16(D)
            junk = A.bf16(D)
            ptmp = A.f32(512)
            A.reset(max(tm_end, A.mark()))
            load_layer_consts(1)
            if os.environ.get('ARENA_DBG'):
                print('L1 arena used', A.mark(), 'of', NW)

            def load_w1(c):
                for kc in range(8):
                    DMA('pool', 'w0_0_%d' % kc, w1b[:, kc, :], w1_d[c, kc * 128:(kc + 1) * 128, :], writes=[('w1', kc)])
            ZB = [[0, 1], [2, 3]]
            PB = [4, 5]
            OBK = 6
            JUNK = 7
            NDUM = int(os.environ.get('NDUM', '1'))
            ntiles_tok = [(tt * 512, min(512, T - tt * 512)) for tt in range(5)]
            xkeys_all = [('xnT1', b, kc) for b in range(NB) for kc in range(8)]

            tiles = []
            for j in range(4):
                kmax = 4 * j + 4
                for kb in range(kmax, -1, -1):
                    tiles.append(dict(j=j, kb=kb, first=(kb == kmax), last=(kb == 0)))
            ntl = len(tiles)

            def proj_groups(c):
                st_ = c % 2
                qd, kd, gd, vd = qT1s[st_], kT1s[st_], gT1s[st_], v1s[st_]
                groups = []

                def g_qkg(which, tt0, ntk):
                    def emit(bk):
                        col = [0, 128, 384][which]
                        blks = range(tt0 // 128, (tt0 + ntk) // 128)
                        for kc in range(8):
                            MM(bank(bk)[:, 0:ntk], w1b[:, kc, col:col + 128], xnT1[:, kc, tt0:tt0 + ntk], kc == 0, kc == 7,
                               [('xnT1', b, kc) for b in blks] + [('w1', kc)], [('ps', bk)])

                        def evac():
                            if which == 0:
                                TS('dve', qd[:, tt0:tt0 + ntk], bank(bk)[:, 0:ntk], 0.125, None, ALU.mult, None, [('ps', bk)],
                                   [('qT1', st_)])
                            elif which == 1:
                                CP('dve', kd[:, tt0:tt0 + ntk], bank(bk)[:, 0:ntk], [('ps', bk)], [('kT1', st_)])
                            else:
                                gsl = gd[:, tt0 - 128:tt0 - 128 + ntk]
                                ACT(tmpg[:, 0:ntk], bank(bk)[:, 0:ntk], AF.Exp, [('ps', bk)], ['tmpg'], scale=-1.0)
                                CP('dve', gsl, bank(bk)[:, 0:ntk], [('ps', bk)], [('gT1', st_)])
                                TS('dve', tmpg[:, 0:ntk], tmpg[:, 0:ntk], 1.0, 1e30, ALU.add, ALU.min, ['tmpg'], ['tmpg'])
                                S.op('dve', 'reciprocal', dict(out=tmpg[:, 0:ntk], in_=tmpg[:, 0:ntk]), ['tmpg'], ['tmpg'])
                                TT('dve', gsl, gsl, tmpg[:, 0:ntk], ALU.mult, [('gT1', st_), 'tmpg'], [('gT1', st_)])
                        return evac
                    return emit

                def g_v(b):
                    def emit(bk):
                        for kc in range(8):
                            MM(bank(bk)[:, 0:128], xnT1[:, kc, b * 128:(b + 1) * 128], w1b[:, kc, 256:384], kc == 0, kc == 7,
                               [('xnT1', b, kc), ('w1', kc)], [('ps', bk)])
                        CP('dve', vd[:, b, :], bank(bk)[:, 0:128], [('ps', bk)], [('v1', st_)])
                    return emit
                for which in (0, 1):
                    for tt0 in range(0, T, 256):
                        ntk = min(256, T - tt0)
                        groups.append(((tt0 + ntk) // 128 - 1, g_qkg(which, tt0, ntk)))
                def g_v2(b):
                    nbk = min(2, NB - b)

                    def emit(bk):
                        for u in range(nbk):
                            for kc in range(8):
                                MM(bank(bk)[:, u * 128:(u + 1) * 128], xnT1[:, kc, (b + u) * 128:(b + u + 1) * 128],
                                   w1b[:, kc, 256:384], kc == 0, kc == 7, [('xnT1', b + u, kc), ('w1', kc)], [('ps', bk)])

                        def evac():
                            CP('dve', vd[:, b:b + nbk, :], bank(bk)[:, 0:nbk * 128].rearrange('p (u d) -> p u d', u=nbk),
                               [('ps', bk)], [('v1', st_)])
                        return evac
                    return emit
                for b in range(0, NB, 2):
                    groups.append((min(b + 1, NB - 1), g_v2(b)))
                for tt0 in range(128, T, 256):
                    groups.append(((tt0 + 256) // 128 - 1, g_qkg(2, tt0, 256)))
                return groups

            def l1_chunk(c):
                st_ = c % 2
                qT1, kT1, gT1, v1 = qT1s[st_], kT1s[st_], gT1s[st_], v1s[st_]
                kq, kk, kg, kv = ('qT1', st_), ('kT1', st_), ('gT1', st_), ('v1', st_)
                if c < 7:
                    load_w1(c + 1)
                    nxt = proj_groups(c + 1)
                else:
                    for kc in range(8):
                        DMA('pool', 'w0_1_%d' % kc, wout1[:, kc, :], wout1_d[kc * 128:(kc + 1) * 128, :], reads=[],
                            writes=xkeys_all + [('wout1', kc)])
                    nxt = []
                nxt_i = [0]
                pend_ev = []

                def fill(i):
                    if i >= 3 and nxt_i[0] < len(nxt) and c0_of(i) <= int(os.environ.get("FILLC0", "128")):
                        pend_ev.append(nxt[nxt_i[0]][1](JUNK))
                        nxt_i[0] += 1
                        return True
                    dummies(NDUM)
                    return False

                def c0_of(i):
                    t = tiles[i]
                    return max(0, t['kb'] - (4 * t['j'] + 1)) * 128

                def s1(s, i):
                    t = tiles[i]
                    j, kb = t['j'], t['kb']
                    q0 = (4 * j + 1) * 128
                    zb = ZB[s][i % 2]
                    d = kb - (4 * j + 1)
                    c0 = c0_of(i)
                    masked = (d >= 0) or (kb == 0)
                    MM(bank(zb)[:, c0:512], kT1[s * 64:(s + 1) * 64, kb * 128:(kb + 1) * 128],
                       qT1[s * 64:(s + 1) * 64, q0 + c0:q0 + 512], True, not masked, [kq, kk], [('ps', zb)])
                    if d >= 0:
                        MM(bank(zb)[:, c0:c0 + 128], ident, cst[:, K_SBM:K_SBM + 128], False, True, ['cst', 'cstm'], [('ps', zb)])
                    elif kb == 0:
                        MM(bank(zb), ident, cst[:, K_SBM + 128:K_SBM + 640], False, True, ['cst', 'cstm'], [('ps', zb)])

                def s2a(s, i):
                    zb = ZB[s][i % 2]
                    c0 = c0_of(i)
                    ACT(e_sb[s][i % 2][:, c0:512], bank(zb)[:, c0:512], AF.Exp, [('ps', zb)], [('e', s, i % 2)])

                def s2b(s, i):
                    c0 = c0_of(i)
                    ACT(sp[s][i % 2][:, c0:512], e_sb[s][i % 2][:, c0:512], AF.Ln, [('e', s, i % 2)], [('sp', s, i % 2)], bias=1.0)

                def s3a(s, i):
                    t = tiles[i]
                    pbk = PB[s]
                    if not t['first']:
                        cp = c0_of(i - 1)
                        MM(bank(pbk)[:, cp:512], comp, sp[s][(i - 1) % 2][:, cp:512], False, False,
                           [('sp', s, (i - 1) % 2), 'cst'], [('ps', pbk)], skip_group_check=True)

                def s3(s, i):
                    t = tiles[i]
                    pbk = PB[s]
                    c0 = c0_of(i)
                    MM(bank(pbk)[:, c0:512], tri, sp[s][i % 2][:, c0:512], t['first'], False, [('sp', s, i % 2), 'cst'],
                       [('ps', pbk)], skip_group_check=True)

                def s4(s, i):
                    pbk = PB[s]
                    c0 = c0_of(i)
                    ACT(eP[s][:, c0:512], bank(pbk)[:, c0:512], AF.Exp, [('ps', pbk)], [('eP', s)], scale=-1.0)

                def s5(s, i):
                    c0 = c0_of(i)
                    TT('dve', a_sb[s][i % 2][:, c0:512], e_sb[s][i % 2][:, c0:512], eP[s][:, c0:512], ALU.mult,
                       [('e', s, i % 2), ('eP', s)], [('a', s, i % 2)])

                def s6(s, i):
                    t = tiles[i]
                    c0 = c0_of(i)
                    MM(bank(OBK)[s * 64:(s + 1) * 64, c0:512], v1[:, t['kb'], s * 64:(s + 1) * 64], a_sb[s][i % 2][:, c0:512],
                       t['first'], t['last'], [('a', s, i % 2), kv], [('ps', OBK)], skip_group_check=True)

                def evac(i):
                    j = tiles[i]['j']
                    q0 = (4 * j + 1) * 128
                    TT('dve', mixT1[:, c, q0 - 128:q0 - 128 + 512], bank(OBK), gT1[:, q0 - 128:q0 - 128 + 512], ALU.mult,
                       [('ps', OBK), kg], [('mixT1', c)])

                def dummies(n):
                    for _ in range(n):
                        MM(bank(JUNK), tri, cst[:, K_SBM + 128:K_SBM + 640], True, True, ['cst', 'cstm'], [('ps', JUNK)])

                s1(0, 0)
                s1(1, 0)
                filled = False
                for i in range(ntl):
                    s3a(0, i)
                    s3a(1, i)
                    if i + 1 < ntl:
                        s1(0, i + 1)
                        s1(1, i + 1)
                    if not filled:
                        dummies(NDUM)
                    s2a(0, i)
                    s2a(1, i)
                    s2b(0, i)
                    s2b(1, i)
                    if os.environ.get('EVLATE', '1') == '1':
                        while pend_ev:
                            pend_ev.pop(0)()
                    s3(0, i)
                    s3(1, i)
                    if i >= 1:
                        s6(0, i - 1)
                        s6(1, i - 1)
                        if tiles[i - 1]['last']:
                            evac(i - 1)
                    filled = fill(i)
                    s4(0, i)
                    s4(1, i)
                    s5(0, i)
                    s5(1, i)
                    while pend_ev and os.environ.get('EVLATE', '1') != '1':
                        pend_ev.pop(0)()
                while pend_ev:
                    pend_ev.pop(0)()
                s6(0, ntl - 1)
                s6(1, ntl - 1)
                evac(ntl - 1)
                while nxt_i[0] < len(nxt):
                    nxt[nxt_i[0]][1](nxt_i[0] % 4)()
                    nxt_i[0] += 1

            load_w1(0)
            pend = proj_groups(0)
            n_g = 8
            late, pend = pend[-n_g:], pend[:-n_g]
            n_emit = [0]
            norm_stats(0, xs, junk, par=0)
            for b in range(NB):
                if b + 1 < NB:
                    norm_stats(b + 1, xs if (b + 1) % 2 == 0 else xs_b, junk, par=(b + 1) % 2)
                norm_tr(V_GPRE1, xs if b % 2 == 0 else xs_b, xnT1[:, :, b * 128:(b + 1) * 128],
                        (lambda kc, b=b: ('xnT1', b, kc)), b % 2, par=b % 2)
                keep = []
                for (need, g) in pend:
                    if need <= b - 1:
                        g([2, 3][n_emit[0] % 2])()
                        n_emit[0] += 1
                    else:
                        keep.append((need, g))
                pend = keep
            for gi, (need, g) in enumerate(pend + late):
                g(gi % 4)()
            pend = []
            assert not pend
            S.barrier()
            for c in range(8):
                l1_chunk(c)
            S.barrier()
            outs = []
            for pi, b in enumerate(range(1, NB, 2)):
                bq = [(0, 1), (2, 3)] if pi % 2 == 0 else [(4, 5), (6, 7)]
                for u in range(2):
                    outproj_mm((lambda c, bb=b + u: mixT1[:, c, (bb - 1) * 128:bb * 128]), [('mixT1', c) for c in range(8)],
                               wout1, 'wout1', bq[u])
                for u in range(2):
                    outproj_epi(b + u, junk, ptmp, bq[u], sc=8 + 8 * ((2 * pi + u) % 4))
                    outs.append(DMA('sp', 'h%d' % (b + u), out_d[(b + u - 1) * 128:(b + u) * 128, :], h[:, b + u, :],
                                    reads=[('h', b + u)]))
            S.wait_all('sp', outs)
        else:
            outs = []
            for b in range(NB):
                outs.append(DMA('sp', 'out%d' % b, hout_d[b * 128:(b + 1) * 128, :], h[:, b, :], reads=[('h', b)]))
            S.wait_all('sp', outs)
        S.run()
    return nc


def _consts():
    c = np.zeros((128, NCD), np.float32)
    K_SBM, K_MCUR, K_MPREV, K_MMETA, K_MCUR0 = D_L1M, D_L0M, D_L0M + 512, D_L0M + 1024, D_L0M + 1536
    r = np.arange(128)
    c[:, K_ID:K_ID + 128] = np.eye(128, dtype=np.float32)
    c[:, K_TRI:K_TRI + 128] = (r[:, None] >= r[None, :]).astype(np.float32)
    c[:, K_COMP:K_COMP + 128] = (r[:, None] < r[None, :]).astype(np.float32)
    c[:, K_ONES:K_ONES + 128] = 1.0 / 512.0
    c[:, K_SBM:K_SBM + 128] = np.where(r[:, None] < r[None, :], 0.0, NEG_SB)
    mp = np.zeros((128, 512), np.float32)
    mp[:PAD, :] = NEG_SB
    c[:, K_SBM + 128:K_SBM + 640] = mp
    cur = np.where(r[:, None] <= r[None, :], 0.0, NEG).astype(np.float32)
    prev = np.where(r[:, None] > r[None, :], 0.0, NEG).astype(np.float32)
    c[:, K_MCUR:K_MCUR + 512] = np.tile(cur, (1, 4))
    c[:, K_MPREV:K_MPREV + 512] = np.tile(prev, (1, 4))
    mm = np.zeros((128, 512), np.float32)
    mm[:PAD, :] = NEG
    c[:, K_MMETA:K_MMETA + 512] = mm
    cur0 = cur.copy()
    cur0[:PAD, :] = NEG
    c[:, K_MCUR0:K_MCUR0 + 512] = np.tile(cur0, (1, 4))
    return c


def _rope_tables():
    p = np.arange(128)
    d = p % 64
    i = d % 32
    inv = (10000.0 ** (-(i.astype(np.float32)) / np.float32(32.0))).astype(np.float32)
    pos = np.maximum(np.arange(T) - PAD, 0).astype(np.float32)
    ang = (pos[None, :] * inv[:, None]).astype(np.float32)
    cos = np.cos(ang).astype(np.float32)
    sin = np.sin(ang).astype(np.float32)
    sgn = np.where(d < 32, -1.0, 1.0).astype(np.float32)[:, None]
    return np.ascontiguousarray(np.concatenate([cos, sin * sgn], axis=1).astype(np.float32))


def _swap_halves(w):
    n = w.shape[1] // 64
    w4 = w.reshape(w.shape[0], n, 2, 32)
    return w4[:, :, ::-1, :].reshape(w.shape[0], n * 64)


def _layout_w0(w_in):
    q, k, v, ga, glu, gb = np.split(w_in, [512, 640, 768, 1280, 2304], axis=1)
    qh = q.reshape(D, 8, 64)
    qperm = np.stack([qh[:, [i, 4 + i], :].reshape(D, 128) for i in range(4)], axis=1).reshape(D, 512)
    qs = _swap_halves(qperm)
    ks = _swap_halves(k)
    return np.ascontiguousarray(np.concatenate([qperm, qs, k, ks, ga, glu, gb, v], axis=1).astype(np.float32))


def _layout_w1(w_in):
    q, k, v, g = np.split(w_in, 4, axis=1)
    out = np.empty((8, D, 512), np.float32)
    for c in range(8):
        sl = slice(c * 128, (c + 1) * 128)
        out[c] = np.concatenate([q[:, sl], k[:, sl], v[:, sl], g[:, sl]], axis=1)
    return out


def _fm(vec, nchunk):
    return np.ascontiguousarray(np.asarray(vec, np.float32).reshape(nchunk, 128).T)


_NC_CACHE = {}


def kernel(x, meta_tokens, ab_pre_norm, ab_w_in, ab_sinks, ab_conv_w, ab_conv_b, ab_conv_ln_g, ab_conv_ln_b, ab_w_pw2,
           ab_w_out, ab_post_norm, sb_pre_norm, sb_w_in, sb_w_out, sb_post_norm):
    f = lambda a: np.ascontiguousarray(np.asarray(a, dtype=np.float32))
    x = f(x)
    vecs = np.zeros((128, NVEC), np.float32)
    vecs[:, V_GPRE0:V_GPRE0 + 8] = _fm(f(ab_pre_norm)[0], 8)
    vecs[:, V_GPRE1:V_GPRE1 + 8] = _fm(f(sb_pre_norm)[0], 8)
    cw = f(ab_conv_w)[0]
    vecs[:, V_CW:V_CW + 124] = cw.T.reshape(4, 128, 31).transpose(1, 0, 2).reshape(128, 124)
    vecs[:, V_CB:V_CB + 4] = _fm(f(ab_conv_b)[0], 4)
    vecs[:, V_LNG:V_LNG + 4] = _fm(f(ab_conv_ln_g)[0], 4)
    vecs[:, V_LNB:V_LNB + 4] = _fm(f(ab_conv_ln_b)[0], 4)
    rows = np.zeros((128, NROW), np.float32)
    rows[:, R_GPOST0:R_GPOST0 + D] = f(ab_post_norm)[0][None, :]
    rows[:, R_GPOST1:R_GPOST1 + D] = f(sb_post_norm)[0][None, :]
    rows[:, R_SINK:R_SINK + 8] = f(ab_sinks)[0][None, :]
    shared = {
        'meta': f(meta_tokens), 'w0': _layout_w0(f(ab_w_in)[0]), 'wpw2': f(ab_w_pw2)[0], 'wout0': f(ab_w_out)[0],
        'w1': _layout_w1(f(sb_w_in)[0]), 'wout1': f(sb_w_out)[0], 'vecs': vecs, 'rows': rows, 'cst': _consts(),
        'rope': _rope_tables(),
    }
    if 'nc' not in _NC_CACHE:
        _NC_CACHE['nc'] = build(True, True)
    nc = _NC_CACHE['nc']
    in_maps = [dict(shared, x=x[b]) for b in range(8)]
    res = run_bass_kernel_spmd(nc, in_maps, core_ids=list(range(8)))
    return np.stack([res.results[b]['out'] for b in range(8)], axis=0).astype(np.float32)
```

```python
import os
import numpy as np
from contextlib import ExitStack
import concourse.bass as bass
import concourse.mybir as mybir
from concourse.bass_utils import run_bass_kernel_spmd

F32 = mybir.dt.float32
BF16 = mybir.dt.bfloat16
AF = mybir.ActivationFunctionType
ALU = mybir.AluOpType

D = 1024
T = 2176
NB = 17
PAD = 112
SEQ = 2048
NEG = -240000.0
NEG_SB = -30000.0
W0C = 27 * 128

ENGS = ['pe', 'act', 'dve', 'pool', 'sp']


class Sched:
    def __init__(self, nc, stack):
        self.nc = nc
        self.stack = stack
        self.q = {e: [] for e in ENGS}
        self.cnt = {e: 0 for e in ENGS}
        self.waited = {e: {} for e in ENGS}
        self.lastw = {}
        self.readers = {}
        self.semh = {}
        self.dval = {}
        for e in ENGS[:4]:
            self.semh['E_' + e] = stack.enter_context(nc.semaphore('E_' + e))

    def _deps(self, reads, writes, eng=None):
        deps = []
        for k in reads:
            t = self.lastw.get(k)
            if t is not None:
                deps.append(t)
            if isinstance(k, tuple) and k[0] == 'ps':
                deps.extend(t2 for t2 in self.readers.get(k, {}).values() if t2[2] != eng)
        for k in writes:
            t = self.lastw.get(k)
            if t is not None:
                deps.append(t)
            deps.extend(self.readers.get(k, {}).values())
        return deps

    def _wait(self, eng, deps):
        w = self.waited[eng]
        for (sk, v, src) in deps:
            if src == 'pe' and eng == 'pe':
                continue
            if w.get(sk, 0) >= v:
                continue
            w[sk] = v
            h = self.semh[sk]
            self.q[eng].append(lambda e, h=h, v=v: e.wait_ge(h, v))

    def _record(self, tok, reads, writes):
        for k in reads:
            self.readers.setdefault(k, {})[tok[0]] = tok
        for k in writes:
            self.lastw[k] = tok
            self.readers[k] = {}

    def op(self, eng, name, kw, reads=(), writes=(), extra=()):
        self._wait(eng, self._deps(reads, writes, eng) + list(extra))
        self.cnt[eng] += 1
        sk = 'E_' + eng
        tok = (sk, self.cnt[eng], eng)
        h = self.semh[sk]
        self.q[eng].append(lambda e, name=name, kw=kw, h=h: getattr(e, name)(**kw).then_inc(h, 1))
        self._record(tok, reads, writes)
        return tok

    def dma(self, eng, name, kw, reads=(), writes=(), extra=()):
        sk = 'D_' + name
        self._wait(eng, self._deps(reads, writes) + list(extra))
        if sk not in self.semh:
            self.semh[sk] = self.stack.enter_context(self.nc.semaphore(sk))
            self.dval[sk] = 0
        if self.dval[sk] > 0:
            self._wait(eng, [(sk, self.dval[sk], 'dma')])
        self.dval[sk] += 16
        tok = (sk, self.dval[sk], 'dma')
        h = self.semh[sk]
        self.q[eng].append(lambda e, kw=kw, h=h: e.dma_start(**kw).then_inc(h, 16))
        self._record(tok, reads, writes)
        return tok

    def barrier(self):
        toks = []
        for e in ENGS[:4]:
            if self.cnt[e] > 0:
                toks.append(('E_' + e, self.cnt[e], e + '_b'))
        for sk, v in self.dval.items():
            toks.append((sk, v, 'dma'))
        for e in ENGS:
            self._wait(e, toks)

    def wait_all(self, eng, toks):
        self._wait(eng, list(toks))

    def run(self):
        nc = self.nc
        with nc.Block() as block:
            @block.tensor
            def _(e):
                for f in self.q['pe']:
                    f(e)

            @block.scalar
            def _(e):
                for f in self.q['act']:
                    f(e)

            @block.vector
            def _(e):
                for f in self.q['dve']:
                    f(e)

            @block.gpsimd
            def _(e):
                for f in self.q['pool']:
                    f(e)

            @block.sync
            def _(e):
                for f in self.q['sp']:
                    f(e)


class Arena:
    def __init__(self, ap, nwords):
        self.ap = ap
        self.n = nwords
        self.off = 0

    def f32(self, n):
        assert self.off + n <= self.n, ('SBUF arena overflow', self.off, n, self.n)
        a = self.ap[:, self.off:self.off + n]
        self.off += n
        return a

    def bf16(self, n):
        w = (n + 1) // 2
        return self.f32(w).bitcast(BF16)

    def mark(self):
        return self.off

    def reset(self, m):
        self.off = m


C_Q, C_QS, C_K, C_KS, C_GA, C_GLA, C_GLB, C_GB, C_V = 0, 4, 8, 9, 10, 14, 18, 22, 26
K_ID, K_TRI, K_COMP, K_ONES = 0, 128, 256, 384
K_MASK = 512
K_MCUR = K_MASK
K_MPREV = K_MCUR + 512
K_MMETA = K_MPREV + 512
K_MCUR0 = K_MMETA + 512
K_SBM = K_MASK
NCB = K_MASK + 5 * 512
D_L0M = 512
D_L1M = 512 + 2048
NCD = 512 + 2048 + 2560
V_GPRE0, V_GPRE1, V_CW, V_CB, V_LNG, V_LNB = 0, 8, 16, 16 + 124, 16 + 128, 16 + 132
NVEC = 16 + 136
R_GPOST0, R_GPOST1, R_SINK = 0, 1024, 2048
NROW = 2048 + 8
RS_GPOST, RS_SINK = 0, 1024
NROWS = 1024 + 8


def build(do_l0=True, do_l1=True):
    nc = bass.Bass('TRN2', target_bir_lowering=False)
    dt = lambda name, shape, kind='ExternalInput': nc.dram_tensor(name, shape, F32, kind=kind).ap()
    x_d = dt('x', [SEQ, D])
    meta_d = dt('meta', [16, D])
    w0_d = dt('w0', [D, W0C])
    wpw2_d = dt('wpw2', [512, 512])
    wout0_d = dt('wout0', [D, D])
    w1_d = dt('w1', [8, D, 512])
    wout1_d = dt('wout1', [D, D])
    vecs_d = dt('vecs', [128, NVEC])
    rows_d = dt('rows', [128, NROW])
    cst_d = dt('cst', [128, NCD])
    rope_d = dt('rope', [128, 2 * T])
    if not do_l0:
        hin_d = dt('hin', [T, D])
    if do_l1:
        out_d = dt('out', [SEQ, D], kind='ExternalOutput')
    else:
        hout_d = dt('hout', [T, D], kind='ExternalOutput')

    with ExitStack() as st:
        NW = 53200
        arena_t = st.enter_context(nc.sbuf_tensor('arena', [128, NW], F32))
        ps = st.enter_context(nc.psum_tensor('ps', [128, 4096], F32))
        S = Sched(nc, st)
        A = Arena(arena_t, NW)

        def bank(i):
            return ps[:, i * 512:(i + 1) * 512]

        def bankbf(i):
            return ps[:, i * 512:(i + 1) * 512].bitcast(BF16)

        def ACT(out, in_, func, reads, writes, **kw):
            return S.op('act', 'activation', dict(out=out, in_=in_, func=func, **kw), reads, writes)

        def TT(eng, out, in0, in1, op, reads, writes):
            return S.op(eng, 'tensor_tensor', dict(out=out, in0=in0, in1=in1, op=op), reads, writes)

        def TS(eng, out, in0, s1, s2, op0, op1, reads, writes):
            kw = dict(out=out, in0=in0, scalar1=s1, scalar2=s2, op0=op0)
            if op1 is not None:
                kw['op1'] = op1
            return S.op(eng, 'tensor_scalar', kw, reads, writes)

        def STT(out, in0, scalar, in1, op0, op1, reads, writes):
            return S.op('dve', 'scalar_tensor_tensor', dict(out=out, in0=in0, scalar=scalar, in1=in1, op0=op0, op1=op1),
                        reads, writes)

        def MM(out, lhsT, rhs, start, stop, reads, writes, **kw):
            return S.op('pe', 'matmul', dict(out=out, lhsT=lhsT, rhs=rhs, start=start, stop=stop, **kw), reads, writes)

        def TR(out, in_, reads, writes):
            return S.op('pe', 'transpose', dict(out=out, in_=in_, identity=ident), reads, writes)

        def CP(eng, out, in_, reads, writes):
            return S.op(eng, 'tensor_copy', dict(out=out, in_=in_), reads, writes)

        def DMA(eng, name, out, in_, reads=(), writes=()):
            return S.dma(eng, name, dict(out=out, in_=in_), reads, writes)

        h = A.f32(NB * D).rearrange('p (b d) -> p b d', b=NB)
        cst = A.bf16(NCB)
        vecs = A.f32(NVEC)
        rows = A.f32(NROWS)
        stat = A.f32(64)
        esink = A.f32(8)
        ident = cst[:, K_ID:K_ID + 128]
        tri = cst[:, K_TRI:K_TRI + 128]
        comp = cst[:, K_COMP:K_COMP + 128]
        ones512 = cst[:, K_ONES:K_ONES + 128]
        base_mark = A.mark()

        DMA('pool', 'cst', cst[:, 0:512], cst_d[:, 0:512], writes=['cst'])
        DMA('sp', 'vecs', vecs, vecs_d, writes=['vecs'])
        DMA('sp', 'sinks', rows[:, RS_SINK:RS_SINK + 8], rows_d[:, R_SINK:R_SINK + 8], writes=['sinks'])

        def load_layer_consts(layer):
            if layer == 0:
                DMA('pool', 'cstm', cst[:, K_MASK:K_MASK + 2048], cst_d[:, D_L0M:D_L0M + 2048], writes=['cstm'])
                DMA('sp', 'rows', rows[:, 0:1024], rows_d[:, R_GPOST0:R_GPOST0 + 1024], writes=['rows'])
            else:
                DMA('pool', 'cstm', cst[:, K_MASK:K_MASK + 640], cst_d[:, D_L1M:D_L1M + 640], writes=['cstm'])
                DMA('sp', 'rows', rows[:, 0:1024], rows_d[:, R_GPOST1:R_GPOST1 + 1024], writes=['rows'])
        if do_l0:
            S.op('dve', 'memset', dict(ap=h[:, 0, :], constant=0.0), [], [('h', 0)])
            DMA('sp', 'h0', h[PAD:128, 0, :], meta_d, writes=[('h', 0)])
            for b in range(1, NB):
                DMA('sp', 'h%d' % b, h[:, b, :], x_d[(b - 1) * 128:b * 128, :], writes=[('h', b)])
        else:
            for b in range(NB):
                DMA('sp', 'h%d' % b, h[:, b, :], hin_d[b * 128:(b + 1) * 128, :], writes=[('h', b)])

        def norm_stats(b, xs, junk, eps=1e-6, par=0):
            hb = h[:, b, :]
            sc = 40 + 3 * par
            k0, k1, k2, kx = ('nst', par, 0), ('nst', par, 1), ('nst', par, 2), ('xs', par)
            ACT(junk, hb, AF.Square, [('h', b)], ['junk', k0], accum_out=stat[:, sc:sc + 1])
            ACT(stat[:, sc + 1:sc + 2], stat[:, sc:sc + 1], AF.Ln, [k0], [k1], scale=1.0 / D, bias=eps)
            ACT(stat[:, sc + 2:sc + 3], stat[:, sc + 1:sc + 2], AF.Exp, [k1], [k2], scale=-0.5)
            TS('dve', xs, hb, stat[:, sc + 2:sc + 3], None, ALU.mult, None, [('h', b), k2], [kx])

        def norm_tr(gcol, xs, dst_all, key_fn, pbank, par=0):
            kx = ('xs', par)
            pb = bankbf(pbank)
            for kc in range(8):
                TR(pb[:, kc * 128:(kc + 1) * 128], xs[:, kc * 128:(kc + 1) * 128], [kx, 'cst'], [('ps', pbank)])
            TT('dve', dst_all, pb.rearrange('p (k n) -> p k n', k=8), vecs[:, gcol:gcol + 8].unsqueeze(2).to_broadcast([128, 8, 128]),
               ALU.mult, [('ps', pbank), 'vecs'], [key_fn(kc) for kc in range(8)])

        def norm_transpose(b, gcol, xs, junk, dst_all, key_fn, pbank, eps=1e-6, par=0):
            hb = h[:, b, :]
            sc = 40 + 3 * par
            k0, k1, k2, kx = ('nst', par, 0), ('nst', par, 1), ('nst', par, 2), ('xs', par)
            ACT(junk, hb, AF.Square, [('h', b)], ['junk', k0], accum_out=stat[:, sc:sc + 1])
            ACT(stat[:, sc + 1:sc + 2], stat[:, sc:sc + 1], AF.Ln, [k0], [k1], scale=1.0 / D, bias=eps)
            ACT(stat[:, sc + 2:sc + 3], stat[:, sc + 1:sc + 2], AF.Exp, [k1], [k2], scale=-0.5)
            TS('dve', xs, hb, stat[:, sc + 2:sc + 3], None, ALU.mult, None, [('h', b), k2], [kx])
            pb = bankbf(pbank)
            for kc in range(8):
                TR(pb[:, kc * 128:(kc + 1) * 128], xs[:, kc * 128:(kc + 1) * 128], [kx, 'cst'], [('ps', pbank)])
            TT('dve', dst_all, pb.rearrange('p (k n) -> p k n', k=8), vecs[:, gcol:gcol + 8].unsqueeze(2).to_broadcast([128, 8, 128]),
               ALU.mult, [('ps', pbank), 'vecs'], [key_fn(kc) for kc in range(8)])

        def outproj_mm(lhs_fn, mix_keys, wout, wkey, banks):
            for half in range(2):
                bk = banks[half]
                for c in range(8):
                    MM(bank(bk), lhs_fn(c), wout[:, c, half * 512:(half + 1) * 512], c == 0, c == 7,
                       list(mix_keys) + [(wkey, c)], [('ps', bk)])

        def outproj_epi(b, junk, ptmp, banks, sc=8, eps=1e-6):
            for half in range(2):
                bk = banks[half]
                ACT(junk[:, 0:512], bank(bk), AF.Square, [('ps', bk)], ['junk', ('st', sc + half)],
                    accum_out=stat[:, sc + half:sc + half + 1])
            TT('dve', stat[:, sc + 2:sc + 3], stat[:, sc:sc + 1], stat[:, sc + 1:sc + 2], ALU.add, [('st', sc), ('st', sc + 1)],
               [('st', sc + 2)])
            ACT(stat[:, sc + 3:sc + 4], stat[:, sc + 2:sc + 3], AF.Ln, [('st', sc + 2)], [('st', sc + 3)], scale=1.0 / D, bias=eps)
            ACT(stat[:, sc + 4:sc + 5], stat[:, sc + 3:sc + 4], AF.Exp, [('st', sc + 3)], [('st', sc + 4)], scale=-0.5)
            for half in range(2):
                bk = banks[half]
                STT(ptmp[:, 0:512], bank(bk), stat[:, sc + 4:sc + 5], rows[:, half * 512:(half + 1) * 512], ALU.mult, ALU.mult,
                    [('ps', bk), ('st', sc + 4), 'rows'], ['ptmp'])
                TT(os.environ.get('EPIENG', 'dve'), h[:, b, half * 512:(half + 1) * 512], h[:, b, half * 512:(half + 1) * 512], ptmp[:, 0:512], ALU.add,
                   [('h', b), 'ptmp'], [('h', b)])

        def outproj_block(b, lhs_fn, mix_keys, wout, wkey, gpost_off, junk, ptmp, banks, eps=1e-6):
            outproj_mm(lhs_fn, mix_keys, wout, wkey, banks)
            outproj_epi(b, junk, ptmp, banks)

        if do_l0:
            w0 = A.bf16(8 * W0C).rearrange('p (k n) -> p k n', k=8)
            wpw2 = A.bf16(4 * 512).rearrange('p (k n) -> p k n', k=4)
            wout0 = A.bf16(8 * D).rearrange('p (k n) -> p k n', k=8)
            NT = 256
            xnT = A.bf16(8 * NT).rearrange('p (k n) -> p k n', k=8)
            qT = A.bf16(4 * NT).rearrange('p (k n) -> p k n', k=4)
            kT = A.bf16(T)
            vext = A.bf16(NB * 2 * 66).rearrange('p (b g d) -> p b g d', b=NB, g=2)
            gaT = A.bf16(4 * NT).rearrange('p (k n) -> p k n', k=4)
            gbT = A.bf16(4 * NT).rearrange('p (k n) -> p k n', k=4)
            ubuf = A.bf16(4 * (30 + NT)).rearrange('p (k n) -> p k n', k=4)
            dg_all = A.bf16(8 * 128)
            dgi = [0]
            acc = A.f32(4 * NT).rearrange('p (k n) -> p k n', k=4)
            ybf = A.bf16(4 * NT).rearrange('p (k n) -> p k n', k=4)
            ysq = A.bf16(4 * NT).rearrange('p (k n) -> p k n', k=4)
            mean_sb = A.f32(NT)
            var_sb = A.f32(NT)
            rstdc = var_sb
            zt = acc
            cact = A.bf16(4 * NT).rearrange('p (k n) -> p k n', k=4)
            mixT = A.bf16(8 * NT).rearrange('p (k n) -> p k n', k=8)
            pT = [A.bf16(512) for _ in range(3)]
            a_tok = A.bf16(512)
            ropeA = A.f32(NT)
            ropeB = A.f32(NT)
            ptmp = A.f32(512)
            sig = ropeA
            cs = A.f32(2 * NT).rearrange('p (k n) -> p k n', k=2)
            xs = A.bf16(D)
            xs_b = A.bf16(D)
            junk = A.bf16(D)
            den = A.f32(8)
            load_layer_consts(0)
            if os.environ.get('ARENA_DBG'):
                print('L0 arena used', A.mark(), 'of', NW)

            W0G = [(0, 10 * 128), (10 * 128, 14 * 128), (26 * 128, 27 * 128), (14 * 128, 26 * 128)]

            def w0grp(oc):
                c = oc * 128
                for gi, (lo, hi) in enumerate(W0G):
                    if lo <= c < hi:
                        return gi
            for gi, (lo, hi) in enumerate(W0G):
                for kc in range(8):
                    DMA('pool', 'w0_%d_%d' % (gi, kc), w0[:, kc, lo:hi], w0_d[kc * 128:(kc + 1) * 128, lo:hi],
                        writes=[('w0', gi, kc)])

            for kc in range(4):
                DMA('pool', 'wpw2_%d' % kc, wpw2[:, kc, :], wpw2_d[kc * 128:(kc + 1) * 128, :], writes=[('wpw2', kc)])
            for kc in range(8):
                DMA('pool', 'wout0_%d' % kc, wout0[:, kc, :], wout0_d[kc * 128:(kc + 1) * 128, :], writes=[('wout0', kc)])

            def w0keys_of(oc):
                return [('w0', w0grp(oc), kc) for kc in range(8)]
            S.op('pool', 'memset', dict(ap=ubuf[:, :, 0:30], constant=0.0), [], ['ubuf'])
            S.op('pool', 'memset', dict(ap=vext[:, :, :, 64:66], constant=1.0), [], ['vones'])
            ACT(esink, rows[:, RS_SINK:RS_SINK + 8], AF.Exp, ['sinks'], ['esink'])
            rope3 = rope_d.rearrange('p (k n) -> p k n', k=2)

            pbi = [0]
            NDUM0 = int(os.environ.get('NDUM0', '0'))

            def next_bank():
                b_ = [1, 2, 3][pbi[0] % 3]
                pbi[0] += 1
                return b_

            STG = int(os.environ.get('L0_STAGE', '99'))

            def l0_norm(b0, nb):
                for bi in range(nb):
                    norm_transpose(b0 + bi, V_GPRE0, xs, junk, xnT[:, :, bi * 128:(bi + 1) * 128],
                                   (lambda kc, bi=bi: ('xnT', bi, kc)), 0)

            def l0_chunk(b0, nb, nxt=None):
                t0 = b0 * 128
                nt = nb * 128
                xkeys = [('xnT', bi, kc) for bi in range(nb) for kc in range(8)]

                def proj(bk, coff, oc):
                    for kc in range(8):
                        MM(bank(bk)[:, coff:coff + nt], w0[:, kc, oc * 128:(oc + 1) * 128], xnT[:, kc, 0:nt], kc == 0, kc == 7,
                           xkeys + w0keys_of(oc), [('ps', bk)])

                if STG <= 0:
                    return
                if os.environ.get('NOCS') is None:
                    DMA('sp', 'cs', cs[:, :, 0:nt], rope3[:, :, t0:t0 + nt], writes=['cs'])
                if STG <= 1:
                    return
                for i in range(5):
                    bk = next_bank()
                    oc_a, oc_b = (C_Q + i, C_QS + i) if i < 4 else (C_K, C_KS)
                    proj(bk, 0, oc_a)
                    proj(bk, 256, oc_b)
                    TT('dve', ropeA[:, 0:nt], bank(bk)[:, 0:nt], cs[:, 0, 0:nt], ALU.mult, [('ps', bk), 'cs'], ['ropeA'])
                    TT('dve', ropeB[:, 0:nt], bank(bk)[:, 256:256 + nt], cs[:, 1, 0:nt], ALU.mult, [('ps', bk), 'cs'], ['ropeB'])
                    if i < 4:
                        TT(os.environ.get('ROPEENG', 'pool'), qT[:, i, 0:nt], ropeA[:, 0:nt], ropeB[:, 0:nt], ALU.add, ['ropeA', 'ropeB'], [('qT', i)])
                    else:
                        TT(os.environ.get('ROPEENG', 'pool'), kT[:, t0:t0 + nt], ropeA[:, 0:nt], ropeB[:, 0:nt], ALU.add, ['ropeA', 'ropeB'],
                           [('kT', b0 + bi) for bi in range(nb)])
                if STG <= 2:
                    return
                for i in range(0, 4, 2):
                    bk = next_bank()
                    proj(bk, 0, C_GA + i)
                    proj(bk, 256, C_GA + i + 1)
                    for u in range(2):
                        ACT(gaT[:, i + u, 0:nt], bank(bk)[:, u * 256:u * 256 + nt], AF.Silu, [('ps', bk)], [('gaT', i + u)])
                for bi in range(nb):
                    bk = next_bank()
                    for kc in range(8):
                        MM(bank(bk)[:, 0:128], xnT[:, kc, bi * 128:(bi + 1) * 128], w0[:, kc, C_V * 128:(C_V + 1) * 128],
                           kc == 0, kc == 7, xkeys + w0keys_of(C_V), [('ps', bk)])
                    ACT(vext[:, b0 + bi, :, 0:64], bank(bk)[:, 0:128].rearrange('p (g d) -> p g d', g=2), AF.Copy,
                        [('ps', bk)], [('v', b0 + bi)])
                if STG <= 3:
                    return
                side = []

                def job_glu(i):
                    def run():
                        bk = next_bank()
                        proj(bk, 0, C_GLA + i)
                        proj(bk, 256, C_GLB + i)
                        ACT(sig[:, 0:nt], bank(bk)[:, 256:256 + nt], AF.Sigmoid, [('ps', bk)], ['ropeA'])
                        TT('dve', ubuf[:, i, 30:30 + nt], bank(bk)[:, 0:nt], sig[:, 0:nt], ALU.mult, [('ps', bk), 'ropeA'],
                           [('ubuf', i)])
                    return run

                def job_gb(i):
                    def run():
                        bk = next_bank()
                        proj(bk, 0, C_GB + i)
                        proj(bk, 256, C_GB + i + 1)
                        for u in range(2):
                            ACT(gbT[:, i + u, 0:nt], bank(bk)[:, u * 256:u * 256 + nt], AF.Silu, [('ps', bk)], [('gbT', i + u)])
                    return run
                for i in range(4):
                    side.append(job_glu(i))
                for i in range(0, 4, 2):
                    side.append(job_gb(i))
                n_iter = [0]
                for bi in range(nb):
                    n = b0 + bi
                    for g in range(2):
                        rhs_q = qT[g * 64:(g + 1) * 64, :, bi * 128:(bi + 1) * 128]
                        tiles = [(n, K_MCUR0 if n == 0 else K_MCUR)]
                        if n >= 2:
                            tiles.append((n - 1, K_MPREV))
                        if n >= 1:
                            tiles.append((0, K_MMETA))
                        for ti, (kb, mcol) in enumerate(tiles):
                            bk = 4 + ti
                            MM(bank(bk), kT[g * 64:(g + 1) * 64, kb * 128:(kb + 1) * 128], rhs_q, True, False,
                               [('qT', i) for i in range(4)] + [('kT', kb)], [('ps', bk)])
                            MM(bank(bk), ident, cst[:, mcol:mcol + 512], False, True, ['cst', 'cstm'], [('ps', bk)])
                            ACT(pT[ti], bank(bk), AF.Exp, [('ps', bk)], [('pT', ti)], scale=0.125)
                            for _ in range(NDUM0):
                                MM(bank(0), ident, cst[:, K_MCUR:K_MCUR + 512], True, True, ['cst', 'cstm'], [('ps', 0)])
                        for _ in range(int(os.environ.get('SIDEPAT', '1122')[min(n_iter[0], 3)])):
                            if side:
                                side.pop(0)()
                        n_iter[0] += 1
                        ob = bank(7)[:, 0:260].rearrange('p (i d) -> p i d', i=4)
                        for i in range(4):
                            for ti, (kb, mcol) in enumerate(tiles):
                                MM(ob[:, i, :], pT[ti][:, i * 128:(i + 1) * 128], vext[:, kb, g, 0:65], ti == 0,
                                   ti == len(tiles) - 1, [('pT', ti), ('v', kb), 'vones'], [('ps', 7)])
                        TT('dve', den[:, 0:4], ob[:, :, 64], esink[:, 4 * g:4 * g + 4], ALU.add, [('ps', 7), 'esink'], ['den'])
                        S.op('dve', 'reciprocal', dict(out=den[:, 4:8], in_=den[:, 0:4]), ['den'], ['rden'])
                        TT('dve', a_tok[:, g * 256:(g + 1) * 256].rearrange('p (i d) -> p i d', i=4), ob[:, :, 0:64],
                           den[:, 4:8].unsqueeze(2).to_broadcast([128, 4, 64]), ALU.mult, [('ps', 7), 'rden'], [('a_tok', g)])
                    pb = bankbf(0)
                    for c in range(4):
                        TR(pb[:, c * 128:(c + 1) * 128], a_tok[:, c * 128:(c + 1) * 128], [('a_tok', 0), ('a_tok', 1), 'cst'],
                           [('ps', 0)])
                    TT('dve', mixT[:, 0:4, bi * 128:(bi + 1) * 128], pb[:, 0:512].rearrange('p (c n) -> p c n', c=4),
                       gaT[:, :, bi * 128:(bi + 1) * 128], ALU.mult, [('ps', 0)] + [('gaT', i) for i in range(4)],
                       [('mixT', bi)])
                if STG <= 4:
                    return
                while side:
                    side.pop(0)()
                if STG <= 5:
                    return
                def stat_mm(i):
                    MM(bank(4)[:, 0:nt], ones512, ybf[:, i, 0:nt], i == 0, i == 3, [('ybf', i), 'cst'], [('ps', 4)])
                    MM(bank(5)[:, 0:nt], ones512, ysq[:, i, 0:nt], i == 0, i == 3, [('ysq', i), 'cst'], [('ps', 5)])
                for i in range(4):
                    bk = next_bank()
                    for j0 in range(0, 31, 4):
                        kk = min(4, 31 - j0)
                        hb = dgi[0] % 2
                        dgi[0] += 1
                        dgv = dg_all[:, hb * 512:(hb + 1) * 512].rearrange('p (k m) -> p k m', k=4)
                        wb = vecs[:, V_CW + i * 31 + j0:V_CW + i * 31 + j0 + kk]
                        TT('dve', dgv[:, 0:kk, :], ident.unsqueeze(1).to_broadcast([128, kk, 128]),
                           wb.unsqueeze(2).to_broadcast([128, kk, 128]), ALU.mult, ['cst', 'vecs'], [('dg', hb)])
                        for jj in range(kk):
                            j = j0 + jj
                            MM(bank(bk)[:, 0:nt], dgv[:, jj, :], ubuf[:, i, j:j + nt], j == 0, j == 30,
                               [('dg', hb), ('ubuf', i), ('uhalo', i)], [('ps', bk)])
                    if nxt is not None and i in (1, 2) and i - 1 < nxt[1]:
                        norm_tr(V_GPRE0, xs if i == 1 else xs_b, xnT[:, :, (i - 1) * 128:i * 128],
                                (lambda kc, bi=i - 1: ('xnT', bi, kc)), 0, par=i - 1)
                    if nxt is not None and i in (0, 1) and i < nxt[1]:
                        norm_stats(nxt[0] + i, xs if i == 0 else xs_b, junk, par=i)
                    ACT(acc[:, i, 0:nt], bank(bk)[:, 0:nt], AF.Identity, [('ps', bk), 'vecs'], [('acc', i)],
                        bias=vecs[:, V_CB + i:V_CB + i + 1])
                    ACT(ybf[:, i, 0:nt], acc[:, i, 0:nt], AF.Copy, [('acc', i)], [('ybf', i)])
                    ACT(ysq[:, i, 0:nt], acc[:, i, 0:nt], AF.Square, [('acc', i)], [('ysq', i)])
                    if i >= 1:
                        stat_mm(i - 1)
                stat_mm(3)
                for i in range(4):
                    CP('pool', ubuf[:, i, 0:30], ubuf[:, i, nt:nt + 30], [('ubuf', i)], [('uhalo', i)])
                ACT(mean_sb[:, 0:nt], bank(4)[:, 0:nt], AF.Copy, [('ps', 4)], ['mean'])
                TT('dve', var_sb[:, 0:nt], mean_sb[:, 0:nt], mean_sb[:, 0:nt], ALU.mult, ['mean'], ['var'])
                TT('dve', var_sb[:, 0:nt], bank(5)[:, 0:nt], var_sb[:, 0:nt], ALU.subtract, [('ps', 5), 'var'], ['var'])
                ACT(var_sb[:, 0:nt], var_sb[:, 0:nt], AF.Ln, ['var'], ['var'], bias=1e-5)
                ACT(rstdc[:, 0:nt], var_sb[:, 0:nt], AF.Exp, ['var'], ['var'], scale=-0.5)
                for i in range(4):
                    TT('dve', zt[:, i, 0:nt], acc[:, i, 0:nt], mean_sb[:, 0:nt], ALU.subtract, [('acc', i), 'mean'], [('acc', i)])
                    TT('dve', zt[:, i, 0:nt], zt[:, i, 0:nt], rstdc[:, 0:nt], ALU.mult, [('acc', i), 'var'], [('acc', i)])
                    ACT(cact[:, i, 0:nt], zt[:, i, 0:nt], AF.Silu, [('acc', i), 'vecs'], [('cact', i)],
                        scale=vecs[:, V_LNG + i:V_LNG + i + 1], bias=vecs[:, V_LNB + i:V_LNB + i + 1])
                for oc in range(4):
                    bk = next_bank()
                    for kc in range(4):
                        MM(bank(bk)[:, 0:nt], wpw2[:, kc, oc * 128:(oc + 1) * 128], cact[:, kc, 0:nt], kc == 0, kc == 3,
                           [('cact', kc), ('wpw2', kc)], [('ps', bk)])
                    TT('dve', mixT[:, 4 + oc, 0:nt], bank(bk)[:, 0:nt], gbT[:, oc, 0:nt], ALU.mult, [('ps', bk), ('gbT', oc)],
                       [('mixTc', oc)])
                if STG <= 7:
                    return
                obanks = [(5, 6), (4, 7)]
                for bi in range(nb):
                    outproj_mm((lambda c, bi=bi: mixT[:, c, bi * 128:(bi + 1) * 128]),
                               [('mixT', bi)] + [('mixTc', oc) for oc in range(4)], wout0, 'wout0', obanks[bi])
                for bi in range(nb):
                    outproj_epi(b0 + bi, junk, ptmp, obanks[bi], sc=8 + 8 * bi)

            l0_norm(0, 2)
            for b0 in range(0, NB, 2):
                nb0 = b0 + 2
                l0_chunk(b0, min(2, NB - b0), (nb0, min(2, NB - nb0)) if nb0 < NB else None)
            S.barrier()
            A.reset(base_mark)

        if do_l1:
            xn_flat = A.bf16(8 * T)
            xnT1 = xn_flat.rearrange('p (k n) -> p k n', k=8)
            wout1 = xn_flat[:, 0:8 * D].rearrange('p (k n) -> p k n', k=8)
            mixT1 = A.bf16(8 * SEQ).rearrange('p (k n) -> p k n', k=8)
            qT1s = [A.bf16(T) for _ in range(2)]
            kT1s = [A.bf16(T) for _ in range(2)]
            gT1s = [A.bf16(SEQ) for _ in range(2)]
            v1s = [A.bf16(NB * 128).rearrange('p (b d) -> p b d', b=NB) for _ in range(2)]
            w1b = A.bf16(8 * 512).rearrange('p (k n) -> p k n', k=8)
            tmpg = A.f32(256)
            tm = A.mark()
            e_sb = [[A.f32(512) for _ in range(2)] for _ in range(2)]
            eP = [A.f32(512) for _ in range(2)]
            sp = [[A.bf16(512) for _ in range(2)] for _ in range(2)]
            a_sb = [[A.bf16(512) for _ in range(2)] for _ in range(2)]
            tm_end = A.mark()
            A.reset(tm)
            xs = A.bf16(D)
            xs_b = A.bf16(D)
            junk = A.bf16(D)
            ptmp = A.f32(512)
            A.reset(max(tm_end, A.mark()))
            load_layer_consts(1)
            if os.environ.get('ARENA_DBG'):
                print('L1 arena used', A.mark(), 'of', NW)

            def load_w1(c):
                for kc in range(8):
                    DMA('pool', 'w0_0_%d' % kc, w1b[:, kc, :], w1_d[c, kc * 128:(kc + 1) * 128, :], writes=[('w1', kc)])
            ZB = [[0, 1], [2, 3]]
            PB = [4, 5]
            OBK = 6
            JUNK = 7
            NDUM = int(os.environ.get('NDUM', '1'))
            ntiles_tok = [(tt * 512, min(512, T - tt * 512)) for tt in range(5)]
            xkeys_all = [('xnT1', b, kc) for b in range(NB) for kc in range(8)]

            tiles = []
            for j in range(4):
                kmax = 4 * j + 4
                for kb in range(kmax, -1, -1):
                    tiles.append(dict(j=j, kb=kb, first=(kb == kmax), last=(kb == 0)))
            ntl = len(tiles)

            def proj_groups(c):
                st_ = c % 2
                qd, kd, gd, vd = qT1s[st_], kT1s[st_], gT1s[st_], v1s[st_]
                groups = []

                def g_qkg(which, tt0, ntk):
                    def emit(bk):
                        col = [0, 128, 384][which]
                        blks = range(tt0 // 128, (tt0 + ntk) // 128)
                        for kc in range(8):
                            MM(bank(bk)[:, 0:ntk], w1b[:, kc, col:col + 128], xnT1[:, kc, tt0:tt0 + ntk], kc == 0, kc == 7,
                               [('xnT1', b, kc) for b in blks] + [('w1', kc)], [('ps', bk)])

                        def evac():
                            if which == 0:
                                TS('dve', qd[:, tt0:tt0 + ntk], bank(bk)[:, 0:ntk], 0.125, None, ALU.mult, None, [('ps', bk)],
                                   [('qT1', st_)])
                            elif which == 1:
                                CP('dve', kd[:, tt0:tt0 + ntk], bank(bk)[:, 0:ntk], [('ps', bk)], [('kT1', st_)])
                            else:
                                gsl = gd[:, tt0 - 128:tt0 - 128 + ntk]
                                ACT(tmpg[:, 0:ntk], bank(bk)[:, 0:ntk], AF.Exp, [('ps', bk)], ['tmpg'], scale=-1.0)
                                CP('dve', gsl, bank(bk)[:, 0:ntk], [('ps', bk)], [('gT1', st_)])
                                TS('dve', tmpg[:, 0:ntk], tmpg[:, 0:ntk], 1.0, 1e30, ALU.add, ALU.min, ['tmpg'], ['tmpg'])
                                S.op('dve', 'reciprocal', dict(out=tmpg[:, 0:ntk], in_=tmpg[:, 0:ntk]), ['tmpg'], ['tmpg'])
                                TT('dve', gsl, gsl, tmpg[:, 0:ntk], ALU.mult, [('gT1', st_), 'tmpg'], [('gT1', st_)])
                        return evac
                    return emit

                def g_v(b):
                    def emit(bk):
                        for kc in range(8):
                            MM(bank(bk)[:, 0:128], xnT1[:, kc, b * 128:(b + 1) * 128], w1b[:, kc, 256:384], kc == 0, kc == 7,
                               [('xnT1', b, kc), ('w1', kc)], [('ps', bk)])
                        CP('dve', vd[:, b, :], bank(bk)[:, 0:128], [('ps', bk)], [('v1', st_)])
                    return emit
                for which in (0, 1):
                    for tt0 in range(0, T, 256):
                        ntk = min(256, T - tt0)
                        groups.append(((tt0 + ntk) // 128 - 1, g_qkg(which, tt0, ntk)))
                def g_v2(b):
                    nbk = min(2, NB - b)

                    def emit(bk):
                        for u in range(nbk):
                            for kc in range(8):
                                MM(bank(bk)[:, u * 128:(u + 1) * 128], xnT1[:, kc, (b + u) * 128:(b + u + 1) * 128],
                                   w1b[:, kc, 256:384], kc == 0, kc == 7, [('xnT1', b + u, kc), ('w1', kc)], [('ps', bk)])

                        def evac():
                            CP('dve', vd[:, b:b + nbk, :], bank(bk)[:, 0:nbk * 128].rearrange('p (u d) -> p u d', u=nbk),
                               [('ps', bk)], [('v1', st_)])
                        return evac
                    return emit
                for b in range(0, NB, 2):
                    groups.append((min(b + 1, NB - 1), g_v2(b)))
                for tt0 in range(128, T, 256):
                    groups.append(((tt0 + 256) // 128 - 1, g_qkg(2, tt0, 256)))
                return groups

            def l1_chunk(c):
                st_ = c % 2
                qT1, kT1, gT1, v1 = qT1s[st_], kT1s[st_], gT1s[st_], v1s[st_]
                kq, kk, kg, kv = ('qT1', st_), ('kT1', st_), ('gT1', st_), ('v1', st_)
                if c < 7:
                    load_w1(c + 1)
                    nxt = proj_groups(c + 1)
                else:
                    for kc in range(8):
                        DMA('pool', 'w0_1_%d' % kc, wout1[:, kc, :], wout1_d[kc * 128:(kc + 1) * 128, :], reads=[],
                            writes=xkeys_all + [('wout1', kc)])
                    nxt = []
                nxt_i = [0]
                pend_ev = []

                def fill(i):
                    if i >= 3 and nxt_i[0] < len(nxt) and c0_of(i) <= int(os.environ.get("FILLC0", "128")):
                        pend_ev.append(nxt[nxt_i[0]][1](JUNK))
                        nxt_i[0] += 1
                        return True
                    dummies(NDUM)
                    return False

                def c0_of(i):
                    t = tiles[i]
                    return max(0, t['kb'] - (4 * t['j'] + 1)) * 128

                def s1(s, i):
                    t = tiles[i]
                    j, kb = t['j'], t['kb']
                    q0 = (4 * j + 1) * 128
                    zb = ZB[s][i % 2]
                    d = kb - (4 * j + 1)
                    c0 = c0_of(i)
                    masked = (d >= 0) or (kb == 0)
                    MM(bank(zb)[:, c0:512], kT1[s * 64:(s + 1) * 64, kb * 128:(kb + 1) * 128],
                       qT1[s * 64:(s + 1) * 64, q0 + c0:q0 + 512], True, not masked, [kq, kk], [('ps', zb)])
                    if d >= 0:
                        MM(bank(zb)[:, c0:c0 + 128], ident, cst[:, K_SBM:K_SBM + 128], False, True, ['cst', 'cstm'], [('ps', zb)])
                    elif kb == 0:
                        MM(bank(zb), ident, cst[:, K_SBM + 128:K_SBM + 640], False, True, ['cst', 'cstm'], [('ps', zb)])

                def s2a(s, i):
                    zb = ZB[s][i % 2]
                    c0 = c0_of(i)
                    ACT(e_sb[s][i % 2][:, c0:512], bank(zb)[:, c0:512], AF.Exp, [('ps', zb)], [('e', s, i % 2)])

                def s2b(s, i):
                    c0 = c0_of(i)
                    ACT(sp[s][i % 2][:, c0:512], e_sb[s][i % 2][:, c0:512], AF.Ln, [('e', s, i % 2)], [('sp', s, i % 2)], bias=1.0)

                def s3a(s, i):
                    t = tiles[i]
                    pbk = PB[s]
                    if not t['first']:
                        cp = c0_of(i - 1)
                        MM(bank(pbk)[:, cp:512], comp, sp[s][(i - 1) % 2][:, cp:512], False, False,
                           [('sp', s, (i - 1) % 2), 'cst'], [('ps', pbk)], skip_group_check=True)

                def s3(s, i):
                    t = tiles[i]
                    pbk = PB[s]
                    c0 = c0_of(i)
                    MM(bank(pbk)[:, c0:512], tri, sp[s][i % 2][:, c0:512], t['first'], False, [('sp', s, i % 2), 'cst'],
                       [('ps', pbk)], skip_group_check=True)

                def s4(s, i):
                    pbk = PB[s]
                    c0 = c0_of(i)
                    ACT(eP[s][:, c0:512], bank(pbk)[:, c0:512], AF.Exp, [('ps', pbk)], [('eP', s)], scale=-1.0)

                def s5(s, i):
                    c0 = c0_of(i)
                    TT('dve', a_sb[s][i % 2][:, c0:512], e_sb[s][i % 2][:, c0:512], eP[s][:, c0:512], ALU.mult,
                       [('e', s, i % 2), ('eP', s)], [('a', s, i % 2)])

                def s6(s, i):
                    t = tiles[i]
                    c0 = c0_of(i)
                    MM(bank(OBK)[s * 64:(s + 1) * 64, c0:512], v1[:, t['kb'], s * 64:(s + 1) * 64], a_sb[s][i % 2][:, c0:512],
                       t['first'], t['last'], [('a', s, i % 2), kv], [('ps', OBK)], skip_group_check=True)

                def evac(i):
                    j = tiles[i]['j']
                    q0 = (4 * j + 1) * 128
                    TT('dve', mixT1[:, c, q0 - 128:q0 - 128 + 512], bank(OBK), gT1[:, q0 - 128:q0 - 128 + 512], ALU.mult,
                       [('ps', OBK), kg], [('mixT1', c)])

                def dummies(n):
                    for _ in range(n):
                        MM(bank(JUNK), tri, cst[:, K_SBM + 128:K_SBM + 640], True, True, ['cst', 'cstm'], [('ps', JUNK)])

                s1(0, 0)
                s1(1, 0)
                filled = False
                for i in range(ntl):
                    s3a(0, i)
                    s3a(1, i)
                    if i + 1 < ntl:
                        s1(0, i + 1)
                        s1(1, i + 1)
                    if not filled:
                        dummies(NDUM)
                    s2a(0, i)
                    s2a(1, i)
                    s2b(0, i)
                    s2b(1, i)
                    if os.environ.get('EVLATE', '1') == '1':
                        while pend_ev:
                            pend_ev.pop(0)()
                    s3(0, i)
                    s3(1, i)
                    if i >= 1:
                        s6(0, i - 1)
                        s6(1, i - 1)
                        if tiles[i - 1]['last']:
                            evac(i - 1)
                    filled = fill(i)
                    s4(0, i)
                    s4(1, i)
                    s5(0, i)
                    s5(1, i)
                    while pend_ev and os.environ.get('EVLATE', '1') != '1':
                        pend_ev.pop(0)()
                while pend_ev:
                    pend_ev.pop(0)()
                s6(0, ntl - 1)
                s6(1, ntl - 1)
                evac(ntl - 1)
                while nxt_i[0] < len(nxt):
                    nxt[nxt_i[0]][1](nxt_i[0] % 4)()
                    nxt_i[0] += 1

            load_w1(0)
            pend = proj_groups(0)
            n_g = 8
            late, pend = pend[-n_g:], pend[:-n_g]
            n_emit = [0]
            norm_stats(0, xs, junk, par=0)
            for b in range(NB):
                if b + 1 < NB:
                    norm_stats(b + 1, xs if (b + 1) % 2 == 0 else xs_b, junk, par=(b + 1) % 2)
                norm_tr(V_GPRE1, xs if b % 2 == 0 else xs_b, xnT1[:, :, b * 128:(b + 1) * 128],
                        (lambda kc, b=b: ('xnT1', b, kc)), b % 2, par=b % 2)
                keep = []
                for (need, g) in pend:
                    if need <= b - 1:
                        g([2, 3][n_emit[0] % 2])()
                        n_emit[0] += 1
                    else:
                        keep.append((need, g))
                pend = keep
            for gi, (need, g) in enumerate(pend + late):
                g(gi % 4)()
            pend = []
            assert not pend
            S.barrier()
            for c in range(8):
                l1_chunk(c)
            S.barrier()
            outs = []
            for pi, b in enumerate(range(1, NB, 2)):
                bq = [(0, 1), (2, 3)] if pi % 2 == 0 else [(4, 5), (6, 7)]
                for u in range(2):
                    outproj_mm((lambda c, bb=b + u: mixT1[:, c, (bb - 1) * 128:bb * 128]), [('mixT1', c) for c in range(8)],
                               wout1, 'wout1', bq[u])
                for u in range(2):
                    outproj_epi(b + u, junk, ptmp, bq[u], sc=8 + 8 * ((2 * pi + u) % 4))
                    outs.append(DMA('sp', 'h%d' % (b + u), out_d[(b + u - 1) * 128:(b + u) * 128, :], h[:, b + u, :],
                                    reads=[('h', b + u)]))
            S.wait_all('sp', outs)
        else:
            outs = []
            for b in range(NB):
                outs.append(DMA('sp', 'out%d' % b, hout_d[b * 128:(b + 1) * 128, :], h[:, b, :], reads=[('h', b)]))
            S.wait_all('sp', outs)
        S.run()
    return nc


def _consts():
    c = np.zeros((128, NCD), np.float32)
    K_SBM, K_MCUR, K_MPREV, K_MMETA, K_MCUR0 = D_L1M, D_L0M, D_L0M + 512, D_L0M + 1024, D_L0M + 1536
    r = np.arange(128)
    c[:, K_ID:K_ID + 128] = np.eye(128, dtype=np.float32)
    c[:, K_TRI:K_TRI + 128] = (r[:, None] >= r[None, :]).astype(np.float32)
    c[:, K_COMP:K_COMP + 128] = (r[:, None] < r[None, :]).astype(np.float32)
    c[:, K_ONES:K_ONES + 128] = 1.0 / 512.0
    c[:, K_SBM:K_SBM + 128] = np.where(r[:, None] < r[None, :], 0.0, NEG_SB)
    mp = np.zeros((128, 512), np.float32)
    mp[:PAD, :] = NEG_SB
    c[:, K_SBM + 128:K_SBM + 640] = mp
    cur = np.where(r[:, None] <= r[None, :], 0.0, NEG).astype(np.float32)
    prev = np.where(r[:, None] > r[None, :], 0.0, NEG).astype(np.float32)
    c[:, K_MCUR:K_MCUR + 512] = np.tile(cur, (1, 4))
    c[:, K_MPREV:K_MPREV + 512] = np.tile(prev, (1, 4))
    mm = np.zeros((128, 512), np.float32)
    mm[:PAD, :] = NEG
    c[:, K_MMETA:K_MMETA + 512] = mm
    cur0 = cur.copy()
    cur0[:PAD, :] = NEG
    c[:, K_MCUR0:K_MCUR0 + 512] = np.tile(cur0, (1, 4))
    return c


def _rope_tables():
    p = np.arange(128)
    d = p % 64
    i = d % 32
    inv = (10000.0 ** (-(i.astype(np.float32)) / np.float32(32.0))).astype(np.float32)
    pos = np.maximum(np.arange(T) - PAD, 0).astype(np.float32)
    ang = (pos[None, :] * inv[:, None]).astype(np.float32)
    cos = np.cos(ang).astype(np.float32)
    sin = np.sin(ang).astype(np.float32)
    sgn = np.where(d < 32, -1.0, 1.0).astype(np.float32)[:, None]
    return np.ascontiguousarray(np.concatenate([cos, sin * sgn], axis=1).astype(np.float32))


def _swap_halves(w):
    n = w.shape[1] // 64
    w4 = w.reshape(w.shape[0], n, 2, 32)
    return w4[:, :, ::-1, :].reshape(w.shape[0], n * 64)


def _layout_w0(w_in):
    q, k, v, ga, glu, gb = np.split(w_in, [512, 640, 768, 1280, 2304], axis=1)
    qh = q.reshape(D, 8, 64)
    qperm = np.stack([qh[:, [i, 4 + i], :].reshape(D, 128) for i in range(4)], axis=1).reshape(D, 512)
    qs = _swap_halves(qperm)
    ks = _swap_halves(k)
    return np.ascontiguousarray(np.concatenate([qperm, qs, k, ks, ga, glu, gb, v], axis=1).astype(np.float32))


def _layout_w1(w_in):
    q, k, v, g = np.split(w_in, 4, axis=1)
    out = np.empty((8, D, 512), np.float32)
    for c in range(8):
        sl = slice(c * 128, (c + 1) * 128)
        out[c] = np.concatenate([q[:, sl], k[:, sl], v[:, sl], g[:, sl]], axis=1)
    return out


def _fm(vec, nchunk):
    return np.ascontiguousarray(np.asarray(vec, np.float32).reshape(nchunk, 128).T)


_NC_CACHE = {}


def kernel(x, meta_tokens, ab_pre_norm, ab_w_in, ab_sinks, ab_conv_w, ab_conv_b, ab_conv_ln_g, ab_conv_ln_b, ab_w_pw2,
           ab_w_out, ab_post_norm, sb_pre_norm, sb_w_in, sb_w_out, sb_post_norm):
    f = lambda a: np.ascontiguousarray(np.asarray(a, dtype=np.float32))
    x = f(x)
    vecs = np.zeros((128, NVEC), np.float32)
    vecs[:, V_GPRE0:V_GPRE0 + 8] = _fm(f(ab_pre_norm)[0], 8)
    vecs[:, V_GPRE1:V_GPRE1 + 8] = _fm(f(sb_pre_norm)[0], 8)
    cw = f(ab_conv_w)[0]
    vecs[:, V_CW:V_CW + 124] = cw.T.reshape(4, 128, 31).transpose(1, 0, 2).reshape(128, 124)
    vecs[:, V_CB:V_CB + 4] = _fm(f(ab_conv_b)[0], 4)
    vecs[:, V_LNG:V_LNG + 4] = _fm(f(ab_conv_ln_g)[0], 4)
    vecs[:, V_LNB:V_LNB + 4] = _fm(f(ab_conv_ln_b)[0], 4)
    rows = np.zeros((128, NROW), np.float32)
    rows[:, R_GPOST0:R_GPOST0 + D] = f(ab_post_norm)[0][None, :]
    rows[:, R_GPOST1:R_GPOST1 + D] = f(sb_post_norm)[0][None, :]
    rows[:, R_SINK:R_SINK + 8] = f(ab_sinks)[0][None, :]
    shared = {
        'meta': f(meta_tokens), 'w0': _layout_w0(f(ab_w_in)[0]), 'wpw2': f(ab_w_pw2)[0], 'wout0': f(ab_w_out)[0],
        'w1': _layout_w1(f(sb_w_in)[0]), 'wout1': f(sb_w_out)[0], 'vecs': vecs, 'rows': rows, 'cst': _consts(),
        'rope': _rope_tables(),
    }
    if 'nc' not in _NC_CACHE:
        _NC_CACHE['nc'] = build(True, True)
    nc = _NC_CACHE['nc']
    in_maps = [dict(shared, x=x[b]) for b in range(8)]
    res = run_bass_kernel_spmd(nc, in_maps, core_ids=list(range(8)))
    return np.stack([res.results[b]['out'] for b in range(8)], axis=0).astype(np.float32)
```

```python
import os
import numpy as np
from contextlib import ExitStack
import concourse.bass as bass
import concourse.mybir as mybir
from concourse.bass_utils import run_bass_kernel_spmd

F32 = mybir.dt.float32
BF16 = mybir.dt.bfloat16
AF = mybir.ActivationFunctionType
ALU = mybir.AluOpType

D = 1024
T = 2176
NB = 17
PAD = 112
SEQ = 2048
NEG = -240000.0
NEG_SB = -30000.0
W0C = 27 * 128

ENGS = ['pe', 'act', 'dve', 'pool', 'sp']


class Sched:
    def __init__(self, nc, stack):
        self.nc = nc
        self.stack = stack
        self.q = {e: [] for e in ENGS}
        self.cnt = {e: 0 for e in ENGS}
        self.waited = {e: {} for e in ENGS}
        self.lastw = {}
        self.readers = {}
        self.semh = {}
        self.dval = {}
        for e in ENGS[:4]:
            self.semh['E_' + e] = stack.enter_context(nc.semaphore('E_' + e))

    def _deps(self, reads, writes, eng=None):
        deps = []
        for k in reads:
            t = self.lastw.get(k)
            if t is not None:
                deps.append(t)
            if isinstance(k, tuple) and k[0] == 'ps':
                deps.extend(t2 for t2 in self.readers.get(k, {}).values() if t2[2] != eng)
        for k in writes:
            t = self.lastw.get(k)
            if t is not None:
                deps.append(t)
            deps.extend(self.readers.get(k, {}).values())
        return deps

    def _wait(self, eng, deps):
        w = self.waited[eng]
        for (sk, v, src) in deps:
            if src == 'pe' and eng == 'pe':
                continue
            if w.get(sk, 0) >= v:
                continue
            w[sk] = v
            h = self.semh[sk]
            self.q[eng].append(lambda e, h=h, v=v: e.wait_ge(h, v))

    def _record(self, tok, reads, writes):
        for k in reads:
            self.readers.setdefault(k, {})[tok[0]] = tok
        for k in writes:
            self.lastw[k] = tok
            self.readers[k] = {}

    def op(self, eng, name, kw, reads=(), writes=(), extra=()):
        self._wait(eng, self._deps(reads, writes, eng) + list(extra))
        self.cnt[eng] += 1
        sk = 'E_' + eng
        tok = (sk, self.cnt[eng], eng)
        h = self.semh[sk]
        self.q[eng].append(lambda e, name=name, kw=kw, h=h: getattr(e, name)(**kw).then_inc(h, 1))
        self._record(tok, reads, writes)
        return tok

    def dma(self, eng, name, kw, reads=(), writes=(), extra=()):
        sk = 'D_' + name
        self._wait(eng, self._deps(reads, writes) + list(extra))
        if sk not in self.semh:
            self.semh[sk] = self.stack.enter_context(self.nc.semaphore(sk))
            self.dval[sk] = 0
        if self.dval[sk] > 0:
            self._wait(eng, [(sk, self.dval[sk], 'dma')])
        self.dval[sk] += 16
        tok = (sk, self.dval[sk], 'dma')
        h = self.semh[sk]
        self.q[eng].append(lambda e, kw=kw, h=h: e.dma_start(**kw).then_inc(h, 16))
        self._record(tok, reads, writes)
        return tok

    def barrier(self):
        toks = []
        for e in ENGS[:4]:
            if self.cnt[e] > 0:
                toks.append(('E_' + e, self.cnt[e], e + '_b'))
        for sk, v in self.dval.items():
            toks.append((sk, v, 'dma'))
        for e in ENGS:
            self._wait(e, toks)

    def wait_all(self, eng, toks):
        self._wait(eng, list(toks))

    def run(self):
        nc = self.nc
        with nc.Block() as block:
            @block.tensor
            def _(e):
                for f in self.q['pe']:
                    f(e)

            @block.scalar
            def _(e):
                for f in self.q['act']:
                    f(e)

            @block.vector
            def _(e):
                for f in self.q['dve']:
                    f(e)

            @block.gpsimd
            def _(e):
                for f in self.q['pool']:
                    f(e)

            @block.sync
            def _(e):
                for f in self.q['sp']:
                    f(e)


class Arena:
    def __init__(self, ap, nwords):
        self.ap = ap
        self.n = nwords
        self.off = 0

    def f32(self, n):
        assert self.off + n <= self.n, ('SBUF arena overflow', self.off, n, self.n)
        a = self.ap[:, self.off:self.off + n]
        self.off += n
        return a

    def bf16(self, n):
        w = (n + 1) // 2
        return self.f32(w).bitcast(BF16)

    def mark(self):
        return self.off

    def reset(self, m):
        self.off = m


C_Q, C_QS, C_K, C_KS, C_GA, C_GLA, C_GLB, C_GB, C_V = 0, 4, 8, 9, 10, 14, 18, 22, 26
K_ID, K_TRI, K_COMP, K_ONES = 0, 128, 256, 384
K_MASK = 512
K_MCUR = K_MASK
K_MPREV = K_MCUR + 512
K_MMETA = K_MPREV + 512
K_MCUR0 = K_MMETA + 512
K_SBM = K_MASK
NCB = K_MASK + 5 * 512
D_L0M = 512
D_L1M = 512 + 2048
NCD = 512 + 2048 + 2560
V_GPRE0, V_GPRE1, V_CW, V_CB, V_LNG, V_LNB = 0, 8, 16, 16 + 124, 16 + 128, 16 + 132
NVEC = 16 + 136
R_GPOST0, R_GPOST1, R_SINK = 0, 1024, 2048
NROW = 2048 + 8
RS_GPOST, RS_SINK = 0, 1024
NROWS = 1024 + 8


def build(do_l0=True, do_l1=True):
    nc = bass.Bass('TRN2', target_bir_lowering=False)
    dt = lambda name, shape, kind='ExternalInput': nc.dram_tensor(name, shape, F32, kind=kind).ap()
    x_d = dt('x', [SEQ, D])
    meta_d = dt('meta', [16, D])
    w0_d = dt('w0', [D, W0C])
    wpw2_d = dt('wpw2', [512, 512])
    wout0_d = dt('wout0', [D, D])
    w1_d = dt('w1', [8, D, 512])
    wout1_d = dt('wout1', [D, D])
    vecs_d = dt('vecs', [128, NVEC])
    rows_d = dt('rows', [128, NROW])
    cst_d = dt('cst', [128, NCD])
    rope_d = dt('rope', [128, 2 * T])
    if not do_l0:
        hin_d = dt('hin', [T, D])
    if do_l1:
        out_d = dt('out', [SEQ, D], kind='ExternalOutput')
    else:
        hout_d = dt('hout', [T, D], kind='ExternalOutput')

    with ExitStack() as st:
        NW = 53200
        arena_t = st.enter_context(nc.sbuf_tensor('arena', [128, NW], F32))
        ps = st.enter_context(nc.psum_tensor('ps', [128, 4096], F32))
        S = Sched(nc, st)
        A = Arena(arena_t, NW)

        def bank(i):
            return ps[:, i * 512:(i + 1) * 512]

        def bankbf(i):
            return ps[:, i * 512:(i + 1) * 512].bitcast(BF16)

        def ACT(out, in_, func, reads, writes, **kw):
            return S.op('act', 'activation', dict(out=out, in_=in_, func=func, **kw), reads, writes)

        def TT(eng, out, in0, in1, op, reads, writes):
            return S.op(eng, 'tensor_tensor', dict(out=out, in0=in0, in1=in1, op=op), reads, writes)

        def TS(eng, out, in0, s1, s2, op0, op1, reads, writes):
            kw = dict(out=out, in0=in0, scalar1=s1, scalar2=s2, op0=op0)
            if op1 is not None:
                kw['op1'] = op1
            return S.op(eng, 'tensor_scalar', kw, reads, writes)

        def STT(out, in0, scalar, in1, op0, op1, reads, writes):
            return S.op('dve', 'scalar_tensor_tensor', dict(out=out, in0=in0, scalar=scalar, in1=in1, op0=op0, op1=op1),
                        reads, writes)

        def MM(out, lhsT, rhs, start, stop, reads, writes, **kw):
            return S.op('pe', 'matmul', dict(out=out, lhsT=lhsT, rhs=rhs, start=start, stop=stop, **kw), reads, writes)

        def TR(out, in_, reads, writes):
            return S.op('pe', 'transpose', dict(out=out, in_=in_, identity=ident), reads, writes)

        def CP(eng, out, in_, reads, writes):
            return S.op(eng, 'tensor_copy', dict(out=out, in_=in_), reads, writes)

        def DMA(eng, name, out, in_, reads=(), writes=()):
            return S.dma(eng, name, dict(out=out, in_=in_), reads, writes)

        h = A.f32(NB * D).rearrange('p (b d) -> p b d', b=NB)
        cst = A.bf16(NCB)
        vecs = A.f32(NVEC)
        rows = A.f32(NROWS)
        stat = A.f32(64)
        esink = A.f32(8)
        ident = cst[:, K_ID:K_ID + 128]
        tri = cst[:, K_TRI:K_TRI + 128]
        comp = cst[:, K_COMP:K_COMP + 128]
        ones512 = cst[:, K_ONES:K_ONES + 128]
        base_mark = A.mark()

        DMA('pool', 'cst', cst[:, 0:512], cst_d[:, 0:512], writes=['cst'])
        DMA('sp', 'vecs', vecs, vecs_d, writes=['vecs'])
        DMA('sp', 'sinks', rows[:, RS_SINK:RS_SINK + 8], rows_d[:, R_SINK:R_SINK + 8], writes=['sinks'])

        def load_layer_consts(layer):
            if layer == 0:
                DMA('pool', 'cstm', cst[:, K_MASK:K_MASK + 2048], cst_d[:, D_L0M:D_L0M + 2048], writes=['cstm'])
                DMA('sp', 'rows', rows[:, 0:1024], rows_d[:, R_GPOST0:R_GPOST0 + 1024], writes=['rows'])
            else:
                DMA('pool', 'cstm', cst[:, K_MASK:K_MASK + 640], cst_d[:, D_L1M:D_L1M + 640], writes=['cstm'])
                DMA('sp', 'rows', rows[:, 0:1024], rows_d[:, R_GPOST1:R_GPOST1 + 1024], writes=['rows'])
        if do_l0:
            S.op('dve', 'memset', dict(ap=h[:, 0, :], constant=0.0), [], [('h', 0)])
            DMA('sp', 'h0', h[PAD:128, 0, :], meta_d, writes=[('h', 0)])
            for b in range(1, NB):
                DMA('sp', 'h%d' % b, h[:, b, :], x_d[(b - 1) * 128:b * 128, :], writes=[('h', b)])
        else:
            for b in range(NB):
                DMA('sp', 'h%d' % b, h[:, b, :], hin_d[b * 128:(b + 1) * 128, :], writes=[('h', b)])

        def norm_stats(b, xs, junk, eps=1e-6, par=0):
            hb = h[:, b, :]
            sc = 40 + 3 * par
            k0, k1, k2, kx = ('nst', par, 0), ('nst', par, 1), ('nst', par, 2), ('xs', par)
            ACT(junk, hb, AF.Square, [('h', b)], ['junk', k0], accum_out=stat[:, sc:sc + 1])
            ACT(stat[:, sc + 1:sc + 2], stat[:, sc:sc + 1], AF.Ln, [k0], [k1], scale=1.0 / D, bias=eps)
            ACT(stat[:, sc + 2:sc + 3], stat[:, sc + 1:sc + 2], AF.Exp, [k1], [k2], scale=-0.5)
            TS('dve', xs, hb, stat[:, sc + 2:sc + 3], None, ALU.mult, None, [('h', b), k2], [kx])

        def norm_tr(gcol, xs, dst_all, key_fn, pbank, par=0):
            kx = ('xs', par)
            pb = bankbf(pbank)
            for kc in range(8):
                TR(pb[:, kc * 128:(kc + 1) * 128], xs[:, kc * 128:(kc + 1) * 128], [kx, 'cst'], [('ps', pbank)])
            TT('dve', dst_all, pb.rearrange('p (k n) -> p k n', k=8), vecs[:, gcol:gcol + 8].unsqueeze(2).to_broadcast([128, 8, 128]),
               ALU.mult, [('ps', pbank), 'vecs'], [key_fn(kc) for kc in range(8)])

        def norm_transpose(b, gcol, xs, junk, dst_all, key_fn, pbank, eps=1e-6, par=0):
            hb = h[:, b, :]
            sc = 40 + 3 * par
            k0, k1, k2, kx = ('nst', par, 0), ('nst', par, 1), ('nst', par, 2), ('xs', par)
            ACT(junk, hb, AF.Square, [('h', b)], ['junk', k0], accum_out=stat[:, sc:sc + 1])
            ACT(stat[:, sc + 1:sc + 2], stat[:, sc:sc + 1], AF.Ln, [k0], [k1], scale=1.0 / D, bias=eps)
            ACT(stat[:, sc + 2:sc + 3], stat[:, sc + 1:sc + 2], AF.Exp, [k1], [k2], scale=-0.5)
            TS('dve', xs, hb, stat[:, sc + 2:sc + 3], None, ALU.mult, None, [('h', b), k2], [kx])
            pb = bankbf(pbank)
            for kc in range(8):
                TR(pb[:, kc * 128:(kc + 1) * 128], xs[:, kc * 128:(kc + 1) * 128], [kx, 'cst'], [('ps', pbank)])
            TT('dve', dst_all, pb.rearrange('p (k n) -> p k n', k=8), vecs[:, gcol:gcol + 8].unsqueeze(2).to_broadcast([128, 8, 128]),
               ALU.mult, [('ps', pbank), 'vecs'], [key_fn(kc) for kc in range(8)])

        def outproj_mm(lhs_fn, mix_keys, wout, wkey, banks):
            for half in range(2):
                bk = banks[half]
                for c in range(8):
                    MM(bank(bk), lhs_fn(c), wout[:, c, half * 512:(half + 1) * 512], c == 0, c == 7,
                       list(mix_keys) + [(wkey, c)], [('ps', bk)])

        def outproj_epi(b, junk, ptmp, banks, sc=8, eps=1e-6):
            for half in range(2):
                bk = banks[half]
                ACT(junk[:, 0:512], bank(bk), AF.Square, [('ps', bk)], ['junk', ('st', sc + half)],
                    accum_out=stat[:, sc + half:sc + half + 1])
            TT('dve', stat[:, sc + 2:sc + 3], stat[:, sc:sc + 1], stat[:, sc + 1:sc + 2], ALU.add, [('st', sc), ('st', sc + 1)],
               [('st', sc + 2)])
            ACT(stat[:, sc + 3:sc + 4], stat[:, sc + 2:sc + 3], AF.Ln, [('st', sc + 2)], [('st', sc + 3)], scale=1.0 / D, bias=eps)
            ACT(stat[:, sc + 4:sc + 5], stat[:, sc + 3:sc + 4], AF.Exp, [('st', sc + 3)], [('st', sc + 4)], scale=-0.5)
            for half in range(2):
                bk = banks[half]
                STT(ptmp[:, 0:512], bank(bk), stat[:, sc + 4:sc + 5], rows[:, half * 512:(half + 1) * 512], ALU.mult, ALU.mult,
                    [('ps', bk), ('st', sc + 4), 'rows'], ['ptmp'])
                TT(os.environ.get('EPIENG', 'dve'), h[:, b, half * 512:(half + 1) * 512], h[:, b, half * 512:(half + 1) * 512], ptmp[:, 0:512], ALU.add,
                   [('h', b), 'ptmp'], [('h', b)])

        def outproj_block(b, lhs_fn, mix_keys, wout, wkey, gpost_off, junk, ptmp, banks, eps=1e-6):
            outproj_mm(lhs_fn, mix_keys, wout, wkey, banks)
            outproj_epi(b, junk, ptmp, banks)

        if do_l0:
            w0 = A.bf16(8 * W0C).rearrange('p (k n) -> p k n', k=8)
            wpw2 = A.bf16(4 * 512).rearrange('p (k n) -> p k n', k=4)
            wout0 = A.bf16(8 * D).rearrange('p (k n) -> p k n', k=8)
            NT = 256
            xnT = A.bf16(8 * NT).rearrange('p (k n) -> p k n', k=8)
            qT = A.bf16(4 * NT).rearrange('p (k n) -> p k n', k=4)
            kT = A.bf16(T)
            vext = A.bf16(NB * 2 * 66).rearrange('p (b g d) -> p b g d', b=NB, g=2)
            gaT = A.bf16(4 * NT).rearrange('p (k n) -> p k n', k=4)
            gbT = A.bf16(4 * NT).rearrange('p (k n) -> p k n', k=4)
            ubuf = A.bf16(4 * (30 + NT)).rearrange('p (k n) -> p k n', k=4)
            dg_all = A.bf16(8 * 128)
            dgi = [0]
            acc = A.f32(4 * NT).rearrange('p (k n) -> p k n', k=4)
            ybf = A.bf16(4 * NT).rearrange('p (k n) -> p k n', k=4)
            ysq = A.bf16(4 * NT).rearrange('p (k n) -> p k n', k=4)
            mean_sb = A.f32(NT)
            var_sb = A.f32(NT)
            rstdc = var_sb
            zt = acc
            cact = A.bf16(4 * NT).rearrange('p (k n) -> p k n', k=4)
            mixT = A.bf16(8 * NT).rearrange('p (k n) -> p k n', k=8)
            pT = [A.bf16(512) for _ in range(3)]
            a_tok = A.bf16(512)
            ropeA = A.f32(NT)
            ropeB = A.f32(NT)
            ptmp = A.f32(512)
            sig = ropeA
            cs = A.f32(2 * NT).rearrange('p (k n) -> p k n', k=2)
            xs = A.bf16(D)
            xs_b = A.bf16(D)
            junk = A.bf16(D)
            den = A.f32(8)
            load_layer_consts(0)
            if os.environ.get('ARENA_DBG'):
                print('L0 arena used', A.mark(), 'of', NW)

            W0G = [(0, 10 * 128), (10 * 128, 14 * 128), (26 * 128, 27 * 128), (14 * 128, 26 * 128)]

            def w0grp(oc):
                c = oc * 128
                for gi, (lo, hi) in enumerate(W0G):
                    if lo <= c < hi:
                        return gi
            for gi, (lo, hi) in enumerate(W0G):
                for kc in range(8):
                    DMA('pool', 'w0_%d_%d' % (gi, kc), w0[:, kc, lo:hi], w0_d[kc * 128:(kc + 1) * 128, lo:hi],
                        writes=[('w0', gi, kc)])

            for kc in range(4):
                DMA('pool', 'wpw2_%d' % kc, wpw2[:, kc, :], wpw2_d[kc * 128:(kc + 1) * 128, :], writes=[('wpw2', kc)])
            for kc in range(8):
                DMA('pool', 'wout0_%d' % kc, wout0[:, kc, :], wout0_d[kc * 128:(kc + 1) * 128, :], writes=[('wout0', kc)])

            def w0keys_of(oc):
                return [('w0', w0grp(oc), kc) for kc in range(8)]
            S.op('pool', 'memset', dict(ap=ubuf[:, :, 0:30], constant=0.0), [], ['ubuf'])
            S.op('pool', 'memset', dict(ap=vext[:, :, :, 64:66], constant=1.0), [], ['vones'])
            ACT(esink, rows[:, RS_SINK:RS_SINK + 8], AF.Exp, ['sinks'], ['esink'])
            rope3 = rope_d.rearrange('p (k n) -> p k n', k=2)

            pbi = [0]
            NDUM0 = int(os.environ.get('NDUM0', '0'))

            def next_bank():
                b_ = [1, 2, 3][pbi[0] % 3]
                pbi[0] += 1
                return b_

            STG = int(os.environ.get('L0_STAGE', '99'))

            def l0_norm(b0, nb):
                for bi in range(nb):
                    norm_transpose(b0 + bi, V_GPRE0, xs, junk, xnT[:, :, bi * 128:(bi + 1) * 128],
                                   (lambda kc, bi=bi: ('xnT', bi, kc)), 0)

            def l0_chunk(b0, nb, nxt=None):
                t0 = b0 * 128
                nt = nb * 128
                xkeys = [('xnT', bi, kc) for bi in range(nb) for kc in range(8)]

                def proj(bk, coff, oc):
                    for kc in range(8):
                        MM(bank(bk)[:, coff:coff + nt], w0[:, kc, oc * 128:(oc + 1) * 128], xnT[:, kc, 0:nt], kc == 0, kc == 7,
                           xkeys + w0keys_of(oc), [('ps', bk)])

                if STG <= 0:
                    return
                if os.environ.get('NOCS') is None:
                    DMA('sp', 'cs', cs[:, :, 0:nt], rope3[:, :, t0:t0 + nt], writes=['cs'])
                if STG <= 1:
                    return
                for i in range(5):
                    bk = next_bank()
                    oc_a, oc_b = (C_Q + i, C_QS + i) if i < 4 else (C_K, C_KS)
                    proj(bk, 0, oc_a)
                    proj(bk, 256, oc_b)
                    TT('dve', ropeA[:, 0:nt], bank(bk)[:, 0:nt], cs[:, 0, 0:nt], ALU.mult, [('ps', bk), 'cs'], ['ropeA'])
                    TT('dve', ropeB[:, 0:nt], bank(bk)[:, 256:256 + nt], cs[:, 1, 0:nt], ALU.mult, [('ps', bk), 'cs'], ['ropeB'])
                    if i < 4:
                        TT(os.environ.get('ROPEENG', 'pool'), qT[:, i, 0:nt], ropeA[:, 0:nt], ropeB[:, 0:nt], ALU.add, ['ropeA', 'ropeB'], [('qT', i)])
                    else:
                        TT(os.environ.get('ROPEENG', 'pool'), kT[:, t0:t0 + nt], ropeA[:, 0:nt], ropeB[:, 0:nt], ALU.add, ['ropeA', 'ropeB'],
                           [('kT', b0 + bi) for bi in range(nb)])
                if STG <= 2:
                    return
                for i in range(0, 4, 2):
                    bk = next_bank()
                    proj(bk, 0, C_GA + i)
                    proj(bk, 256, C_GA + i + 1)
                    for u in range(2):
                        ACT(gaT[:, i + u, 0:nt], bank(bk)[:, u * 256:u * 256 + nt], AF.Silu, [('ps', bk)], [('gaT', i + u)])
                for bi in range(nb):
                    bk = next_bank()
                    for kc in range(8):
                        MM(bank(bk)[:, 0:128], xnT[:, kc, bi * 128:(bi + 1) * 128], w0[:, kc, C_V * 128:(C_V + 1) * 128],
                           kc == 0, kc == 7, xkeys + w0keys_of(C_V), [('ps', bk)])
                    ACT(vext[:, b0 + bi, :, 0:64], bank(bk)[:, 0:128].rearrange('p (g d) -> p g d', g=2), AF.Copy,
                        [('ps', bk)], [('v', b0 + bi)])
                if STG <= 3:
                    return
                side = []

                def job_glu(i):
                    def run():
                        bk = next_bank()
                        proj(bk, 0, C_GLA + i)
                        proj(bk, 256, C_GLB + i)
                        ACT(sig[:, 0:nt], bank(bk)[:, 256:256 + nt], AF.Sigmoid, [('ps', bk)], ['ropeA'])
                        TT('dve', ubuf[:, i, 30:30 + nt], bank(bk)[:, 0:nt], sig[:, 0:nt], ALU.mult, [('ps', bk), 'ropeA'],
                           [('ubuf', i)])
                    return run

                def job_gb(i):
                    def run():
                        bk = next_bank()
                        proj(bk, 0, C_GB + i)
                        proj(bk, 256, C_GB + i + 1)
                        for u in range(2):
                            ACT(gbT[:, i + u, 0:nt], bank(bk)[:, u * 256:u * 256 + nt], AF.Silu, [('ps', bk)], [('gbT', i + u)])
                    return run
                for i in range(4):
                    side.append(job_glu(i))
                for i in range(0, 4, 2):
                    side.append(job_gb(i))
                n_iter = [0]
                for bi in range(nb):
                    n = b0 + bi
                    for g in range(2):
                        rhs_q = qT[g * 64:(g + 1) * 64, :, bi * 128:(bi + 1) * 128]
                        tiles = [(n, K_MCUR0 if n == 0 else K_MCUR)]
                        if n >= 2:
                            tiles.append((n - 1, K_MPREV))
                        if n >= 1:
                            tiles.append((0, K_MMETA))
                        for ti, (kb, mcol) in enumerate(tiles):
                            bk = 4 + ti
                            MM(bank(bk), kT[g * 64:(g + 1) * 64, kb * 128:(kb + 1) * 128], rhs_q, True, False,
                               [('qT', i) for i in range(4)] + [('kT', kb)], [('ps', bk)])
                            MM(bank(bk), ident, cst[:, mcol:mcol + 512], False, True, ['cst', 'cstm'], [('ps', bk)])
                            ACT(pT[ti], bank(bk), AF.Exp, [('ps', bk)], [('pT', ti)], scale=0.125)
                            for _ in range(NDUM0):
                                MM(bank(0), ident, cst[:, K_MCUR:K_MCUR + 512], True, True, ['cst', 'cstm'], [('ps', 0)])
                        for _ in range(int(os.environ.get('SIDEPAT', '1122')[min(n_iter[0], 3)])):
                            if side:
                                side.pop(0)()
                        n_iter[0] += 1
                        ob = bank(7)[:, 0:260].rearrange('p (i d) -> p i d', i=4)
                        for i in range(4):
                            for ti, (kb, mcol) in enumerate(tiles):
                                MM(ob[:, i, :], pT[ti][:, i * 128:(i + 1) * 128], vext[:, kb, g, 0:65], ti == 0,
                                   ti == len(tiles) - 1, [('pT', ti), ('v', kb), 'vones'], [('ps', 7)])
                        TT('dve', den[:, 0:4], ob[:, :, 64], esink[:, 4 * g:4 * g + 4], ALU.add, [('ps', 7), 'esink'], ['den'])
                        S.op('dve', 'reciprocal', dict(out=den[:, 4:8], in_=den[:, 0:4]), ['den'], ['rden'])
                        TT('dve', a_tok[:, g * 256:(g + 1) * 256].rearrange('p (i d) -> p i d', i=4), ob[:, :, 0:64],
                           den[:, 4:8].unsqueeze(2).to_broadcast([128, 4, 64]), ALU.mult, [('ps', 7), 'rden'], [('a_tok', g)])
                    pb = bankbf(0)
                    for c in range(4):
                        TR(pb[:, c * 128:(c + 1) * 128], a_tok[:, c * 128:(c + 1) * 128], [('a_tok', 0), ('a_tok', 1), 'cst'],
                           [('ps', 0)])
                    TT('dve', mixT[:, 0:4, bi * 128:(bi + 1) * 128], pb[:, 0:512].rearrange('p (c n) -> p c n', c=4),
                       gaT[:, :, bi * 128:(bi + 1) * 128], ALU.mult, [('ps', 0)] + [('gaT', i) for i in range(4)],
                       [('mixT', bi)])
                if STG <= 4:
                    return
                while side:
                    side.pop(0)()
                if STG <= 5:
                    return
                def stat_mm(i):
                    MM(bank(4)[:, 0:nt], ones512, ybf[:, i, 0:nt], i == 0, i == 3, [('ybf', i), 'cst'], [('ps', 4)])
                    MM(bank(5)[:, 0:nt], ones512, ysq[:, i, 0:nt], i == 0, i == 3, [('ysq', i), 'cst'], [('ps', 5)])
                for i in range(4):
                    bk = next_bank()
                    for j0 in range(0, 31, 4):
                        kk = min(4, 31 - j0)
                        hb = dgi[0] % 2
                        dgi[0] += 1
                        dgv = dg_all[:, hb * 512:(hb + 1) * 512].rearrange('p (k m) -> p k m', k=4)
                        wb = vecs[:, V_CW + i * 31 + j0:V_CW + i * 31 + j0 + kk]
                        TT('dve', dgv[:, 0:kk, :], ident.unsqueeze(1).to_broadcast([128, kk, 128]),
                           wb.unsqueeze(2).to_broadcast([128, kk, 128]), ALU.mult, ['cst', 'vecs'], [('dg', hb)])
                        for jj in range(kk):
                            j = j0 + jj
                            MM(bank(bk)[:, 0:nt], dgv[:, jj, :], ubuf[:, i, j:j + nt], j == 0, j == 30,
                               [('dg', hb), ('ubuf', i), ('uhalo', i)], [('ps', bk)])
                    if nxt is not None and i in (1, 2) and i - 1 < nxt[1]:
                        norm_tr(V_GPRE0, xs if i == 1 else xs_b, xnT[:, :, (i - 1) * 128:i * 128],
                                (lambda kc, bi=i - 1: ('xnT', bi, kc)), 0, par=i - 1)
                    if nxt is not None and i in (0, 1) and i < nxt[1]:
                        norm_stats(nxt[0] + i, xs if i == 0 else xs_b, junk, par=i)
                    ACT(acc[:, i, 0:nt], bank(bk)[:, 0:nt], AF.Identity, [('ps', bk), 'vecs'], [('acc', i)],
                        bias=vecs[:, V_CB + i:V_CB + i + 1])
                    ACT(ybf[:, i, 0:nt], acc[:, i, 0:nt], AF.Copy, [('acc', i)], [('ybf', i)])
                    ACT(ysq[:, i, 0:nt], acc[:, i, 0:nt], AF.Square, [('acc', i)], [('ysq', i)])
                    if i >= 1:
                        stat_mm(i - 1)
                stat_mm(3)
                for i in range(4):
                    CP('pool', ubuf[:, i, 0:30], ubuf[:, i, nt:nt + 30], [('ubuf', i)], [('uhalo', i)])
                ACT(mean_sb[:, 0:nt], bank(4)[:, 0:nt], AF.Copy, [('ps', 4)], ['mean'])
                TT('dve', var_sb[:, 0:nt], mean_sb[:, 0:nt], mean_sb[:, 0:nt], ALU.mult, ['mean'], ['var'])
                TT('dve', var_sb[:, 0:nt], bank(5)[:, 0:nt], var_sb[:, 0:nt], ALU.subtract, [('ps', 5), 'var'], ['var'])
                ACT(var_sb[:, 0:nt], var_sb[:, 0:nt], AF.Ln, ['var'], ['var'], bias=1e-5)
                ACT(rstdc[:, 0:nt], var_sb[:, 0:nt], AF.Exp, ['var'], ['var'], scale=-0.5)
                for i in range(4):
                    TT('dve', zt[:, i, 0:nt], acc[:, i, 0:nt], mean_sb[:, 0:nt], ALU.subtract, [('acc', i), 'mean'], [('acc', i)])
                    TT('dve', zt[:, i, 0:nt], zt[:, i, 0:nt], rstdc[:, 0:nt], ALU.mult, [('acc', i), 'var'], [('acc', i)])
                    ACT(cact[:, i, 0:nt], zt[:, i, 0:nt], AF.Silu, [('acc', i), 'vecs'], [('cact', i)],
                        scale=vecs[:, V_LNG + i:V_LNG + i + 1], bias=vecs[:, V_LNB + i:V_LNB + i + 1])
                for oc in range(4):
                    bk = next_bank()
                    for kc in range(4):
                        MM(bank(bk)[:, 0:nt], wpw2[:, kc, oc * 128:(oc + 1) * 128], cact[:, kc, 0:nt], kc == 0, kc == 3,
                           [('cact', kc), ('wpw2', kc)], [('ps', bk)])
                    TT('dve', mixT[:, 4 + oc, 0:nt], bank(bk)[:, 0:nt], gbT[:, oc, 0:nt], ALU.mult, [('ps', bk), ('gbT', oc)],
                       [('mixTc', oc)])
                if STG <= 7:
                    return
                obanks = [(5, 6), (4, 7)]
                for bi in range(nb):
                    outproj_mm((lambda c, bi=bi: mixT[:, c, bi * 128:(bi + 1) * 128]),
                               [('mixT', bi)] + [('mixTc', oc) for oc in range(4)], wout0, 'wout0', obanks[bi])
                for bi in range(nb):
                    outproj_epi(b0 + bi, junk, ptmp, obanks[bi], sc=8 + 8 * bi)

            l0_norm(0, 2)
            for b0 in range(0, NB, 2):
                nb0 = b0 + 2
                l0_chunk(b0, min(2, NB - b0), (nb0, min(2, NB - nb0)) if nb0 < NB else None)
            S.barrier()
            A.reset(base_mark)

        if do_l1:
            xn_flat = A.bf16(8 * T)
            xnT1 = xn_flat.rearrange('p (k n) -> p k n', k=8)
            wout1 = xn_flat[:, 0:8 * D].rearrange('p (k n) -> p k n', k=8)
            mixT1 = A.bf16(8 * SEQ).rearrange('p (k n) -> p k n', k=8)
            qT1s = [A.bf16(T) for _ in range(2)]
            kT1s = [A.bf16(T) for _ in range(2)]
            gT1s = [A.bf16(SEQ) for _ in range(2)]
            v1s = [A.bf16(NB * 128).rearrange('p (b d) -> p b d', b=NB) for _ in range(2)]
            w1b = A.bf16(8 * 512).rearrange('p (k n) -> p k n', k=8)
            tmpg = A.f32(256)
            tm = A.mark()
            e_sb = [[A.f32(512) for _ in range(2)] for _ in range(2)]
            eP = [A.f32(512) for _ in range(2)]
            sp = [[A.bf16(512) for _ in range(2)] for _ in range(2)]
            a_sb = [[A.bf16(512) for _ in range(2)] for _ in range(2)]
            tm_end = A.mark()
            A.reset(tm)
            xs = A.bf16(D)
            xs_b = A.bf16(D)
            junk = A.bf16(D)
            ptmp = A.f32(512)
            A.reset(max(tm_end, A.mark()))
            load_layer_consts(1)
            if os.environ.get('ARENA_DBG'):
                print('L1 arena used', A.mark(), 'of', NW)

            def load_w1(c):
                for kc in range(8):
                    DMA('pool', 'w0_0_%d' % kc, w1b[:, kc, :], w1_d[c, kc * 128:(kc + 1) * 128, :], writes=[('w1', kc)])
            ZB = [[0, 1], [2, 3]]
            PB = [4, 5]
            OBK = 6
            JUNK = 7
            NDUM = int(os.environ.get('NDUM', '1'))
            DUMW = int(os.environ.get('DUMW', '256'))
            ntiles_tok = [(tt * 512, min(512, T - tt * 512)) for tt in range(5)]
            xkeys_all = [('xnT1', b, kc) for b in range(NB) for kc in range(8)]

            tiles = []
            for j in range(4):
                kmax = 4 * j + 4
                for kb in range(kmax, -1, -1):
                    tiles.append(dict(j=j, kb=kb, first=(kb == kmax), last=(kb == 0)))
            ntl = len(tiles)

            def proj_groups(c):
                st_ = c % 2
                qd, kd, gd, vd = qT1s[st_], kT1s[st_], gT1s[st_], v1s[st_]
                groups = []

                def g_qkg(which, tt0, ntk):
                    def emit(bk):
                        col = [0, 128, 384][which]
                        blks = range(tt0 // 128, (tt0 + ntk) // 128)
                        for kc in range(8):
                            MM(bank(bk)[:, 0:ntk], w1b[:, kc, col:col + 128], xnT1[:, kc, tt0:tt0 + ntk], kc == 0, kc == 7,
                               [('xnT1', b, kc) for b in blks] + [('w1', kc)], [('ps', bk)])

                        def evac():
                            if which == 0:
                                TS('dve', qd[:, tt0:tt0 + ntk], bank(bk)[:, 0:ntk], 0.125, None, ALU.mult, None, [('ps', bk)],
                                   [('qT1', st_)])
                            elif which == 1:
                                CP('dve', kd[:, tt0:tt0 + ntk], bank(bk)[:, 0:ntk], [('ps', bk)], [('kT1', st_)])
                            else:
                                gsl = gd[:, tt0 - 128:tt0 - 128 + ntk]
                                ACT(tmpg[:, 0:ntk], bank(bk)[:, 0:ntk], AF.Exp, [('ps', bk)], ['tmpg'], scale=-1.0)
                                CP('dve', gsl, bank(bk)[:, 0:ntk], [('ps', bk)], [('gT1', st_)])
                                TS('dve', tmpg[:, 0:ntk], tmpg[:, 0:ntk], 1.0, 1e30, ALU.add, ALU.min, ['tmpg'], ['tmpg'])
                                S.op('dve', 'reciprocal', dict(out=tmpg[:, 0:ntk], in_=tmpg[:, 0:ntk]), ['tmpg'], ['tmpg'])
                                TT('dve', gsl, gsl, tmpg[:, 0:ntk], ALU.mult, [('gT1', st_), 'tmpg'], [('gT1', st_)])
                        return evac
                    return emit

                def g_v(b):
                    def emit(bk):
                        for kc in range(8):
                            MM(bank(bk)[:, 0:128], xnT1[:, kc, b * 128:(b + 1) * 128], w1b[:, kc, 256:384], kc == 0, kc == 7,
                               [('xnT1', b, kc), ('w1', kc)], [('ps', bk)])
                        CP('dve', vd[:, b, :], bank(bk)[:, 0:128], [('ps', bk)], [('v1', st_)])
                    return emit
                for which in (0, 1):
                    for tt0 in range(0, T, 256):
                        ntk = min(256, T - tt0)
                        groups.append(((tt0 + ntk) // 128 - 1, g_qkg(which, tt0, ntk)))
                def g_v2(b):
                    nbk = min(2, NB - b)

                    def emit(bk):
                        for u in range(nbk):
                            for kc in range(8):
                                MM(bank(bk)[:, u * 128:(u + 1) * 128], xnT1[:, kc, (b + u) * 128:(b + u + 1) * 128],
                                   w1b[:, kc, 256:384], kc == 0, kc == 7, [('xnT1', b + u, kc), ('w1', kc)], [('ps', bk)])

                        def evac():
                            CP('dve', vd[:, b:b + nbk, :], bank(bk)[:, 0:nbk * 128].rearrange('p (u d) -> p u d', u=nbk),
                               [('ps', bk)], [('v1', st_)])
                        return evac
                    return emit
                for b in range(0, NB, 2):
                    groups.append((min(b + 1, NB - 1), g_v2(b)))
                for tt0 in range(128, T, 256):
                    groups.append(((tt0 + 256) // 128 - 1, g_qkg(2, tt0, 256)))
                return groups

            def l1_chunk(c):
                st_ = c % 2
                qT1, kT1, gT1, v1 = qT1s[st_], kT1s[st_], gT1s[st_], v1s[st_]
                kq, kk, kg, kv = ('qT1', st_), ('kT1', st_), ('gT1', st_), ('v1', st_)
                if c < 7:
                    load_w1(c + 1)
                    nxt = proj_groups(c + 1)
                else:
                    for kc in range(8):
                        DMA('pool', 'w0_1_%d' % kc, wout1[:, kc, :], wout1_d[kc * 128:(kc + 1) * 128, :], reads=[],
                            writes=xkeys_all + [('wout1', kc)])
                    nxt = []
                nxt_i = [0]
                pend_ev = []

                def fill(i):
                    if i >= 3 and nxt_i[0] < len(nxt) and c0_of(i) <= int(os.environ.get("FILLC0", "128")):
                        pend_ev.append(nxt[nxt_i[0]][1](JUNK))
                        nxt_i[0] += 1
                        return True
                    dummies(NDUM)
                    return False

                def c0_of(i):
                    t = tiles[i]
                    return max(0, t['kb'] - (4 * t['j'] + 1)) * 128

                def s1(s, i):
                    t = tiles[i]
                    j, kb = t['j'], t['kb']
                    q0 = (4 * j + 1) * 128
                    zb = ZB[s][i % 2]
                    d = kb - (4 * j + 1)
                    c0 = c0_of(i)
                    masked = (d >= 0) or (kb == 0)
                    MM(bank(zb)[:, c0:512], kT1[s * 64:(s + 1) * 64, kb * 128:(kb + 1) * 128],
                       qT1[s * 64:(s + 1) * 64, q0 + c0:q0 + 512], True, not masked, [kq, kk], [('ps', zb)])
                    if d >= 0:
                        MM(bank(zb)[:, c0:c0 + 128], ident, cst[:, K_SBM:K_SBM + 128], False, True, ['cst', 'cstm'], [('ps', zb)])
                    elif kb == 0:
                        MM(bank(zb), ident, cst[:, K_SBM + 128:K_SBM + 640], False, True, ['cst', 'cstm'], [('ps', zb)])

                def s2a(s, i):
                    zb = ZB[s][i % 2]
                    c0 = c0_of(i)
                    ACT(e_sb[s][i % 2][:, c0:512], bank(zb)[:, c0:512], AF.Exp, [('ps', zb)], [('e', s, i % 2)])

                def s2b(s, i):
                    c0 = c0_of(i)
                    ACT(sp[s][i % 2][:, c0:512], e_sb[s][i % 2][:, c0:512], AF.Ln, [('e', s, i % 2)], [('sp', s, i % 2)], bias=1.0)

                def s3a(s, i):
                    t = tiles[i]
                    pbk = PB[s]
                    if not t['first']:
                        cp = c0_of(i - 1)
                        MM(bank(pbk)[:, cp:512], comp, sp[s][(i - 1) % 2][:, cp:512], False, False,
                           [('sp', s, (i - 1) % 2), 'cst'], [('ps', pbk)], skip_group_check=True)

                def s3(s, i):
                    t = tiles[i]
                    pbk = PB[s]
                    c0 = c0_of(i)
                    MM(bank(pbk)[:, c0:512], tri, sp[s][i % 2][:, c0:512], t['first'], False, [('sp', s, i % 2), 'cst'],
                       [('ps', pbk)], skip_group_check=True)

                def s4(s, i):
                    pbk = PB[s]
                    c0 = c0_of(i)
                    ACT(eP[s][:, c0:512], bank(pbk)[:, c0:512], AF.Exp, [('ps', pbk)], [('eP', s)], scale=-1.0)

                def s5(s, i):
                    c0 = c0_of(i)
                    TT('dve', a_sb[s][i % 2][:, c0:512], e_sb[s][i % 2][:, c0:512], eP[s][:, c0:512], ALU.mult,
                       [('e', s, i % 2), ('eP', s)], [('a', s, i % 2)])

                def s6(s, i):
                    t = tiles[i]
                    c0 = c0_of(i)
                    MM(bank(OBK)[s * 64:(s + 1) * 64, c0:512], v1[:, t['kb'], s * 64:(s + 1) * 64], a_sb[s][i % 2][:, c0:512],
                       t['first'], t['last'], [('a', s, i % 2), kv], [('ps', OBK)], skip_group_check=True)

                def evac(i):
                    j = tiles[i]['j']
                    q0 = (4 * j + 1) * 128
                    TT('dve', mixT1[:, c, q0 - 128:q0 - 128 + 512], bank(OBK), gT1[:, q0 - 128:q0 - 128 + 512], ALU.mult,
                       [('ps', OBK), kg], [('mixT1', c)])

                def dummies(n):
                    for _ in range(n):
                        MM(bank(JUNK)[:, 0:DUMW], tri, cst[:, K_SBM + 128:K_SBM + 128 + DUMW], True, True, ['cst', 'cstm'], [('ps', JUNK)])

                s1(0, 0)
                s1(1, 0)
                filled = False
                for i in range(ntl):
                    s3a(0, i)
                    s3a(1, i)
                    if i + 1 < ntl:
                        s1(0, i + 1)
                        s1(1, i + 1)
                    if not filled:
                        dummies(NDUM)
                    s2a(0, i)
                    s2a(1, i)
                    s2b(0, i)
                    s2b(1, i)
                    if os.environ.get('EVLATE', '1') == '1':
                        while pend_ev:
                            pend_ev.pop(0)()
                    s3(0, i)
                    s3(1, i)
                    if i >= 1:
                        s6(0, i - 1)
                        s6(1, i - 1)
                        if tiles[i - 1]['last']:
                            evac(i - 1)
                    filled = fill(i)
                    s4(0, i)
                    s4(1, i)
                    s5(0, i)
                    s5(1, i)
                    while pend_ev and os.environ.get('EVLATE', '1') != '1':
                        pend_ev.pop(0)()
                while pend_ev:
                    pend_ev.pop(0)()
                s6(0, ntl - 1)
                s6(1, ntl - 1)
                evac(ntl - 1)
                while nxt_i[0] < len(nxt):
                    nxt[nxt_i[0]][1](nxt_i[0] % 4)()
                    nxt_i[0] += 1

            load_w1(0)
            pend = proj_groups(0)
            n_g = 8
            late, pend = pend[-n_g:], pend[:-n_g]
            n_emit = [0]
            norm_stats(0, xs, junk, par=0)
            for b in range(NB):
                if b + 1 < NB:
                    norm_stats(b + 1, xs if (b + 1) % 2 == 0 else xs_b, junk, par=(b + 1) % 2)
                norm_tr(V_GPRE1, xs if b % 2 == 0 else xs_b, xnT1[:, :, b * 128:(b + 1) * 128],
                        (lambda kc, b=b: ('xnT1', b, kc)), b % 2, par=b % 2)
                keep = []
                for (need, g) in pend:
                    if need <= b - 1:
                        g([2, 3][n_emit[0] % 2])()
                        n_emit[0] += 1
                    else:
                        keep.append((need, g))
                pend = keep
            for gi, (need, g) in enumerate(pend + late):
                g(gi % 4)()
            pend = []
            assert not pend
            S.barrier()
            for c in range(8):
                l1_chunk(c)
            S.barrier()
            outs = []
            for pi, b in enumerate(range(1, NB, 2)):
                bq = [(0, 1), (2, 3)] if pi % 2 == 0 else [(4, 5), (6, 7)]
                for u in range(2):
                    outproj_mm((lambda c, bb=b + u: mixT1[:, c, (bb - 1) * 128:bb * 128]), [('mixT1', c) for c in range(8)],
                               wout1, 'wout1', bq[u])
                for u in range(2):
                    outproj_epi(b + u, junk, ptmp, bq[u], sc=8 + 8 * ((2 * pi + u) % 4))
                    outs.append(DMA('sp', 'h%d' % (b + u), out_d[(b + u - 1) * 128:(b + u) * 128, :], h[:, b + u, :],
                                    reads=[('h', b + u)]))
            S.wait_all('sp', outs)
        else:
            outs = []
            for b in range(NB):
                outs.append(DMA('sp', 'out%d' % b, hout_d[b * 128:(b + 1) * 128, :], h[:, b, :], reads=[('h', b)]))
            S.wait_all('sp', outs)
        S.run()
    return nc


def _consts():
    c = np.zeros((128, NCD), np.float32)
    K_SBM, K_MCUR, K_MPREV, K_MMETA, K_MCUR0 = D_L1M, D_L0M, D_L0M + 512, D_L0M + 1024, D_L0M + 1536
    r = np.arange(128)
    c[:, K_ID:K_ID + 128] = np.eye(128, dtype=np.float32)
    c[:, K_TRI:K_TRI + 128] = (r[:, None] >= r[None, :]).astype(np.float32)
    c[:, K_COMP:K_COMP + 128] = (r[:, None] < r[None, :]).astype(np.float32)
    c[:, K_ONES:K_ONES + 128] = 1.0 / 512.0
    c[:, K_SBM:K_SBM + 128] = np.where(r[:, None] < r[None, :], 0.0, NEG_SB)
    mp = np.zeros((128, 512), np.float32)
    mp[:PAD, :] = NEG_SB
    c[:, K_SBM + 128:K_SBM + 640] = mp
    cur = np.where(r[:, None] <= r[None, :], 0.0, NEG).astype(np.float32)
    prev = np.where(r[:, None] > r[None, :], 0.0, NEG).astype(np.float32)
    c[:, K_MCUR:K_MCUR + 512] = np.tile(cur, (1, 4))
    c[:, K_MPREV:K_MPREV + 512] = np.tile(prev, (1, 4))
    mm = np.zeros((128, 512), np.float32)
    mm[:PAD, :] = NEG
    c[:, K_MMETA:K_MMETA + 512] = mm
    cur0 = cur.copy()
    cur0[:PAD, :] = NEG
    c[:, K_MCUR0:K_MCUR0 + 512] = np.tile(cur0, (1, 4))
    return c


def _rope_tables():
    p = np.arange(128)
    d = p % 64
    i = d % 32
    inv = (10000.0 ** (-(i.astype(np.float32)) / np.float32(32.0))).astype(np.float32)
    pos = np.maximum(np.arange(T) - PAD, 0).astype(np.float32)
    ang = (pos[None, :] * inv[:, None]).astype(np.float32)
    cos = np.cos(ang).astype(np.float32)
    sin = np.sin(ang).astype(np.float32)
    sgn = np.where(d < 32, -1.0, 1.0).astype(np.float32)[:, None]
    return np.ascontiguousarray(np.concatenate([cos, sin * sgn], axis=1).astype(np.float32))


def _swap_halves(w):
    n = w.shape[1] // 64
    w4 = w.reshape(w.shape[0], n, 2, 32)
    return w4[:, :, ::-1, :].reshape(w.shape[0], n * 64)


def _layout_w0(w_in):
    q, k, v, ga, glu, gb = np.split(w_in, [512, 640, 768, 1280, 2304], axis=1)
    qh = q.reshape(D, 8, 64)
    qperm = np.stack([qh[:, [i, 4 + i], :].reshape(D, 128) for i in range(4)], axis=1).reshape(D, 512)
    qs = _swap_halves(qperm)
    ks = _swap_halves(k)
    return np.ascontiguousarray(np.concatenate([qperm, qs, k, ks, ga, glu, gb, v], axis=1).astype(np.float32))


def _layout_w1(w_in):
    q, k, v, g = np.split(w_in, 4, axis=1)
    out = np.empty((8, D, 512), np.float32)
    for c in range(8):
        sl = slice(c * 128, (c + 1) * 128)
        out[c] = np.concatenate([q[:, sl], k[:, sl], v[:, sl], g[:, sl]], axis=1)
    return out


def _fm(vec, nchunk):
    return np.ascontiguousarray(np.asarray(vec, np.float32).reshape(nchunk, 128).T)


_NC_CACHE = {}


def kernel(x, meta_tokens, ab_pre_norm, ab_w_in, ab_sinks, ab_conv_w, ab_conv_b, ab_conv_ln_g, ab_conv_ln_b, ab_w_pw2,
           ab_w_out, ab_post_norm, sb_pre_norm, sb_w_in, sb_w_out, sb_post_norm):
    f = lambda a: np.ascontiguousarray(np.asarray(a, dtype=np.float32))
    x = f(x)
    vecs = np.zeros((128, NVEC), np.float32)
    vecs[:, V_GPRE0:V_GPRE0 + 8] = _fm(f(ab_pre_norm)[0], 8)
    vecs[:, V_GPRE1:V_GPRE1 + 8] = _fm(f(sb_pre_norm)[0], 8)
    cw = f(ab_conv_w)[0]
    vecs[:, V_CW:V_CW + 124] = cw.T.reshape(4, 128, 31).transpose(1, 0, 2).reshape(128, 124)
    vecs[:, V_CB:V_CB + 4] = _fm(f(ab_conv_b)[0], 4)
    vecs[:, V_LNG:V_LNG + 4] = _fm(f(ab_conv_ln_g)[0], 4)
    vecs[:, V_LNB:V_LNB + 4] = _fm(f(ab_conv_ln_b)[0], 4)
    rows = np.zeros((128, NROW), np.float32)
    rows[:, R_GPOST0:R_GPOST0 + D] = f(ab_post_norm)[0][None, :]
    rows[:, R_GPOST1:R_GPOST1 + D] = f(sb_post_norm)[0][None, :]
    rows[:, R_SINK:R_SINK + 8] = f(ab_sinks)[0][None, :]
    shared = {
        'meta': f(meta_tokens), 'w0': _layout_w0(f(ab_w_in)[0]), 'wpw2': f(ab_w_pw2)[0], 'wout0': f(ab_w_out)[0],
        'w1': _layout_w1(f(sb_w_in)[0]), 'wout1': f(sb_w_out)[0], 'vecs': vecs, 'rows': rows, 'cst': _consts(),
        'rope': _rope_tables(),
    }
    if 'nc' not in _NC_CACHE:
        _NC_CACHE['nc'] = build(True, True)
    nc = _NC_CACHE['nc']
    in_maps = [dict(shared, x=x[b]) for b in range(8)]
    res = run_bass_kernel_spmd(nc, in_maps, core_ids=list(range(8)))
    return np.stack([res.results[b]['out'] for b in range(8)], axis=0).astype(np.float32)
```
